# Optimizing a Trainium2 kernel written in Bass

```python
import math
import jax, jax.numpy as jnp
from jax import lax
import numpy as np

D_MODEL = 2048
BATCH = 1
SEQ = 16384
DEPTH = 4

EPS = 1e-6
MLA_HEADS = 8
MLA_NOPE = 128
MLA_ROPE = 64
MLA_V = 128
MLA_Q_RANK = 512
MLA_KV_RANK = 512
ROPE_THETA = 10000.0
Q_BLOCK = 128
GLA_HEADS = 4
GLA_DK = 128
GLA_DV = 256
GLA_GATE_RANK = 16
GLA_GATE_NORM = 16.0
GDN_QK_HEADS = 4
GDN_V_HEADS = 8
GDN_DK = 128
GDN_DV = 128
CONV_WIDTH = 4
CHUNK = 64
D_FF = 5632
FFN_CONV = 3
N_BRANCH = 3

MLA_W = MLA_HEADS * MLA_V
GLA_W = GLA_HEADS * GLA_DV
GDN_W = GDN_V_HEADS * GDN_DV
GLA_KW = GLA_HEADS * GLA_DK
GDN_KW = GDN_QK_HEADS * GDN_DK
GDN_CONV_DIM = 2 * GDN_KW + GDN_W
IN_SPLITS = (MLA_Q_RANK, MLA_KV_RANK, MLA_ROPE,
             GLA_KW, GLA_KW, GLA_W, GLA_GATE_RANK, GLA_W,
             GDN_CONV_DIM, GDN_V_HEADS, GDN_V_HEADS, GDN_W,
             N_BRANCH * D_MODEL)
IN_DIM = sum(IN_SPLITS)

kernel_name = "hybrid_mla_gla_gdn_convffn"


def _split_points():
    pts, acc = [], 0
    for w in IN_SPLITS[:-1]:
        acc += w
        pts.append(acc)
    return pts


def rms_norm(x, g):
    xf = x.astype(jnp.float32)
    y = xf * lax.rsqrt(jnp.mean(xf * xf, axis=-1, keepdims=True) + EPS)
    return (y * g.astype(jnp.float32)).astype(x.dtype)


def l2_normalize(x):
    xf = x.astype(jnp.float32)
    return xf * lax.rsqrt(jnp.sum(xf * xf, axis=-1, keepdims=True) + EPS)


def causal_dwconv(x, w):
    width, ch = w.shape
    return lax.conv_general_dilated(x, w[:, None, :].astype(x.dtype), window_strides=(1,),
                                    padding=[(width - 1, 0)],
                                    dimension_numbers=('NWC', 'WIO', 'NWC'),
                                    feature_group_count=ch)


def rope_tables(positions):
    inv = ROPE_THETA ** (-jnp.arange(0, MLA_ROPE, 2, dtype=jnp.float32) / MLA_ROPE)
    ang = positions.astype(jnp.float32)[..., None] * inv
    return jnp.cos(ang), jnp.sin(ang)


def apply_rope(x, cos, sin):
    x1, x2 = jnp.split(x, 2, axis=-1)
    return jnp.concatenate([x1 * cos - x2 * sin, x1 * sin + x2 * cos], axis=-1).astype(x.dtype)


def to_chunks(t):
    b, s, h, d = t.shape
    return t.reshape(b, s // CHUNK, CHUNK, h, d).transpose(1, 0, 3, 2, 4)


def from_chunks(t):
    n, b, h, c, d = t.shape
    return t.transpose(1, 0, 3, 2, 4).reshape(b, n * c, h, d)


def mla_branch(c_q, c_kv, k_rope, cos, sin, q_norm, kv_norm, w_uq, w_ukv):
    B, S, _ = c_q.shape
    q = (rms_norm(c_q, q_norm) @ w_uq).reshape(B, S, MLA_HEADS, MLA_NOPE + MLA_ROPE)
    q_nope = q[..., :MLA_NOPE]
    q_rope = apply_rope(q[..., MLA_NOPE:], cos[:, :, None], sin[:, :, None])
    kv = (rms_norm(c_kv, kv_norm) @ w_ukv).reshape(B, S, MLA_HEADS, MLA_NOPE + MLA_V)
    k_nope, v = kv[..., :MLA_NOPE], kv[..., MLA_NOPE:]
    k_rope = apply_rope(k_rope, cos, sin)
    scale = (MLA_NOPE + MLA_ROPE) ** -0.5
    nb = S // Q_BLOCK
    qn_b = q_nope.reshape(B, nb, Q_BLOCK, MLA_HEADS, MLA_NOPE).transpose(1, 0, 2, 3, 4)
    qr_b = q_rope.reshape(B, nb, Q_BLOCK, MLA_HEADS, MLA_ROPE).transpose(1, 0, 2, 3, 4)
    key_pos = jnp.arange(S)

    def block(args):
        i, qn, qr = args
        s = (jnp.einsum('bqhd,bkhd->bhqk', qn, k_nope)
             + jnp.einsum('bqhr,bkr->bhqk', qr, k_rope)).astype(jnp.float32) * scale
        qpos = i * Q_BLOCK + jnp.arange(Q_BLOCK)
        s = jnp.where(key_pos[None, :] <= qpos[:, None], s, -jnp.inf)
        p = jax.nn.softmax(s, axis=-1).astype(v.dtype)
        return jnp.einsum('bhqk,bkhd->bqhd', p, v)

    o = lax.map(block, (jnp.arange(nb), qn_b, qr_b))
    return o.transpose(1, 0, 2, 3, 4).reshape(B, S, MLA_W)


def gla_chunk_scan(q, k, v, gk):
    B, S, H, DK = q.shape
    DV = v.shape[-1]
    causal = jnp.tril(jnp.ones((CHUNK, CHUNK), dtype=bool))
    xs = tuple(to_chunks(t.astype(jnp.float32)) for t in (q, k, v, gk))

    def step(state, xc):
        qc, kc, vc, gc = xc
        b = jnp.cumsum(gc, axis=-2)
        o_inter = jnp.einsum('bhcd,bhde->bhce', qc * jnp.exp(b), state)
        diff = b[:, :, :, None, :] - b[:, :, None, :, :]
        decay = jnp.exp(jnp.where(causal[:, :, None], diff, -jnp.inf))
        a = jnp.einsum('bhid,bhjd,bhijd->bhij', qc, kc, decay)
        o = o_inter + jnp.einsum('bhij,bhje->bhie', a, vc)
        b_last = b[:, :, -1:, :]
        new_state = (jnp.exp(b_last[:, :, 0, :, None]) * state
                     + jnp.einsum('bhcd,bhce->bhde', kc * jnp.exp(b_last - b), vc))
        return new_state, o

    init = jnp.zeros((B, H, DK, DV), jnp.float32)
    _, o = lax.scan(step, init, xs)
    return from_chunks(o)


def gla_branch(q, k, v, g_lr, r, w_gate2, gate_bias, out_norm):
    B, S, _ = q.shape
    qh = q.reshape(B, S, GLA_HEADS, GLA_DK) * (GLA_DK ** -0.5)
    kh = k.reshape(B, S, GLA_HEADS, GLA_DK)
    vh = v.reshape(B, S, GLA_HEADS, GLA_DV)
    gk = jax.nn.log_sigmoid((g_lr @ w_gate2 + gate_bias).astype(jnp.float32)) / GLA_GATE_NORM
    gk = gk.reshape(B, S, GLA_HEADS, GLA_DK)
    o = gla_chunk_scan(qh, kh, vh, gk)
    o = rms_norm(o, out_norm.reshape(GLA_HEADS, GLA_DV)).astype(v.dtype)
    return o.reshape(B, S, GLA_W) * jax.nn.silu(r)


def gated_delta_chunked(q, k, v, beta, g):
    B, S, H, DK = q.shape
    DV = v.shape[-1]
    qc, kc, vc = (to_chunks(t.astype(jnp.float32)) for t in (q, k, v))
    bc = to_chunks(beta[..., None].astype(jnp.float32))[..., 0]
    G = jnp.cumsum(to_chunks(g[..., None].astype(jnp.float32))[..., 0], axis=-1)
    causal = jnp.tril(jnp.ones((CHUNK, CHUNK), dtype=bool))
    strict = jnp.tril(jnp.ones((CHUNK, CHUNK), dtype=bool), -1)
    gam = jnp.exp(jnp.where(causal, G[..., :, None] - G[..., None, :], -jnp.inf))
    kk = jnp.einsum('nbhid,nbhjd->nbhij', kc, kc)
    t_mat = jnp.where(strict, bc[..., :, None] * gam * kk, 0.0) + jnp.eye(CHUNK, dtype=jnp.float32)
    w = lax.linalg.triangular_solve(t_mat, (bc * jnp.exp(G))[..., None] * kc,
                                    left_side=True, lower=True, unit_diagonal=True)
    u0 = lax.linalg.triangular_solve(t_mat, bc[..., None] * vc,
                                     left_side=True, lower=True, unit_diagonal=True)
    qk = gam * jnp.einsum('nbhid,nbhjd->nbhij', qc, kc)
    q_dec = qc * jnp.exp(G)[..., None]
    k_dec = kc * jnp.exp(G[..., -1:] - G)[..., None]
    g_tot = jnp.exp(G[..., -1])

    def step(state, xc):
        qd, wc, u0c, qkc, kd, gt = xc
        u = u0c - jnp.einsum('bhcd,bhde->bhce', wc, state)
        o = jnp.einsum('bhcd,bhde->bhce', qd, state) + jnp.einsum('bhij,bhje->bhie', qkc, u)
        new_state = gt[..., None, None] * state + jnp.einsum('bhcd,bhce->bhde', kd, u)
        return new_state, o

    init = jnp.zeros((B, H, DK, DV), jnp.float32)
    _, o = lax.scan(step, init, (q_dec, w, u0, qk, k_dec, g_tot))
    return from_chunks(o)


def gdn_branch(qkv, b_logit, a_logit, z, conv_w, a_log, dt_bias, out_norm):
    B, S, _ = qkv.shape
    qkv = jax.nn.silu(causal_dwconv(qkv, conv_w))
    q, k, v = jnp.split(qkv, [GDN_KW, 2 * GDN_KW], axis=-1)
    rep = GDN_V_HEADS // GDN_QK_HEADS
    q = jnp.repeat(l2_normalize(q.reshape(B, S, GDN_QK_HEADS, GDN_DK)), rep, axis=2) * (GDN_DK ** -0.5)
    k = jnp.repeat(l2_normalize(k.reshape(B, S, GDN_QK_HEADS, GDN_DK)), rep, axis=2)
    vh = v.reshape(B, S, GDN_V_HEADS, GDN_DV)
    beta = jax.nn.sigmoid(b_logit.astype(jnp.float32))
    g = -jnp.exp(a_log.astype(jnp.float32)) * jax.nn.softplus(
        a_logit.astype(jnp.float32) + dt_bias.astype(jnp.float32))
    o = gated_delta_chunked(q, k, vh, beta, g)
    o = rms_norm(o, out_norm.reshape(GDN_V_HEADS, GDN_DV)).astype(qkv.dtype)
    return o.reshape(B, S, GDN_W) * jax.nn.silu(z)


def token_mixer(h, cos, sin, w_in, mla_q_norm, mla_kv_norm, mla_w_uq, mla_w_ukv,
                gla_w_gate2, gla_gate_bias, gla_out_norm, gdn_conv_w, gdn_a_log,
                gdn_dt_bias, gdn_out_norm, w_branch_mla, w_branch_gla, w_branch_gdn, w_out):
    (c_q, c_kv, k_rope, gla_q, gla_k, gla_v, gla_g_lr, gla_r,
     gdn_qkv, gdn_b, gdn_a, gdn_z, merge) = jnp.split(h @ w_in, _split_points(), axis=-1)
    y_mla = mla_branch(c_q, c_kv, k_rope, cos, sin, mla_q_norm, mla_kv_norm, mla_w_uq, mla_w_ukv)
    y_gla = gla_branch(gla_q, gla_k, gla_v, gla_g_lr, gla_r, gla_w_gate2, gla_gate_bias, gla_out_norm)
    y_gdn = gdn_branch(gdn_qkv, gdn_b, gdn_a, gdn_z, gdn_conv_w, gdn_a_log, gdn_dt_bias, gdn_out_norm)
    g_mla, g_gla, g_gdn = jnp.split(jax.nn.sigmoid(merge), N_BRANCH, axis=-1)
    mixed = (g_mla * (y_mla @ w_branch_mla) + g_gla * (y_gla @ w_branch_gla)
             + g_gdn * (y_gdn @ w_branch_gdn))
    return mixed @ w_out


def conv_ffn(h, w_up, conv_w, conv_b, w_down):
    gate, up = jnp.split(h @ w_up, 2, axis=-1)
    gate = causal_dwconv(gate, conv_w) + conv_b
    return (jax.nn.silu(gate) * up) @ w_down


def setup_inputs(seed: int = 0) -> dict:
    key = jax.random.key(seed)
    ks = jax.random.split(key, 32)
    L = DEPTH
    f32 = jnp.float32

    def dense(k, fan_in, fan_out):
        return jax.random.normal(k, (L, fan_in, fan_out), f32) * fan_in ** -0.5

    def gain(k, *shape):
        return 1.0 + 0.02 * jax.random.normal(k, shape, f32)

    x = jax.random.normal(ks[0], (BATCH, SEQ, D_MODEL), f32)
    positions = (jax.random.randint(ks[1], (BATCH, 1), 0, 4096, dtype=jnp.int32)
                 + jnp.arange(SEQ, dtype=jnp.int32)[None, :]).astype(jnp.int32)
    dt = jnp.exp(jax.random.uniform(ks[14], (L, GDN_V_HEADS), f32,
                                    minval=math.log(1e-3), maxval=math.log(1e-1)))
    return {
        "x": x,
        "positions": positions,
        "attn_norm": gain(ks[2], L, D_MODEL),
        "w_in": dense(ks[3], D_MODEL, IN_DIM),
        "mla_q_norm": gain(ks[4], L, MLA_Q_RANK),
        "mla_kv_norm": gain(ks[5], L, MLA_KV_RANK),
        "mla_w_uq": dense(ks[6], MLA_Q_RANK, MLA_HEADS * (MLA_NOPE + MLA_ROPE)),
        "mla_w_ukv": dense(ks[7], MLA_KV_RANK, MLA_HEADS * (MLA_NOPE + MLA_V)),
        "gla_w_gate2": dense(ks[8], GLA_GATE_RANK, GLA_KW),
        "gla_gate_bias": 0.02 * jax.random.normal(ks[9], (L, GLA_KW), f32),
        "gla_out_norm": gain(ks[10], L, GLA_W),
        "gdn_conv_w": jax.random.normal(ks[11], (L, CONV_WIDTH, GDN_CONV_DIM), f32) * CONV_WIDTH ** -0.5,
        "gdn_a_log": jnp.log(jax.random.uniform(ks[12], (L, GDN_V_HEADS), f32, minval=1.0, maxval=16.0)),
        "gdn_dt_bias": dt + jnp.log(-jnp.expm1(-dt)),
        "gdn_out_norm": gain(ks[13], L, GDN_W),
        "w_branch_mla": dense(ks[15], MLA_W, D_MODEL),
        "w_branch_gla": dense(ks[16], GLA_W, D_MODEL),
        "w_branch_gdn": dense(ks[17], GDN_W, D_MODEL),
        "w_out": dense(ks[18], D_MODEL, D_MODEL),
        "ffn_norm": gain(ks[19], L, D_MODEL),
        "ffn_w_up": dense(ks[20], D_MODEL, 2 * D_FF),
        "ffn_conv_w": jax.random.normal(ks[21], (L, FFN_CONV, D_FF), f32) * FFN_CONV ** -0.5,
        "ffn_conv_b": 0.01 * jax.random.normal(ks[22], (L, D_FF), f32),
        "ffn_w_down": dense(ks[23], D_FF, D_MODEL),
        "final_norm": gain(ks[24], D_MODEL),
    }


def reference(x, positions, attn_norm, w_in, mla_q_norm, mla_kv_norm, mla_w_uq, mla_w_ukv,
              gla_w_gate2, gla_gate_bias, gla_out_norm, gdn_conv_w, gdn_a_log, gdn_dt_bias,
              gdn_out_norm, w_branch_mla, w_branch_gla, w_branch_gdn, w_out, ffn_norm,
              ffn_w_up, ffn_conv_w, ffn_conv_b, ffn_w_down, final_norm):
    cos, sin = rope_tables(positions)
    for l in range(DEPTH):
        h = rms_norm(x, attn_norm[l])
        x = x + token_mixer(h, cos, sin, w_in[l], mla_q_norm[l], mla_kv_norm[l], mla_w_uq[l],
                            mla_w_ukv[l], gla_w_gate2[l], gla_gate_bias[l], gla_out_norm[l],
                            gdn_conv_w[l], gdn_a_log[l], gdn_dt_bias[l], gdn_out_norm[l],
                            w_branch_mla[l], w_branch_gla[l], w_branch_gdn[l], w_out[l])
        h = rms_norm(x, ffn_norm[l])
        x = x + conv_ffn(h, ffn_w_up[l], ffn_conv_w[l], ffn_conv_b[l], ffn_w_down[l])
    return rms_norm(x, final_norm)
```

```python
import contextlib
import math
import numpy as np
import ml_dtypes
import concourse.bass as bass
import concourse.mybir as mybir
from concourse.bass_utils import run_bass_kernel_spmd

F32 = mybir.dt.float32
BF16 = mybir.dt.bfloat16
I32 = mybir.dt.int32
AF = mybir.ActivationFunctionType
ALU = mybir.AluOpType
AX = mybir.AxisListType

ENGS = ("pe", "dve", "act", "pool", "sp")
EPOCH = 60000
EPS = 1e-6

D = 2048
INW = 13408
DFF = 5632
NH = 8


def _sz(dt):
    return 2 if dt == BF16 else 4


class Buf:
    __slots__ = ("name", "w", "r", "dsem", "dcnt", "mw")

    def __init__(self, name):
        self.name = name
        self.w = None
        self.r = []
        self.dsem = None
        self.dcnt = 0
        self.mw = False


class T:
    __slots__ = ("t", "b")

    def __init__(self, t, name):
        self.t = t
        self.b = Buf(name)

    def __getitem__(self, k):
        return self.t[k]


class Sched:
    def __init__(self, nc, es):
        self.nc = nc
        self.es = es
        self.q = {e: [] for e in ENGS}
        self.cnt = {e: 0 for e in ENGS}
        self.semobj = {}
        self.esem = {}
        self.nsem = 0
        self.free_dsems = []
        self.dma_bufs = []
        for e in ENGS:
            self._new_eng_sem(e)
        self.seen = {e: {} for e in ENGS}
        self.n_ins = 0
        self.n_wait = 0
        self.arena = 16640
        self.uid = 0
        self.rr = 0
        self.stage_sb = []

    def _alloc_sem(self, name):
        s = self.es.enter_context(self.nc.semaphore(name))
        self.nsem += 1
        key = "s%d" % self.nsem
        self.semobj[key] = s
        return key

    def _new_eng_sem(self, e):
        self.esem[e] = self._alloc_sem("e_%s_%d" % (e, self.nsem))
        self.cnt[e] = 0

    def sb(self, name, shape, dt):
        self.uid += 1
        nbytes = int(np.prod(shape[1:])) * _sz(dt)
        off = (self.arena + 63) // 64 * 64
        assert off + nbytes <= 229376, ("SBUF overflow", name, off, nbytes)
        t = self.nc.alloc_sbuf_tensor_at("%s_%d" % (name, self.uid), list(shape), dt, offset=off)
        self.arena = off + nbytes
        r = T(t, name)
        self.stage_sb.append(r.b)
        return r

    def dram(self, name, shape, dt, kind="Internal"):
        t = self.nc.dram_tensor(name, list(shape), dt, kind=kind)
        return T(t.ap(), name)

    def _deps(self, eng, reads, writes):
        need = {}

        def add(ev):
            if ev is None:
                return
            k, v, src = ev
            if src == "pe" and eng == "pe":
                return
            if need.get(k, 0) < v:
                need[k] = v
        for b in reads:
            add(b.w)
        for b in writes:
            if not b.mw:
                add(b.w)
            for ev in b.r:
                add(ev)
        waits = []
        seen = self.seen[eng]
        for k, v in need.items():
            if seen.get(k, 0) >= v:
                continue
            seen[k] = v
            waits.append((k, v))
        return waits

    def _mark(self, ev, reads, writes):
        for b in reads:
            if b in writes:
                continue
            b.r = [x for x in b.r if x[0] != ev[0]] + [ev]
        for b in writes:
            b.w = ev
            b.r = []

    def op(self, eng, fn, reads=(), writes=()):
        reads = [x.b if isinstance(x, T) else x for x in reads]
        writes = [x.b if isinstance(x, T) else x for x in writes]
        waits = self._deps(eng, reads, writes)
        if self.cnt[eng] >= EPOCH:
            self._new_eng_sem(eng)
        self.cnt[eng] += 1
        ev = (self.esem[eng], self.cnt[eng], eng)
        self._mark(ev, reads, writes)
        self.q[eng].append((waits, fn, (self.esem[eng], 1)))
        self.n_ins += 1
        self.n_wait += len(waits)
        return ev

    def dma(self, fn, reads=(), writes=(), eng=None):
        reads = [x.b if isinstance(x, T) else x for x in reads]
        writes = [x.b if isinstance(x, T) else x for x in writes]
        if eng is None:
            eng = "sp"
        waits = self._deps(eng, reads, writes)
        sb = writes[0]
        if sb.dsem is None:
            if self.free_dsems:
                sb.dsem, sb.dcnt = self.free_dsems.pop()
            else:
                sb.dsem = self._alloc_sem("d_%s_%d" % (sb.name, self.nsem))
            self.dma_bufs.append(sb)
        sb.dcnt += 16
        ev = (sb.dsem, sb.dcnt, "dma")
        self._mark(ev, reads, writes)
        self.q[eng].append((waits, fn, (sb.dsem, 16)))
        self.n_ins += 1
        self.n_wait += len(waits)
        return ev

    def release_from(self, idx):
        for b in self.stage_sb[idx:]:
            if b.dsem is not None:
                self.free_dsems.append((b.dsem, b.dcnt))
                self.dma_bufs.remove(b)
                b.dsem = None
        del self.stage_sb[idx:]

    def barrier(self):
        waits = []
        seen = self.seen["sp"]
        for e in ENGS:
            if e == "sp":
                continue
            k, v = self.esem[e], self.cnt[e]
            if v > 0 and seen.get(k, 0) < v:
                seen[k] = v
                waits.append((k, v))
        for b in self.dma_bufs:
            if b.dcnt > 0 and seen.get(b.dsem, 0) < b.dcnt:
                seen[b.dsem] = b.dcnt
                waits.append((b.dsem, b.dcnt))
        if self.cnt["sp"] >= EPOCH:
            self._new_eng_sem("sp")
        self.cnt["sp"] += 1
        k, v = self.esem["sp"], self.cnt["sp"]
        self.q["sp"].append((waits, lambda e: e.nop(), (k, 1)))
        for e in ENGS:
            if e == "sp":
                continue
            self.seen[e][k] = v
            self.q[e].append(([(k, v)], None, None))

    def emit(self):
        nc = self.nc
        so = self.semobj
        with nc.Block() as block:
            def run(e):
                def body(engine):
                    for waits, fn, inc in self.q[e]:
                        for k, v in waits:
                            engine.wait_ge(so[k], v)
                        if fn is not None:
                            ins = fn(engine)
                            ins.then_inc(so[inc[0]], inc[1])
                return body
            block.tensor(run("pe"))
            block.vector(run("dve"))
            block.scalar(run("act"))
            block.gpsimd(run("pool"))
            block.sync(run("sp"))


NCOLV = 320
CV_GLN, CV_GDNN = 304, 312
CV_AN, CV_FN, CV_QN, CV_KVN, CV_GB, CV_DCW, CV_FCW, CV_FCB, CV_ALOG, CV_DTB, CV_FIN = (
    0, 16, 32, 36, 40, 44, 108, 240, 284, 285, 286)


class Prog:
    def __init__(self, nc, es, SEQ, DEPTH, debug=False, per_layer=False):
        self.nc = nc
        self.S = Sched(nc, es)
        self.SEQ = SEQ
        self.DEPTH = DEPTH
        self.debug = debug
        self.MT = min(1024, SEQ)
        self.ev = 0
        self.pi = 0
        S = self.S
        kind = "ExternalOutput" if debug else "Internal"
        self.dbg_kind = kind

        def ext(name, shape, dt=F32):
            t = nc.dram_tensor(name, list(shape), dt, kind="ExternalInput")
            return T(t.ap(), name)
        L = DEPTH
        self.x = ext("x", [SEQ, D])
        self.pos = ext("pos", [1, SEQ], I32)
        self.w_in = ext("w_in", [L, D, INW])
        self.w_uq = ext("w_uq", [L, 512, 1536])
        self.w_ukv = ext("w_ukv", [L, 512, 2048])
        self.w_g2 = ext("w_g2", [L, 16, 512])
        self.w_bm = ext("w_bm", [L, 1024, D])
        self.w_bg = ext("w_bg", [L, 1024, D])
        self.w_bd = ext("w_bd", [L, 1024, D])
        self.w_o = ext("w_o", [L, D, D])
        self.w_up = ext("w_up", [L, D, 2 * DFF])
        self.w_dn = ext("w_dn", [L, DFF, D])
        self.colv_in = ext("colv", [L, 128, NCOLV])
        self.rowv_in = ext("rowv", [L, 128, 2048])
        self.cst_in = ext("cst", [128, 8])
        self.identb_in = ext("identb", [128, 128], BF16)
        self.identf_in = ext("identf", [128, 128])
        self.masks_in = ext("masks", [128, 4 * 512], BF16)
        self.cmask_in = ext("cmask", [128, 6 * 128])
        out_t = nc.dram_tensor("out", [SEQ, D], F32, kind="ExternalOutput")
        self.out = T(out_t.ap(), "out")
        self.xres = None
        if per_layer:
            xr = nc.dram_tensor("xres", [SEQ, D], F32, kind="ExternalOutput")
            self.xres = T(xr.ap(), "xres")
            self.xres.b.mw = True

        dr = S.dram
        self.X = dr("Xs", [SEQ, D], F32)
        self.WIN = dr("WIN", [D, INW + 64], BF16)
        self.WUQ = dr("WUQ", [512, 1536 + 512], BF16)
        self.WUKV = dr("WUKV", [512, 2048], BF16)
        self.WG2 = dr("WG2", [16, 512], BF16)
        self.WBR = [dr("WBR%d" % i, [1024, D], BF16) for i in range(3)]
        self.WO = dr("WO", [D, D], BF16)
        self.WUP = dr("WUP", [D, 2 * DFF], BF16)
        self.WDN = dr("WDN", [DFF, D], BF16)
        self.QN = dr("QN", [NH * 128, SEQ], BF16, kind)
        self.QR = dr("QR", [NH * 64, SEQ], BF16, kind)
        self.KN = dr("KN", [NH * 128, SEQ], BF16, kind)
        self.KR = dr("KR", [64, SEQ], BF16, kind)
        self.V = dr("Vv", [SEQ, 1024], BF16, kind)
        self.GQ = dr("GQ", [512, SEQ], BF16, kind)
        self.GK = dr("GK", [512, SEQ], BF16, kind)
        self.GG = dr("GG", [512, SEQ], F32, kind)
        self.GV = dr("GV", [SEQ, 1024], BF16, kind)
        self.GR = dr("GR", [SEQ, 1024], BF16, kind)
        self.DQKV = dr("DQKV", [2048, SEQ], BF16, kind)
        self.DB = dr("DB", [8, SEQ], F32, kind)
        self.DA = dr("DA", [8, SEQ], F32, kind)
        self.DZ = dr("DZ", [SEQ, 1024], BF16, kind)
        self.MG = dr("MG", [3 * D, SEQ], BF16, kind)
        self.Y = [dr("Y%d" % i, [1024, SEQ], BF16, kind) for i in range(3)]
        for t in ([self.X, self.WIN, self.WUQ, self.WUKV, self.WG2, self.WO, self.WUP, self.WDN,
                   self.QN, self.QR, self.KN, self.KR, self.V, self.GQ, self.GK, self.GG, self.GV,
                   self.GR, self.DQKV, self.DB, self.DA, self.DZ, self.MG, self.out]
                  + self.WBR + self.Y):
            t.b.mw = True

        self.psf = [T(nc.alloc_psum_tensor("psf%d" % i, [128, 512], F32), "psf%d" % i) for i in range(6)]
        self.psb = [T(nc.alloc_psum_tensor("psb%d" % i, [128, 1024], BF16), "psb%d" % i) for i in range(2)]

        self.identb = S.sb("identb", [128, 128], BF16)
        self.identf = S.sb("identf", [128, 128], F32)
        self.onesb = S.sb("onesb", [128, 128], BF16)
        self.cst = S.sb("cst", [128, 8], F32)
        self.colv = S.sb("colv", [128, NCOLV], F32)
        self.ncolv = S.sb("ncolv", [128, 8], F32)
        ld = self.load
        ld(self.identb, self.identb[:], self.identb_in, self.identb_in[:, :])
        ld(self.identf, self.identf[:], self.identf_in, self.identf_in[:, :])
        ld(self.cst, self.cst[:], self.cst_in, self.cst_in[:, :])
        S.op("pool", lambda e: e.memset(self.onesb[:], 1.0), writes=[self.onesb])
        self.base_arena = S.arena
        self.persist = list(S.stage_sb)
        S.stage_sb = []

    def load(self, dT, dap, sT, sap, eng="sp"):
        self.S.dma(lambda e: e.dma_start(out=dap, in_=sap), reads=[sT], writes=[dT], eng=eng)

    def store(self, dT, dap, sT, sap, eng="pool"):
        self.S.dma(lambda e: e.dma_start(out=dap, in_=sap), reads=[sT], writes=[dT], eng=eng)

    def nps(self):
        self.pi += 1
        return self.psf[self.pi % 4]

    def copy_eng(self):
        self.ev += 1
        return "act" if self.ev % 2 else "dve"

    def copy(self, oT, oap, iT, iap, eng=None):
        eng = eng or self.copy_eng()
        if eng == "act":
            self.S.op("act", lambda e: e.activation(out=oap, in_=iap, func=AF.Copy), reads=[iT], writes=[oT])
        else:
            self.S.op(eng, lambda e: e.tensor_copy(out=oap, in_=iap), reads=[iT], writes=[oT])

    def act(self, oT, oap, iT, iap, func, scale=1.0, bias=0.0, extra_reads=()):
        self.S.op("act", lambda e: e.activation(out=oap, in_=iap, func=func, bias=bias, scale=scale),
                  reads=[iT] + list(extra_reads), writes=[oT])

    def tt(self, eng, oT, oap, aT, aap, bT, bap, op):
        self.S.op(eng, lambda e: e.tensor_tensor(out=oap, in0=aap, in1=bap, op=op), reads=[aT, bT], writes=[oT])

    def ts(self, eng, oT, oap, aT, aap, s1, s2, op0, op1=None, extra_reads=()):
        if op1 is None:
            self.S.op(eng, lambda e: e.tensor_scalar(out=oap, in0=aap, scalar1=s1, scalar2=None, op0=op0),
                      reads=[aT] + list(extra_reads), writes=[oT])
        else:
            self.S.op(eng, lambda e: e.tensor_scalar(out=oap, in0=aap, scalar1=s1, scalar2=s2, op0=op0, op1=op1),
                      reads=[aT] + list(extra_reads), writes=[oT])

    def stt(self, eng, oT, oap, aT, aap, scalar, bT, bap, op0, op1, extra_reads=()):
        self.S.op(eng, lambda e: e.scalar_tensor_tensor(out=oap, in0=aap, scalar=scalar, in1=bap, op0=op0, op1=op1),
                  reads=[aT, bT] + list(extra_reads), writes=[oT])

    def mm(self, psT, pap, lT, lap, rT, rap, start, stop):
        self.S.op("pe", lambda e: e.matmul(pap, lap, rap, start=start, stop=stop), reads=[lT, rT], writes=[psT])

    def tr(self, psT, pap, iT, iap, ident):
        self.S.op("pe", lambda e: e.transpose(pap, iap, ident), reads=[iT], writes=[psT])

    def end_stage(self):
        S = self.S
        S.barrier()
        S.arena = self.base_arena
        keep = set(id(b) for b in self.persist)
        for b in S.stage_sb:
            if id(b) in keep:
                continue
            if b.dsem is not None:
                S.free_dsems.append((b.dsem, b.dcnt))
                S.dma_bufs.remove(b)
                b.dsem = None
        S.stage_sb = []

    def stage_W(self, l):
        S = self.S
        cv = self.colv
        self.load(cv, cv[:], self.colv_in, self.colv_in[l, :, :])
        self.ts("dve", self.ncolv, self.ncolv[:, 0:4], cv, cv[:, CV_GB:CV_GB + 4], -1.0, None, ALU.mult)
        self.act(self.ncolv, self.ncolv[0:8, 4:5], cv, cv[0:8, CV_ALOG:CV_ALOG + 1], AF.Exp)
        self.ts("dve", self.ncolv, self.ncolv[0:8, 4:5], self.ncolv, self.ncolv[0:8, 4:5], -1.0, None, ALU.mult)
        stf = [S.sb("wstf%d" % i, [128, 2048], F32) for i in range(3)]
        stb = [S.sb("wstb%d" % i, [128, 2048], BF16) for i in range(3)]
        cnt = [0]

        def cast(src_ap, R, C, dstT, dc0=0, scol=None):
            for r0 in range(0, R, 128):
                rr = min(128, R - r0)
                for c0 in range(0, C, 2048):
                    cc = min(2048, C - c0)
                    i = cnt[0] % 3
                    cnt[0] += 1
                    f, b = stf[i], stb[i]
                    self.load(f, f[:rr, :cc], self.w_in, src_ap[r0:r0 + rr, c0:c0 + cc])
                    eng = "dve" if cnt[0] % 2 else "pool"
                    if scol is not None:
                        sc = cv[:rr, scol + r0 // 128: scol + r0 // 128 + 1]
                        self.ts(eng, b, b[:rr, :cc], f, f[:rr, :cc], sc, None, ALU.mult, extra_reads=[cv])
                    else:
                        self.copy(b, b[:rr, :cc], f, f[:rr, :cc], eng=eng)
                    self.store(dstT, dstT[r0:r0 + rr, dc0 + c0:dc0 + c0 + cc], b, b[:rr, :cc])
        cast(self.w_in[l], D, INW, self.WIN, 0, CV_AN)
        cast(self.w_in[l][:, 1056:1088], D, 32, self.WIN, INW, CV_AN)
        cast(self.w_in[l][:, 1024:1056], D, 32, self.WIN, INW + 32, CV_AN)
        cast(self.w_uq[l], 512, 1536, self.WUQ, 0, CV_QN)
        for h in range(NH):
            c = h * 192 + 128
            cast(self.w_uq[l][:, c + 32:c + 64], 512, 32, self.WUQ, 1536 + h * 64, CV_QN)
            cast(self.w_uq[l][:, c:c + 32], 512, 32, self.WUQ, 1536 + h * 64 + 32, CV_QN)
        cast(self.w_ukv[l], 512, 2048, self.WUKV, 0, CV_KVN)
        cast(self.w_g2[l], 16, 512, self.WG2)
        cast(self.w_bm[l], 1024, D, self.WBR[0])
        cast(self.w_bg[l], 1024, D, self.WBR[1], 0, CV_GLN)
        cast(self.w_bd[l], 1024, D, self.WBR[2], 0, CV_GDNN)
        cast(self.w_o[l], D, D, self.WO)
        cast(self.w_up[l], D, 2 * DFF, self.WUP, 0, CV_FN)
        cast(self.w_dn[l], DFF, D, self.WDN)
        self.end_stage()

    def norm_transpose(self, srcT, t0, MT, hT, xkeep=None):
        S = self.S
        mark = S.arena
        sidx = len(S.stage_sb)
        xin = [S.sb("xin%d" % i, [128, D], F32) for i in range(2)]
        xn = [S.sb("xn%d" % i, [128, D], BF16) for i in range(2)]
        junk = S.sb("junk", [128, D], BF16)
        ss = S.sb("ss", [128, 4], F32)
        for s in range(MT // 128):
            xt, xb = xin[s % 2], xn[s % 2]
            self.load(xt, xt[:], srcT, srcT[t0 + s * 128:t0 + (s + 1) * 128, :])
            S.op("act", lambda e, xt=xt: e.activation(out=junk[:], in_=xt[:], func=AF.Square, accum_out=ss[:, 0:1]),
                 reads=[xt], writes=[junk, ss])
            self.ts("dve", ss, ss[:, 1:2], ss, ss[:, 0:1], 1.0 / D, EPS, ALU.mult, ALU.add)
            self.act(ss, ss[:, 2:3], ss, ss[:, 1:2], AF.Sqrt)
            S.op("dve", lambda e: e.reciprocal(out=ss[:, 2:3], in_=ss[:, 2:3]), reads=[ss], writes=[ss])
            self.ts("dve", xb, xb[:], xt, xt[:], ss[:, 2:3], None, ALU.mult, extra_reads=[ss])
            for half in range(2):
                pb = self.psb[half]
                for k8 in range(8):
                    kc = half * 8 + k8
                    self.tr(pb, pb[:, k8 * 128:(k8 + 1) * 128], xb, xb[:, kc * 128:(kc + 1) * 128], self.identb[:])
                self.copy(hT, hT[:, half * 8:(half + 1) * 8, s * 128:(s + 1) * 128],
                          pb, pb[:, :].rearrange("p (k t) -> p k t", k=8))
        S.barrier()
        S.arena = mark
        S.release_from(sidx)

    def stage_A(self, l, m):
        S = self.S
        MT = self.MT
        t0 = m * MT
        NT = MT // 512
        NSUB = MT // 128
        cv = self.colv
        src = self.x if l == 0 else self.X
        Ct = S.sb("Ct", [64, MT], F32)
        Sg = S.sb("Sg", [64, MT], F32)
        mark = S.arena
        sidx = len(S.stage_sb)
        posi = S.sb("posi", [64, MT], I32)
        ang = S.sb("ang", [64, MT], F32)
        tmpa = S.sb("tmpa", [64, MT], F32)
        self.load(posi, posi[:], self.pos, self.pos[0:1, t0:t0 + MT].partition_broadcast(64))
        self.copy(ang, ang[:], posi, posi[:], eng="dve")
        self.ts("dve", ang, ang[:], ang, ang[:], self.cst[0:64, 0:1], None, ALU.mult, extra_reads=[self.cst])
        HI = 6.28125
        LO = 2.0 * math.pi - 6.28125

        def sin_of(dst, shift):
            src = ang
            if shift != 0.0:
                self.ts("dve", tmpa, tmpa[:], ang, ang[:], shift, None, ALU.add)
                src = tmpa
            self.ts("dve", posi, posi[:], src, src[:], 1.0 / (2.0 * math.pi), None, ALU.mult)
            self.copy(dst, dst[:], posi, posi[:], eng="dve")
            self.stt("dve", tmpa, tmpa[:], dst, dst[:], -HI, src, src[:], ALU.mult, ALU.add)
            self.stt("dve", tmpa, tmpa[:], dst, dst[:], -LO, tmpa, tmpa[:], ALU.mult, ALU.add)
            self.act(dst, dst[:], tmpa, tmpa[:], AF.Sin)
        sin_of(Sg, 0.0)
        self.ts("dve", Sg, Sg[:], Sg, Sg[:], self.cst[0:64, 1:2], None, ALU.mult, extra_reads=[self.cst])
        sin_of(Ct, 0.5 * math.pi)
        S.barrier()
        S.arena = mark
        S.release_from(sidx)
        hT = S.sb("hT", [128, 16, MT], BF16)
        self.norm_transpose(src, t0, MT, hT)
        r1 = S.sb("r1", [64, 512], F32)
        r2 = S.sb("r2", [64, 512], F32)
        wb = [S.sb("wb%d" % i, [128, 16, 512], BF16) for i in range(2)]
        stg = [S.sb("stg%d" % i, [128, MT], BF16) for i in range(3)]
        stgf = [S.sb("stgf%d" % i, [128, MT], F32) for i in range(2)]
        stt = [S.sb("stt%d" % i, [128, 512], BF16) for i in range(3)]
        cq = S.sb("cq", [128, 4, MT], BF16)
        ckv = S.sb("ckv", [128, 4, MT], BF16)
        kr = S.sb("kr", [64, 2, MT], F32)
        glr = S.sb("glr", [16, MT], BF16)
        ci = [0, 0, 0, 0]

        def wload(col0, ncols, nk=16, WT=None):
            WT = WT or self.WIN
            w = wb[ci[0] % 2]
            ci[0] += 1
            self.load(w, w[:, :nk, :ncols],
                      WT, WT[0:nk * 128, col0:col0 + ncols].rearrange("(k p) c -> p k c", p=128))
            return w

        def fm_group(col0, ncols, evac, after=None):
            w = wload(col0, ncols)
            for j in range((ncols + 127) // 128):
                M = min(128, ncols - j * 128)
                for ts_ in range(NT):
                    ps = self.nps()
                    for k in range(16):
                        self.mm(ps, ps[:M, :], w, w[:, k, j * 128:j * 128 + M], hT, hT[:, k, ts_ * 512:(ts_ + 1) * 512],
                                k == 0, k == 15)
                    evac(j, M, ts_, ps)
                if after is not None:
                    after(j, M)

        def to_dram(dstT, row0, func=None, scale=1.0, f32=False):
            cur = {}

            def evac(j, M, ts_, ps):
                if ts_ == 0:
                    if f32:
                        cur["s"] = stgf[ci[2] % 2]
                        ci[2] += 1
                    else:
                        cur["s"] = stg[ci[1] % 3]
                        ci[1] += 1
                s = cur["s"]
                o = s[:M, ts_ * 512:(ts_ + 1) * 512]
                if func is None and scale == 1.0:
                    self.copy(s, o, ps, ps[:M, :])
                elif func is None:
                    self.ts("dve", s, o, ps, ps[:M, :], scale, None, ALU.mult)
                else:
                    self.act(s, o, ps, ps[:M, :], func, scale=scale)

            def after(j, M):
                s = cur["s"]
                self.store(dstT, dstT[row0 + j * 128:row0 + j * 128 + M, t0:t0 + MT], s, s[:M, :])
            return evac, after

        def to_sb(dstT, dst3):
            def evac(j, M, ts_, ps):
                self.copy(dstT, dst3(j, M, ts_), ps, ps[:M, :])
            return evac

        fm_group(0, 512, to_sb(cq, lambda j, M, ts_: cq[:, j, ts_ * 512:(ts_ + 1) * 512]))
        fm_group(512, 512, to_sb(ckv, lambda j, M, ts_: ckv[:, j, ts_ * 512:(ts_ + 1) * 512]))
        fm_group(1024, 64, to_sb(kr, lambda j, M, ts_: kr[:, 0, ts_ * 512:(ts_ + 1) * 512]))
        fm_group(INW, 64, to_sb(kr, lambda j, M, ts_: kr[:, 1, ts_ * 512:(ts_ + 1) * 512]))
        ev, af = to_dram(self.GQ, 0, scale=128.0 ** -0.5)
        fm_group(1088, 512, ev, af)
        ev, af = to_dram(self.GK, 0)
        fm_group(1600, 512, ev, af)
        fm_group(3136, 16, to_sb(glr, lambda j, M, ts_: glr[:, ts_ * 512:(ts_ + 1) * 512]))
        for i in range(4):
            ev, af = to_dram(self.DQKV, i * 512)
            fm_group(4176 + i * 512, 512, ev, af)
        ev, af = to_dram(self.DB, 0, func=AF.Sigmoid, f32=True)
        fm_group(6224, 8, ev, af)
        cur = {}

        def ev_a(j, M, ts_, ps):
            if ts_ == 0:
                cur["s"] = stgf[ci[2] % 2]
                ci[2] += 1
            s = cur["s"]
            o = s[:8, ts_ * 512:(ts_ + 1) * 512]
            self.act(s, o, ps, ps[:8, :], AF.Exp, bias=cv[0:8, CV_DTB:CV_DTB + 1], extra_reads=[cv])
            self.act(s, o, s, o, AF.Ln, bias=1.0)
            self.ts("dve", s, o, s, o, self.ncolv[0:8, 4:5], None, ALU.mult, extra_reads=[self.ncolv])

        def af_a(j, M):
            s = cur["s"]
            self.store(self.DA, self.DA[0:8, t0:t0 + MT], s, s[:8, :])
        fm_group(6232, 8, ev_a, af_a)
        for i in range(12):
            ev, af = to_dram(self.MG, i * 512, func=AF.Sigmoid)
            fm_group(7264 + i * 512, 512, ev, af)

        def tm_group(col0, dstT, dcol0, func):
            w = wload(col0, 512)
            for s in range(NSUB):
                ps = self.nps()
                for k in range(16):
                    self.mm(ps, ps[:, :], hT, hT[:, k, s * 128:(s + 1) * 128], w, w[:, k, :], k == 0, k == 15)
                st = stt[ci[3] % 3]
                ci[3] += 1
                if func is None:
                    self.copy(st, st[:], ps, ps[:, :])
                else:
                    self.act(st, st[:], ps, ps[:, :], func)
                self.store(dstT, dstT[t0 + s * 128:t0 + (s + 1) * 128, dcol0:dcol0 + 512], st, st[:])
        tm_group(2112, self.GV, 0, None)
        tm_group(2624, self.GV, 512, None)
        tm_group(3152, self.GR, 0, AF.Silu)
        tm_group(3664, self.GR, 512, AF.Silu)
        tm_group(6240, self.DZ, 0, AF.Silu)
        tm_group(6752, self.DZ, 512, AF.Silu)

        wg = S.sb("wg", [16, 512], BF16)
        self.load(wg, wg[:], self.WG2, self.WG2[:, :])
        for j in range(4):
            s = stgf[ci[2] % 2]
            ci[2] += 1
            for ts_ in range(NT):
                ps = self.nps()
                self.mm(ps, ps[:, :], wg, wg[:, j * 128:(j + 1) * 128], glr, glr[:, ts_ * 512:(ts_ + 1) * 512], True, True)
                o = s[:, ts_ * 512:(ts_ + 1) * 512]
                self.act(s, o, ps, ps[:, :], AF.Exp, scale=-1.0, bias=self.ncolv[:, j:j + 1], extra_reads=[self.ncolv])
                self.act(s, o, s, o, AF.Ln, bias=1.0)
                self.ts("dve", s, o, s, o, -1.0 / 16.0, None, ALU.mult)
            self.store(self.GG, self.GG[j * 128:(j + 1) * 128, t0:t0 + MT], s, s[:, :])

        def rope(dstT, dap, aT, a_ap, bT, b_ap, sl):
            self.tt("dve", r1, r1[:], aT, a_ap, Ct, Ct[:, sl], ALU.mult)
            self.tt("dve", r2, r2[:], bT, b_ap, Sg, Sg[:, sl], ALU.mult)
            self.tt("dve", dstT, dap, r1, r1[:], r2, r2[:], ALU.add)
        s = stg[ci[1] % 3]
        ci[1] += 1
        for ts_ in range(NT):
            sl = slice(ts_ * 512, (ts_ + 1) * 512)
            rope(s, s[0:64, sl], kr, kr[:, 0, sl], kr, kr[:, 1, sl], sl)
        self.store(self.KR, self.KR[0:64, t0:t0 + MT], s, s[0:64, :])

        sq = S.sb("sq", [128, 4, 512], BF16)
        rstd = S.sb("rstd", [128, 512], F32)
        for lat in (cq, ckv):
            for ts_ in range(NT):
                sl = slice(ts_ * 512, (ts_ + 1) * 512)
                self.tt("pool", sq, sq[:], lat, lat[:, :, sl], lat, lat[:, :, sl], ALU.mult)
                ps = self.nps()
                for k in range(4):
                    self.mm(ps, ps[:, :], self.onesb, self.onesb[:], sq, sq[:, k, :], k == 0, k == 3)
                self.ts("dve", rstd, rstd[:], ps, ps[:, :], 1.0 / 512, EPS, ALU.mult, ALU.add)
                self.act(rstd, rstd[:], rstd, rstd[:], AF.Sqrt)
                S.op("dve", lambda e: e.reciprocal(out=rstd[:], in_=rstd[:]), reads=[rstd], writes=[rstd])
                for k in range(4):
                    self.tt("dve", lat, lat[:, k, sl], lat, lat[:, k, sl], rstd, rstd[:], ALU.mult)

        wq = S.sb("wq", [128, 4, 2048], BF16)
        self.load(wq, wq[:], self.WUQ, self.WUQ[:, :].rearrange("(k p) c -> p k c", p=128))
        for h in range(NH):
            s = stg[ci[1] % 3]
            ci[1] += 1
            s2 = stg[ci[1] % 3]
            ci[1] += 1
            for ts_ in range(NT):
                sl = slice(ts_ * 512, (ts_ + 1) * 512)
                ps = self.nps()
                for k in range(4):
                    self.mm(ps, ps[:, :], wq, wq[:, k, h * 192:h * 192 + 128], cq, cq[:, k, sl], k == 0, k == 3)
                self.copy(s, s[:, sl], ps, ps[:, :])
                pa = self.nps()
                for k in range(4):
                    self.mm(pa, pa[0:64, :], wq, wq[:, k, h * 192 + 128:h * 192 + 192], cq, cq[:, k, sl], k == 0, k == 3)
                pb = self.nps()
                for k in range(4):
                    self.mm(pb, pb[0:64, :], wq, wq[:, k, 1536 + h * 64:1536 + (h + 1) * 64], cq, cq[:, k, sl], k == 0, k == 3)
                rope(s2, s2[0:64, sl], pa, pa[0:64, :], pb, pb[0:64, :], sl)
            self.store(self.QN, self.QN[h * 128:(h + 1) * 128, t0:t0 + MT], s, s[:, :])
            self.store(self.QR, self.QR[h * 64:(h + 1) * 64, t0:t0 + MT], s2, s2[0:64, :])

        wkv = S.sb("wkv", [128, 4, 2048], BF16)
        self.load(wkv, wkv[:], self.WUKV, self.WUKV[:, :].rearrange("(k p) c -> p k c", p=128))
        for h in range(NH):
            s = stg[ci[1] % 3]
            ci[1] += 1
            for ts_ in range(NT):
                sl = slice(ts_ * 512, (ts_ + 1) * 512)
                ps = self.nps()
                for k in range(4):
                    self.mm(ps, ps[:, :], wkv, wkv[:, k, h * 256:h * 256 + 128], ckv, ckv[:, k, sl], k == 0, k == 3)
                self.copy(s, s[:, sl], ps, ps[:, :])
            self.store(self.KN, self.KN[h * 128:(h + 1) * 128, t0:t0 + MT], s, s[:, :])
        for s_ in range(NSUB):
            for hh in range(2):
                ps = self.nps()
                for k in range(4):
                    rhs = wkv[:, k, :].rearrange("p (h c) -> p h c", c=256)[:, hh * 4:(hh + 1) * 4, 128:256]
                    self.mm(ps, ps[:, :].rearrange("p (h c) -> p h c", c=128), ckv, ckv[:, k, s_ * 128:(s_ + 1) * 128],
                            wkv, rhs, k == 0, k == 3)
                st = stt[ci[3] % 3]
                ci[3] += 1
                self.copy(st, st[:], ps, ps[:, :])
                self.store(self.V, self.V[t0 + s_ * 128:t0 + (s_ + 1) * 128, hh * 512:(hh + 1) * 512], st, st[:])
        self.end_stage()

    def stage_B_mla(self):
        S = self.S
        SEQ = self.SEQ
        NKB = SEQ // 128
        NQT = SEQ // 512
        scale = 192.0 ** -0.5
        masks = S.sb("masks", [128, 4, 512], BF16)
        self.load(masks, masks[:], self.masks_in, self.masks_in[:, :].rearrange("p (j q) -> p j q", j=4))
        krT = S.sb("krT", [64, SEQ], BF16)
        self.load(krT, krT[:], self.KR, self.KR[:, :])
        knT = S.sb("knT", [128, SEQ], BF16)
        vT = S.sb("vT", [128, NKB, 128], BF16)
        qn = [S.sb("qn%d" % i, [128, 512], BF16) for i in range(2)]
        qr = [S.sb("qr%d" % i, [64, 512], BF16) for i in range(2)]
        pT = [S.sb("pT%d" % i, [128, 512], BF16) for i in range(3)]
        rec = S.sb("rec", [128, 512], F32)
        yst = [S.sb("yst%d" % i, [128, 512], BF16) for i in range(2)]
        po, pd = self.psf[4], self.psf[5]
        it = 0
        pi = 0
        for h in range(NH):
            self.load(knT, knT[:], self.KN, self.KN[h * 128:(h + 1) * 128, :])
            for kb0 in range(0, NKB, 16):
                nb = min(16, NKB - kb0)
                self.load(vT, vT[:, kb0:kb0 + nb, :], self.V,
                          self.V[kb0 * 128:(kb0 + nb) * 128, h * 128:(h + 1) * 128].rearrange("(kb p) c -> p kb c", p=128))
            for qt in range(NQT):
                a, b = qn[it % 2], qr[it % 2]
                self.load(a, a[:], self.QN, self.QN[h * 128:(h + 1) * 128, qt * 512:(qt + 1) * 512])
                self.load(b, b[:], self.QR, self.QR[h * 64:(h + 1) * 64, qt * 512:(qt + 1) * 512])
                nkb = 4 * qt + 4
                for kb in range(nkb):
                    ps = self.nps()
                    ksl = slice(kb * 128, (kb + 1) * 128)
                    self.mm(ps, ps[:, :], knT, knT[:, ksl], a, a[:], True, False)
                    self.mm(ps, ps[:, :], krT, krT[:, ksl], b, b[:], False, True)
                    p = pT[pi % 3]
                    pi += 1
                    self.act(p, p[:], ps, ps[:, :], AF.Exp, scale=scale)
                    if kb >= 4 * qt:
                        j = kb - 4 * qt
                        self.tt("pool", p, p[:], p, p[:], masks, masks[:, j, :], ALU.mult)
                    self.mm(po, po[:, :], vT, vT[:, kb, :], p, p[:], kb == 0, kb == nkb - 1)
                    self.mm(pd, pd[:, :], self.onesb, self.onesb[:], p, p[:], kb == 0, kb == nkb - 1)
                S.op("dve", lambda e: e.reciprocal(out=rec[:], in_=pd[:, :]), reads=[pd], writes=[rec])
                y = yst[it % 2]
                self.tt("dve", y, y[:], po, po[:, :], rec, rec[:], ALU.mult)
                self.store(self.Y[0], self.Y[0][h * 128:(h + 1) * 128, qt * 512:(qt + 1) * 512], y, y[:])
                it += 1
        self.end_stage()

    def mmt(self, psT, pap, lT, lap, rT, rap, start, stop, tp):
        self.S.op("pe", lambda e: e.matmul(pap, lap, rap, start=start, stop=stop, tile_position=tp),
                  reads=[lT, rT], writes=[psT])

    def stage_B_gla(self):
        S = self.S
        SEQ = self.SEQ
        TM = min(1024, SEQ)
        NTL = TM // 128
        NCH = TM // 64
        cm = S.sb("cm", [128, 6, 128], F32)
        self.load(cm, cm[:], self.cmask_in, self.cmask_in[:, :].rearrange("p (a b) -> p a b", a=6))
        St = S.sb("St", [128, 256], F32)
        Sb = S.sb("Sb", [128, 256], BF16)
        qT = S.sb("qT", [128, TM], BF16)
        kT = S.sb("kT", [128, TM], BF16)
        gk = S.sb("gk", [128, TM], F32)
        bb = S.sb("bb", [128, TM], F32)
        eb = S.sb("eb", [128, TM], F32)
        enb = S.sb("enb", [128, TM], F32)
        qd = S.sb("qd", [128, TM], BF16)
        kdn = S.sb("kdn", [128, TM], BF16)
        kdl = S.sb("kdl", [128, TM], BF16)
        kdl_tm = S.sb("kdl_tm", [128, NTL, 128], BF16)
        v_tm = S.sb("v_tm", [128, NTL, 256], BF16)
        r_tm = S.sb("r_tm", [128, NTL, 256], BF16)
        o_sb = S.sb("o_sb", [128, NTL, 256], F32)
        y_tm = S.sb("y_tm", [128, NTL, 256], BF16)
        yfm = [S.sb("yfm%d" % i, [128, TM], BF16) for i in range(2)]
        ssq = S.sb("ssq", [128, 2 * NTL], F32)
        junk = S.sb("junkg", [128, 256], BF16)
        AT = [S.sb("AT%d" % i, [128, 128], BF16) for i in range(2)]
        po = [self.psf[4], self.psf[5]]
        for h in range(4):
            S.op("pool", lambda e: e.memset(St[:], 0.0), writes=[St])
            S.op("pool", lambda e: e.memset(Sb[:], 0.0), writes=[Sb])
            for m in range(SEQ // TM):
                t0 = m * TM
                rows = slice(h * 128, (h + 1) * 128)
                self.load(qT, qT[:], self.GQ, self.GQ[rows, t0:t0 + TM])
                self.load(kT, kT[:], self.GK, self.GK[rows, t0:t0 + TM])
                self.load(gk, gk[:], self.GG, self.GG[rows, t0:t0 + TM])
                self.load(v_tm, v_tm[:], self.GV,
                          self.GV[t0:t0 + TM, h * 256:(h + 1) * 256].rearrange("(n p) c -> p n c", p=128))
                self.load(r_tm, r_tm[:], self.GR,
                          self.GR[t0:t0 + TM, h * 256:(h + 1) * 256].rearrange("(n p) c -> p n c", p=128))
                for c in range(NCH):
                    sl = slice(c * 64, (c + 1) * 64)
                    S.op("dve", lambda e, sl=sl: e.tensor_tensor_scan(out=bb[:, sl], data0=gk[:, sl], data1=gk[:, sl],
                                                                      initial=0.0, op0=ALU.add, op1=ALU.bypass),
                         reads=[gk], writes=[bb])
                self.act(eb, eb[:], bb, bb[:], AF.Exp)
                self.act(enb, enb[:], bb, bb[:], AF.Exp, scale=-1.0)
                self.tt("dve", qd, qd[:], qT, qT[:], eb, eb[:], ALU.mult)
                self.tt("pool", kdn, kdn[:], kT, kT[:], enb, enb[:], ALU.mult)
                for c in range(NCH):
                    sl = slice(c * 64, (c + 1) * 64)
                    self.ts("pool" if c % 2 else "dve", kdl, kdl[:, sl], kdn, kdn[:, sl],
                            eb[:, c * 64 + 63:c * 64 + 64], None, ALU.mult, extra_reads=[eb])
                pb = self.psb[0]
                for n in range(NTL):
                    self.tr(pb, pb[:, n * 128:(n + 1) * 128], kdl, kdl[:, n * 128:(n + 1) * 128], self.identb[:])
                self.copy(kdl_tm, kdl_tm[:], pb, pb[:, :].rearrange("p (n c) -> p n c", n=NTL))
                for n in range(NTL):
                    tsl = slice(n * 128, (n + 1) * 128)
                    pa = self.nps()
                    self.mm(pa, pa[:, 0:128], kdn, kdn[:, tsl], qd, qd[:, tsl], True, True)
                    at = AT[n % 2]
                    self.tt("dve", at, at[:], pa, pa[:, 0:128], cm, cm[:, 0, :], ALU.mult)
                    p_o = po[n % 2]
                    for c in range(2):
                        r0 = 64 * c
                        csl = slice(n * 128 + r0, n * 128 + r0 + 64)
                        self.mmt(p_o, p_o[r0:r0 + 64, 0:256], qd, qd[:, csl], Sb, Sb[:], True, False, (0, r0))
                        self.mmt(p_o, p_o[r0:r0 + 64, 0:256], at, at[r0:r0 + 64, r0:r0 + 64],
                                 v_tm, v_tm[r0:r0 + 64, n, :], False, True, (r0, r0))
                        pst = self.nps()
                        self.mm(pst, pst[:, 0:256], kdl_tm, kdl_tm[r0:r0 + 64, n, :], v_tm, v_tm[r0:r0 + 64, n, :], True, True)
                        ecol = n * 128 + r0 + 63
                        self.stt("dve", St, St[:], St, St[:], eb[:, ecol:ecol + 1], pst, pst[:, 0:256],
                                 ALU.mult, ALU.add, extra_reads=[eb])
                        self.copy(Sb, Sb[:], St, St[:], eng="act")
                    S.op("act", lambda e, p_o=p_o, n=n: e.activation(out=junk[:], in_=p_o[:, 0:256], func=AF.Square,
                                                                     accum_out=ssq[:, n:n + 1]),
                         reads=[p_o], writes=[junk, ssq])
                    self.copy(o_sb, o_sb[:, n, :], p_o, p_o[:, 0:256], eng="dve")
                self.ts("dve", ssq, ssq[:, NTL:2 * NTL], ssq, ssq[:, 0:NTL], 1.0 / 256, EPS, ALU.mult, ALU.add)
                self.act(ssq, ssq[:, NTL:2 * NTL], ssq, ssq[:, NTL:2 * NTL], AF.Sqrt)
                S.op("dve", lambda e: e.reciprocal(out=ssq[:, NTL:2 * NTL], in_=ssq[:, NTL:2 * NTL]), reads=[ssq], writes=[ssq])
                for n in range(NTL):
                    self.stt("dve", y_tm, y_tm[:, n, :], o_sb, o_sb[:, n, :],
                             ssq[:, NTL + n:NTL + n + 1], r_tm, r_tm[:, n, :], ALU.mult, ALU.mult, extra_reads=[ssq])
                for half in range(2):
                    pb = self.psb[1 - half % 2] if False else self.psb[half]
                    for n in range(NTL):
                        self.tr(pb, pb[:, n * 128:(n + 1) * 128], y_tm, y_tm[:, n, half * 128:(half + 1) * 128], self.identb[:])
                    yf = yfm[half]
                    self.copy(yf, yf[:], pb, pb[:, 0:TM])
                    r_ = h * 256 + half * 128
                    self.store(self.Y[1], self.Y[1][r_:r_ + 128, t0:t0 + TM], yf, yf[:])
        self.end_stage()

    def stage_B_gdn(self):
        S = self.S
        SEQ = self.SEQ
        TM = min(1024, SEQ)
        NTL = TM // 128
        cv = self.colv
        cm = S.sb("cm", [128, 6, 128], F32)
        self.load(cm, cm[:], self.cmask_in, self.cmask_in[:, :].rearrange("p (a b) -> p a b", a=6))
        St = [S.sb("St%d" % i, [128, 128], F32) for i in range(8)]
        Sb = [S.sb("Sb%d" % i, [128, 128], BF16) for i in range(8)]
        for i in range(8):
            S.op("pool", lambda e, i=i: e.memset(St[i][:], 0.0), writes=[St[i]])
            S.op("pool", lambda e, i=i: e.memset(Sb[i][:], 0.0), writes=[Sb[i]])
        xt = [S.sb("xt%d" % i, [128, TM + 4], BF16) for i in range(2)]
        acc = S.sb("acc", [128, TM], F32)
        ysil = S.sb("ysil", [128, TM], F32)
        sq = S.sb("sqd", [128, TM], BF16)
        rn = S.sb("rn", [128, 512], F32)
        qnT = S.sb("qnT", [128, TM], BF16)
        knT = S.sb("knT", [128, TM], BF16)
        vT = S.sb("vTd", [128, TM], BF16)
        k_tm = S.sb("k_tm", [128, NTL, 128], BF16)
        v_tm = [S.sb("v_tm%d" % i, [128, NTL, 128], BF16) for i in range(2)]
        z_tm = [S.sb("z_tm%d" % i, [128, NTL, 128], BF16) for i in range(2)]
        o_sb = [S.sb("o_sb%d" % i, [128, NTL, 128], F32) for i in range(2)]
        ssq = [S.sb("ssq%d" % i, [128, 2 * NTL], F32) for i in range(2)]
        y_tm = S.sb("y_tmd", [128, NTL, 128], BF16)
        yfm = [S.sb("yfmd%d" % i, [128, TM], BF16) for i in range(2)]
        bgr = S.sb("bgr", [8, 2, TM], F32)
        bg_tm = S.sb("bg_tm", [128, NTL, 16], F32)
        Gc = S.sb("Gc", [128, NTL, 16], F32)
        dG = S.sb("dG", [128, NTL, 8], F32)
        eGc = S.sb("eGc", [128, NTL, 8], F32)
        KKs = S.sb("KKs", [128, 128], F32)
        QKs = S.sb("QKs", [128, 128], F32)
        junk = S.sb("junkd", [128, 128], BF16)

        def f32t(name):
            return [S.sb("%s%d" % (name, i), [128, 128], F32) for i in range(2)]

        def b16t(name):
            return [S.sb("%s%d" % (name, i), [128, 128], BF16) for i in range(2)]
        Dm, gamT, eG, gs, gm, Xa, XTa, Pm, Ya, YTa, u0 = [f32t(n_) for n_ in
                                                         ("Dm", "gamT", "eG", "gs", "gm", "Xa", "XTa", "Pm", "Ya", "YTa", "u0")]
        qkT, Pb, RV2, w_tm, wT, qdT, kdec, u_tm = [b16t(n_) for n_ in ("qkT", "Pb", "RVx", "w_tm", "wT", "qdT", "kdec", "u_tm")]
        RV = [S.sb("RV%d" % i, [128, 256], BF16) for i in range(2)]
        po = [self.psf[4], self.psf[5]]
        itc = [0]

        def conv_silu(kc, t0, first):
            x = xt[itc[0] % 2]
            itc[0] += 1
            rows = slice(kc * 128, (kc + 1) * 128)
            if first:
                S.op("pool", lambda e: e.memset(x[:, 0:4], 0.0), writes=[x])
                self.load(x, x[:, 4:4 + TM], self.DQKV, self.DQKV[rows, t0:t0 + TM])
            else:
                self.load(x, x[:, 0:4 + TM], self.DQKV, self.DQKV[rows, t0 - 4:t0 + TM])

            def w(tap):
                c_ = CV_DCW + tap * 16 + kc
                return cv[:, c_:c_ + 1]
            self.ts("dve", acc, acc[:], x, x[:, 4:4 + TM], w(3), None, ALU.mult, extra_reads=[cv])
            for tap in (2, 1, 0):
                self.stt("dve", acc, acc[:], x, x[:, 1 + tap:1 + tap + TM], w(tap), acc, acc[:], ALU.mult, ALU.add,
                         extra_reads=[cv])
            self.act(ysil, ysil[:], acc, acc[:], AF.Silu)

        def l2n(dst, qscale):
            self.tt("pool", sq, sq[:], ysil, ysil[:], ysil, ysil[:], ALU.mult)
            for b_ in range(TM // 512):
                sl = slice(b_ * 512, (b_ + 1) * 512)
                ps = self.nps()
                self.mm(ps, ps[:, :], self.onesb, self.onesb[:], sq, sq[:, sl], True, True)
                self.ts("dve", rn, rn[:], ps, ps[:, :], EPS, None, ALU.add)
                self.act(rn, rn[:], rn, rn[:], AF.Sqrt)
                S.op("dve", lambda e: e.reciprocal(out=rn[:], in_=rn[:]), reads=[rn], writes=[rn])
                self.stt("dve", dst, dst[:, sl], ysil, ysil[:, sl], qscale, rn, rn[:], ALU.mult, ALU.mult)

        def to_tm(dst, src):
            pb = self.psb[itc[0] % 2]
            itc[0] += 1
            for n in range(NTL):
                self.tr(pb, pb[:, n * 128:(n + 1) * 128], src, src[:, n * 128:(n + 1) * 128], self.identb[:])
            self.copy(dst, dst[:], pb, pb[:, 0:TM].rearrange("p (n c) -> p n c", n=NTL))

        for m in range(SEQ // TM):
            t0 = m * TM
            first = (m == 0)
            self.load(bgr, bgr[:, 0, :], self.DB, self.DB[0:8, t0:t0 + TM])
            self.load(bgr, bgr[:, 1, :], self.DA, self.DA[0:8, t0:t0 + TM])
            ps = self.nps()
            for n in range(NTL):
                for a_ in range(2):
                    self.tr(ps, ps[:, n * 16 + a_ * 8:n * 16 + a_ * 8 + 8], bgr, bgr[:, a_, n * 128:(n + 1) * 128],
                            self.identf[0:8, 0:8])
            self.copy(bg_tm, bg_tm[:], ps, ps[:, 0:NTL * 16].rearrange("p (n c) -> p n c", n=NTL), eng="dve")
            ps = self.nps()
            for n in range(NTL):
                self.mm(ps, ps[:, n * 16:n * 16 + 8], cm, cm[:, 0, :], bg_tm, bg_tm[:, n, 8:16], True, True)
                self.mm(ps, ps[:, n * 16 + 8:n * 16 + 16], cm, cm[:, 2, :], bg_tm, bg_tm[:, n, 8:16], True, True)
            self.copy(Gc, Gc[:], ps, ps[:, 0:NTL * 16].rearrange("p (n c) -> p n c", n=NTL), eng="dve")
            self.tt("dve", dG, dG[:], Gc, Gc[:, :, 8:16], Gc, Gc[:, :, 0:8], ALU.subtract)
            self.act(dG, dG[:], dG, dG[:], AF.Exp)
            self.act(eGc, eGc[:], Gc, Gc[:, :, 0:8], AF.Exp)
            import os
            STOP = int(os.environ.get("GDN_STOP", "99"))
            if STOP <= 1:
                continue
            for hq in range(4):
                conv_silu(hq, t0, first)
                l2n(qnT, 128.0 ** -0.5)
                conv_silu(4 + hq, t0, first)
                l2n(knT, 1.0)
                to_tm(k_tm, knT)
                for a_ in range(2):
                    hv = 2 * hq + a_
                    conv_silu(8 + hv, t0, first)
                    self.copy(vT, vT[:], ysil, ysil[:], eng="pool")
                    to_tm(v_tm[a_], vT)
                    self.load(z_tm[a_], z_tm[a_][:], self.DZ,
                              self.DZ[t0:t0 + TM, hv * 128:(hv + 1) * 128].rearrange("(n p) c -> p n c", p=128))
                if STOP <= 2:
                    continue
                for n in range(NTL):
                    tsl = slice(n * 128, (n + 1) * 128)
                    ps = self.nps()
                    self.mm(ps, ps[:, 0:128], knT, knT[:, tsl], knT, knT[:, tsl], True, True)
                    self.copy(KKs, KKs[:], ps, ps[:, 0:128], eng="act")
                    ps = self.nps()
                    self.mm(ps, ps[:, 0:128], knT, knT[:, tsl], qnT, qnT[:, tsl], True, True)
                    self.copy(QKs, QKs[:], ps, ps[:, 0:128], eng="act")
                    for a_ in range(2):
                        hv = 2 * hq + a_
                        bcol = bg_tm[:, n, hv:hv + 1]
                        gcol = bg_tm[:, n, 8 + hv:9 + hv]
                        pg = self.nps()
                        self.mm(pg, pg[:, 0:128], bg_tm, gcol.to_broadcast([128, 128]), cm, cm[:, 0, :], True, True)
                        self.ts("dve", Dm[a_], Dm[a_][:], pg, pg[:, 0:128], Gc[:, n, hv:hv + 1], 0.0,
                                ALU.subtract, ALU.min, extra_reads=[Gc])
                        self.act(gamT[a_], gamT[a_][:], Dm[a_], Dm[a_][:], AF.Exp)
                        self.act(eG[a_], eG[a_][:], pg, pg[:, 0:128], AF.Exp)
                        self.tt("pool", gs[a_], gs[a_][:], gamT[a_], gamT[a_][:], cm, cm[:, 1, :], ALU.mult)
                        self.tt("pool", gm[a_], gm[a_][:], gamT[a_], gamT[a_][:], cm, cm[:, 0, :], ALU.mult)
                        X = Xa[a_]
                        self.stt("dve", X, X[:], KKs, KKs[:], bcol, gs[a_], gs[a_][:], ALU.mult, ALU.mult,
                                 extra_reads=[bg_tm])
                        self.tt("dve", qkT[a_], qkT[a_][:], QKs, QKs[:], gm[a_], gm[a_][:], ALU.mult)
                        px = self.nps()
                        self.tr(px, px[:, 0:128], X, X[:], self.identf[:])
                        XT = XTa[a_]
                        self.copy(XT, XT[:], px, px[:, 0:128], eng="act")
                        if STOP <= 3:
                            continue
                        P_ = Pm[a_]
                        self.tt("pool", P_, P_[:], cm, cm[:, 4, :], X, X[:], ALU.subtract)
                        Y, YT = X, XT
                        for k in range(1, 6):
                            pyt = self.nps()
                            self.mm(pyt, pyt[:, 0:128], Y, Y[:], YT, YT[:], True, True)
                            if k < 5:
                                py = self.nps()
                                self.mm(py, py[:, 0:128], YT, YT[:], Y, Y[:], True, True)
                            YTn = YTa[a_] if YT is XT else XT
                            Yn = Ya[a_] if Y is X else X
                            self.copy(YTn, YTn[:], pyt, pyt[:, 0:128], eng="act")
                            if k < 5:
                                self.copy(Yn, Yn[:], py, py[:, 0:128], eng="dve")
                            pp = self.nps()
                            self.mm(pp, pp[:, 0:128], YTn, YTn[:], P_, P_[:], True, True)
                            if k < 5:
                                self.tt("dve", P_, P_[:], P_, P_[:], pp, pp[:, 0:128], ALU.add)
                            else:
                                self.tt("dve", Pb[a_], Pb[a_][:], P_, P_[:], pp, pp[:, 0:128], ALU.add)
                            Y, YT = Yn, YTn
                        if STOP <= 4:
                            continue
                        rv = RV[a_]
                        self.ts("pool", rv, rv[:, 0:128], k_tm, k_tm[:, n, :], eGc[:, n, hv:hv + 1], None, ALU.mult,
                                extra_reads=[eGc])
                        self.copy(rv, rv[:, 128:256], v_tm[a_], v_tm[a_][:, n, :], eng="pool")
                        pw = self.nps()
                        self.mm(pw, pw[:, 0:256], Pb[a_], Pb[a_][:], rv, rv[:], True, True)
                        self.ts("dve", w_tm[a_], w_tm[a_][:], pw, pw[:, 0:128], bcol, None, ALU.mult, extra_reads=[bg_tm])
                        self.ts("dve", u0[a_], u0[a_][:], pw, pw[:, 128:256], bcol, None, ALU.mult, extra_reads=[bg_tm])
                        pb = self.psb[a_]
                        self.tr(pb, pb[:, 0:128], w_tm[a_], w_tm[a_][:], self.identb[:])
                        self.copy(wT[a_], wT[a_][:], pb, pb[:, 0:128], eng="act")
                        self.tt("pool", qdT[a_], qdT[a_][:], qnT, qnT[:, tsl], eG[a_], eG[a_][:], ALU.mult)
                        self.ts("pool", kdec[a_], kdec[a_][:], k_tm, k_tm[:, n, :], dG[:, n, hv:hv + 1], None, ALU.mult,
                                extra_reads=[dG])
                    if STOP <= 5:
                        continue
                    for c in range(2):
                        r0 = 64 * c
                        rs = slice(r0, r0 + 64)
                        for a_ in range(2):
                            hv = 2 * hq + a_
                            SUB = int(os.environ.get("GDN_SUB", "0"))
                            pu = self.nps()
                            self.mm(pu, pu[:, 0:128], wT[a_], wT[a_][:], Sb[hv], Sb[hv][:], True, True)
                            if not SUB & 16:
                                self.tt("dve", u_tm[a_], u_tm[a_][rs, :], u0[a_], u0[a_][rs, :], pu, pu[rs, 0:128], ALU.subtract)
                            p_o = po[a_]
                            if not SUB & 1:
                                self.mm(p_o, p_o[:, 0:128], qdT[a_], qdT[a_][:], Sb[hv], Sb[hv][:], True, False)
                                self.mm(p_o, p_o[:, 0:128], qkT[a_], qkT[a_][rs, :], u_tm[a_], u_tm[a_][rs, :], False, True)
                            else:
                                self.mm(p_o, p_o[:, 0:128], qdT[a_], qdT[a_][:], Sb[hv], Sb[hv][:], True, True)
                            if not SUB & 4:
                                self.copy(o_sb[a_], o_sb[a_][rs, n, :], p_o, p_o[rs, 0:128], eng="act")
                            if not SUB & 2:
                                S.op("act", lambda e, p_o=p_o, a_=a_, rs=rs, n=n: e.activation(
                                    out=junk[rs, :], in_=p_o[rs, 0:128], func=AF.Square, accum_out=ssq[a_][rs, n:n + 1]),
                                    reads=[p_o], writes=[junk, ssq[a_]])
                            if not SUB & 8:
                                pss = self.nps()
                                self.mm(pss, pss[:, 0:128], kdec[a_], kdec[a_][rs, :], u_tm[a_], u_tm[a_][rs, :], True, True)
                                self.stt("dve", St[hv], St[hv][:], St[hv], St[hv][:], eG[a_][:, r0 + 63:r0 + 64],
                                         pss, pss[:, 0:128], ALU.mult, ALU.add, extra_reads=[eG[a_]])
                                self.copy(Sb[hv], Sb[hv][:], St[hv], St[hv][:], eng="act")
                if STOP <= 6:
                    continue
                for a_ in range(2):
                    hv = 2 * hq + a_
                    sq_ = ssq[a_]
                    self.ts("dve", sq_, sq_[:, NTL:2 * NTL], sq_, sq_[:, 0:NTL], 1.0 / 128, EPS, ALU.mult, ALU.add)
                    self.act(sq_, sq_[:, NTL:2 * NTL], sq_, sq_[:, NTL:2 * NTL], AF.Sqrt)
                    S.op("dve", lambda e, sq_=sq_: e.reciprocal(out=sq_[:, NTL:2 * NTL], in_=sq_[:, NTL:2 * NTL]),
                         reads=[sq_], writes=[sq_])
                    for n in range(NTL):
                        self.stt("dve", y_tm, y_tm[:, n, :], o_sb[a_], o_sb[a_][:, n, :], sq_[:, NTL + n:NTL + n + 1],
                                 z_tm[a_], z_tm[a_][:, n, :], ALU.mult, ALU.mult, extra_reads=[sq_])
                    pb = self.psb[a_]
                    for n in range(NTL):
                        self.tr(pb, pb[:, n * 128:(n + 1) * 128], y_tm, y_tm[:, n, :], self.identb[:])
                    yf = yfm[a_]
                    self.copy(yf, yf[:], pb, pb[:, 0:TM])
                    self.store(self.Y[2], self.Y[2][hv * 128:(hv + 1) * 128, t0:t0 + TM], yf, yf[:])
        self.end_stage()

    def stage_C(self, l):
        S = self.S
        SEQ = self.SEQ
        TC = 512
        src = self.x if l == 0 else self.X
        wbr = [S.sb("wbr%d" % i, [128, 8, D], BF16) for i in range(3)]
        for b in range(3):
            self.load(wbr[b], wbr[b][:], self.WBR[b], self.WBR[b][:, :].rearrange("(k p) c -> p k c", p=128))
        yb = [S.sb("yb%d" % i, [128, 8, TC], BF16) for i in range(3)]
        mixed = S.sb("mixed", [128, 16, TC], BF16)
        gt = [S.sb("gt%d" % i, [128, 3, TC], BF16) for i in range(2)]
        t1 = S.sb("t1", [128, TC], F32)
        t2 = S.sb("t2", [128, TC], F32)
        wo = [S.sb("wo%d" % i, [128, 16, 512], BF16) for i in range(2)]
        xs = [S.sb("xs%d" % i, [128, D], F32) for i in range(2)]
        pbk = [self.psf[0], self.psf[1], self.psf[2]]
        wi = 0
        xi = 0
        for m in range(SEQ // TC):
            t0 = m * TC
            for b in range(3):
                self.load(yb[b], yb[b][:], self.Y[b], self.Y[b][:, t0:t0 + TC].rearrange("(k p) t -> p k t", p=128))
            for j in range(16):
                g = gt[j % 2]
                self.load(g, g[:], self.MG,
                          self.MG[:, t0:t0 + TC].rearrange("(b j p) t -> p b j t", b=3, p=128)[:, :, j, :])
                for b in range(3):
                    ps = pbk[b]
                    for k in range(8):
                        self.mm(ps, ps[:, :], wbr[b], wbr[b][:, k, j * 128:(j + 1) * 128], yb[b], yb[b][:, k, :], k == 0, k == 7)
                self.tt("dve", t1, t1[:], pbk[0], pbk[0][:, :], g, g[:, 0, :], ALU.mult)
                self.tt("dve", t2, t2[:], pbk[1], pbk[1][:, :], g, g[:, 1, :], ALU.mult)
                self.tt("pool", t1, t1[:], t1, t1[:], t2, t2[:], ALU.add)
                self.tt("dve", t2, t2[:], pbk[2], pbk[2][:, :], g, g[:, 2, :], ALU.mult)
                self.tt("pool", mixed, mixed[:, j, :], t1, t1[:], t2, t2[:], ALU.add)
            xt = []
            for s_ in range(4):
                x = xs[s_ % 2]
                xt.append(x)
            for pair in range(2):
                for s2 in range(2):
                    s_ = pair * 2 + s2
                    x = xs[s2]
                    self.load(x, x[:], src, src[t0 + s_ * 128:t0 + (s_ + 1) * 128, :])
                for cb in range(4):
                    w = wo[wi % 2]
                    wi += 1
                    self.load(w, w[:], self.WO, self.WO[:, cb * 512:(cb + 1) * 512].rearrange("(k p) c -> p k c", p=128))
                    for s2 in range(2):
                        s_ = pair * 2 + s2
                        x = xs[s2]
                        ps = self.psf[3 + s2]
                        for k in range(16):
                            self.mm(ps, ps[:, :], mixed, mixed[:, k, s_ * 128:(s_ + 1) * 128], w, w[:, k, :], k == 0, k == 15)
                        csl = slice(cb * 512, (cb + 1) * 512)
                        self.tt("dve", x, x[:, csl], x, x[:, csl], ps, ps[:, :], ALU.add)
                for s2 in range(2):
                    s_ = pair * 2 + s2
                    x = xs[s2]
                    self.store(self.X, self.X[t0 + s_ * 128:t0 + (s_ + 1) * 128, :], x, x[:])
        self.end_stage()

    def stage_D(self, l, last):
        S = self.S
        SEQ = self.SEQ
        TD = 512
        cv = self.colv
        NJ = DFF // 128
        halo = S.sb("halo", [128, NJ, 2], F32)
        S.op("pool", lambda e: e.memset(halo[:], 0.0), writes=[halo])
        rowv = S.sb("rowv", [128, D], F32)
        if last:
            self.load(rowv, rowv[:], self.rowv_in, self.rowv_in[l, :, :])
        fixed = S.arena
        fidx = len(S.stage_sb)
        for m in range(SEQ // TD):
            t0 = m * TD
            S.arena = fixed
            S.release_from(fidx)
            hT = S.sb("hTd", [128, 16, TD], BF16)
            self.norm_transpose(self.X, t0, TD, hT)
            wg = [S.sb("wgd%d" % i, [128, 16, 256], BF16) for i in range(2)]
            wu = [S.sb("wud%d" % i, [128, 16, 256], BF16) for i in range(2)]
            actT = S.sb("actT", [128, NJ, TD], BF16)
            gsb = [S.sb("gsb%d" % i, [128, TD + 2], F32) for i in range(2)]
            cc = [S.sb("cc%d" % i, [128, TD], F32) for i in range(2)]
            wdn = [S.sb("wdn%d" % i, [128, NJ, 256], BF16) for i in range(2)]
            xs = [S.sb("xsd%d" % i, [128, D], F32) for i in range(4)]
            junk = S.sb("junkf", [128, D], BF16)
            ssf = S.sb("ssf", [128, 4], F32)
            for blk in range(DFF // 256):
                a, b = wg[blk % 2], wu[blk % 2]
                self.load(a, a[:], self.WUP, self.WUP[:, blk * 256:(blk + 1) * 256].rearrange("(k p) c -> p k c", p=128))
                self.load(b, b[:], self.WUP,
                          self.WUP[:, DFF + blk * 256:DFF + (blk + 1) * 256].rearrange("(k p) c -> p k c", p=128))
                for jj in range(2):
                    j = blk * 2 + jj
                    pg = self.psf[jj]
                    pu = self.psf[2 + jj]
                    for k in range(16):
                        self.mm(pg, pg[:, :], a, a[:, k, jj * 128:(jj + 1) * 128], hT, hT[:, k, :], k == 0, k == 15)
                    for k in range(16):
                        self.mm(pu, pu[:, :], b, b[:, k, jj * 128:(jj + 1) * 128], hT, hT[:, k, :], k == 0, k == 15)
                    g = gsb[j % 2]
                    c = cc[j % 2]
                    self.copy(g, g[:, 0:2], halo, halo[:, j, :], eng="pool")
                    self.copy(g, g[:, 2:TD + 2], pg, pg[:, :], eng="act")
                    self.copy(halo, halo[:, j, :], g, g[:, TD:TD + 2], eng="pool")

                    def w(tap):
                        c_ = CV_FCW + tap * NJ + j
                        return cv[:, c_:c_ + 1]
                    self.ts("dve", c, c[:], g, g[:, 2:TD + 2], w(2), cv[:, CV_FCB + j:CV_FCB + j + 1], ALU.mult, ALU.add,
                            extra_reads=[cv])
                    self.stt("dve", c, c[:], g, g[:, 1:TD + 1], w(1), c, c[:], ALU.mult, ALU.add, extra_reads=[cv])
                    self.stt("dve", c, c[:], g, g[:, 0:TD], w(0), c, c[:], ALU.mult, ALU.add, extra_reads=[cv])
                    self.act(c, c[:], c, c[:], AF.Silu)
                    self.tt("dve", actT, actT[:, j, :], c, c[:], pu, pu[:, :], ALU.mult)
            for s_ in range(4):
                self.load(xs[s_], xs[s_][:], self.X, self.X[t0 + s_ * 128:t0 + (s_ + 1) * 128, :])
            for cb in range(8):
                w_ = wdn[cb % 2]
                self.load(w_, w_[:], self.WDN, self.WDN[:, cb * 256:(cb + 1) * 256].rearrange("(k p) c -> p k c", p=128))
                for s_ in range(4):
                    ps = self.psf[s_ % 4]
                    for k in range(NJ):
                        self.mm(ps, ps[:, 0:256], actT, actT[:, k, s_ * 128:(s_ + 1) * 128], w_, w_[:, k, :], k == 0, k == NJ - 1)
                    csl = slice(cb * 256, (cb + 1) * 256)
                    self.tt("dve", xs[s_], xs[s_][:, csl], xs[s_], xs[s_][:, csl], ps, ps[:, 0:256], ALU.add)
            for s_ in range(4):
                x = xs[s_]
                rows = slice(t0 + s_ * 128, t0 + (s_ + 1) * 128)
                if not last:
                    self.store(self.X, self.X[rows, :], x, x[:])
                else:
                    if self.xres is not None:
                        self.store(self.xres, self.xres[rows, :], x, x[:])
                    S.op("act", lambda e, x=x: e.activation(out=junk[:], in_=x[:], func=AF.Square, accum_out=ssf[:, 0:1]),
                         reads=[x], writes=[junk, ssf])
                    self.ts("dve", ssf, ssf[:, 1:2], ssf, ssf[:, 0:1], 1.0 / D, EPS, ALU.mult, ALU.add)
                    self.act(ssf, ssf[:, 2:3], ssf, ssf[:, 1:2], AF.Sqrt)
                    S.op("dve", lambda e: e.reciprocal(out=ssf[:, 2:3], in_=ssf[:, 2:3]), reads=[ssf], writes=[ssf])
                    self.stt("dve", x, x[:], x, x[:], ssf[:, 2:3], rowv, rowv[:], ALU.mult, ALU.mult, extra_reads=[ssf])
                    self.store(self.out, self.out[rows, :], x, x[:])
            S.barrier()
        self.end_stage()


def build(SEQ, DEPTH, debug=False, stages="all", per_layer=False):
    nc = bass.Bass("TRN2", target_bir_lowering=False)
    with contextlib.ExitStack() as es:
        P = Prog(nc, es, SEQ, DEPTH, debug, per_layer)
        NM = SEQ // P.MT
        for l in range(DEPTH):
            P.stage_W(l)
            for m in range(NM):
                P.stage_A(l, m)
            if stages == "A":
                break
            if "B1" in stages or stages == "all":
                P.stage_B_mla()
            if "B2" in stages or stages == "all":
                P.stage_B_gla()
            if "B3" in stages or stages == "all":
                P.stage_B_gdn()
            if stages != "all":
                break
            P.stage_C(l)
            P.stage_D(l, l == DEPTH - 1)
        P.S.barrier()
        P.S.emit()
        print("instructions", P.S.n_ins, "waits", P.S.n_wait, "sems", P.S.nsem, flush=True)
    return nc


def host_inputs(inp, SEQ, DEPTH):
    bf = ml_dtypes.bfloat16
    L = DEPTH
    f = lambda k: np.ascontiguousarray(np.asarray(inp[k], dtype=np.float32))

    def colmaj(v):
        return np.ascontiguousarray(v.reshape(-1, 128).T)
    colv = np.zeros((L, 128, NCOLV), np.float32)
    rowv = np.zeros((L, 128, 2048), np.float32)
    for l in range(L):
        colv[l, :, CV_AN:CV_AN + 16] = colmaj(f("attn_norm")[l])
        colv[l, :, CV_FN:CV_FN + 16] = colmaj(f("ffn_norm")[l])
        colv[l, :, CV_QN:CV_QN + 4] = colmaj(f("mla_q_norm")[l])
        colv[l, :, CV_KVN:CV_KVN + 4] = colmaj(f("mla_kv_norm")[l])
        colv[l, :, CV_GB:CV_GB + 4] = colmaj(f("gla_gate_bias")[l])
        for tap in range(4):
            colv[l, :, CV_DCW + tap * 16:CV_DCW + (tap + 1) * 16] = colmaj(f("gdn_conv_w")[l, tap])
        for tap in range(3):
            colv[l, :, CV_FCW + tap * 44:CV_FCW + (tap + 1) * 44] = colmaj(f("ffn_conv_w")[l, tap])
        colv[l, :, CV_FCB:CV_FCB + 44] = colmaj(f("ffn_conv_b")[l])
        colv[l, 0:8, CV_ALOG] = f("gdn_a_log")[l]
        colv[l, 0:8, CV_DTB] = f("gdn_dt_bias")[l]
        colv[l, :, CV_FIN:CV_FIN + 16] = colmaj(f("final_norm"))
        colv[l, :, CV_GLN:CV_GLN + 8] = colmaj(f("gla_out_norm")[l])
        colv[l, :, CV_GDNN:CV_GDNN + 8] = colmaj(f("gdn_out_norm")[l])
        rowv[l, :, :] = f("final_norm")[None, :]
    cst = np.zeros((128, 8), np.float32)
    inv = (10000.0 ** (-np.arange(0, 64, 2, dtype=np.float32) / 64)).astype(np.float32)
    cst[0:32, 0] = inv
    cst[32:64, 0] = inv
    cst[0:32, 1] = -1.0
    cst[32:64, 1] = 1.0
    cst[:, 2] = -math.pi
    masks = np.zeros((128, 4, 512), np.float32)
    kl = np.arange(128)[:, None]
    ql = np.arange(512)[None, :]
    for j in range(4):
        masks[:, j, :] = (ql >= 128 * j + kl)
    jj = np.arange(128)[:, None]
    ii = np.arange(128)[None, :]
    same = (jj // 64) == (ii // 64)
    cm = np.zeros((128, 6, 128), np.float32)
    cm[:, 0] = same & (ii >= jj)
    cm[:, 1] = same & (ii > jj)
    cm[:, 2] = same
    cm[:, 3] = same & (jj <= ii)
    cm[:, 4] = np.eye(128)
    m = {
        "x": f("x").reshape(SEQ, D),
        "pos": np.ascontiguousarray(np.asarray(inp["positions"], np.int32).reshape(1, SEQ)),
        "w_in": f("w_in"), "w_uq": f("mla_w_uq"), "w_ukv": f("mla_w_ukv"), "w_g2": f("gla_w_gate2"),
        "w_bm": f("w_branch_mla"), "w_bg": f("w_branch_gla"), "w_bd": f("w_branch_gdn"),
        "w_o": f("w_out"), "w_up": f("ffn_w_up"), "w_dn": f("ffn_w_down"),
        "colv": colv, "rowv": rowv, "cst": cst,
        "identb": np.eye(128, dtype=np.float32).astype(bf), "identf": np.eye(128, dtype=np.float32),
        "masks": masks.reshape(128, 2048).astype(bf), "cmask": cm.reshape(128, 768),
    }
    return m


_NC_CACHE = {}
LAUNCH_MODE = "layer"


def kernel(**inputs):
    SEQ = int(np.asarray(inputs["x"]).shape[1])
    DEPTH = int(np.asarray(inputs["w_in"]).shape[0])
    if LAUNCH_MODE == "fused":
        key = (SEQ, DEPTH, "fused")
        if key not in _NC_CACHE:
            _NC_CACHE[key] = build(SEQ, DEPTH)
        nc = _NC_CACHE[key]
        m = host_inputs(inputs, SEQ, DEPTH)
        res = run_bass_kernel_spmd(nc, [m], core_ids=[0])
        return np.asarray(res.results[0]["out"], dtype=np.float32).reshape(1, SEQ, D)
    key = (SEQ, "layer")
    if key not in _NC_CACHE:
        _NC_CACHE[key] = build(SEQ, 1, per_layer=True)
    nc = _NC_CACHE[key]
    wkeys = ("attn_norm", "w_in", "mla_q_norm", "mla_kv_norm", "mla_w_uq", "mla_w_ukv", "gla_w_gate2",
             "gla_gate_bias", "gla_out_norm", "gdn_conv_w", "gdn_a_log", "gdn_dt_bias", "gdn_out_norm",
             "w_branch_mla", "w_branch_gla", "w_branch_gdn", "w_out", "ffn_norm", "ffn_w_up", "ffn_conv_w",
             "ffn_conv_b", "ffn_w_down")
    x = np.asarray(inputs["x"], np.float32)
    out = None
    for l in range(DEPTH):
        sub = {k: np.asarray(inputs[k])[l:l + 1] for k in wkeys}
        sub["x"] = x
        sub["positions"] = inputs["positions"]
        sub["final_norm"] = inputs["final_norm"]
        m = host_inputs(sub, SEQ, 1)
        res = run_bass_kernel_spmd(nc, [m], core_ids=[0])
        r = res.results[0]
        x = np.asarray(r["xres"], dtype=np.float32).reshape(1, SEQ, D)
        out = r["out"]
    return np.asarray(out, dtype=np.float32).reshape(1, SEQ, D)
```

```python
import contextlib
import math
import numpy as np
import ml_dtypes
import concourse.bass as bass
import concourse.mybir as mybir
from concourse.bass_utils import run_bass_kernel_spmd

F32 = mybir.dt.float32
BF16 = mybir.dt.bfloat16
I32 = mybir.dt.int32
AF = mybir.ActivationFunctionType
ALU = mybir.AluOpType
AX = mybir.AxisListType

ENGS = ("pe", "dve", "act", "pool", "sp")
EPOCH = 60000
EPS = 1e-6

D = 2048
INW = 13408
DFF = 5632
NH = 8


def _sz(dt):
    return 2 if dt == BF16 else 4


class Buf:
    __slots__ = ("name", "w", "r", "dsem", "dcnt", "mw")

    def __init__(self, name):
        self.name = name
        self.w = None
        self.r = []
        self.dsem = None
        self.dcnt = 0
        self.mw = False


class T:
    __slots__ = ("t", "b")

    def __init__(self, t, name):
        self.t = t
        self.b = Buf(name)

    def __getitem__(self, k):
        return self.t[k]


class Sched:
    def __init__(self, nc, es):
        self.nc = nc
        self.es = es
        self.q = {e: [] for e in ENGS}
        self.cnt = {e: 0 for e in ENGS}
        self.semobj = {}
        self.esem = {}
        self.nsem = 0
        self.free_dsems = []
        self.dma_bufs = []
        for e in ENGS:
            self._new_eng_sem(e)
        self.seen = {e: {} for e in ENGS}
        self.n_ins = 0
        self.n_wait = 0
        self.arena = 16640
        self.uid = 0
        self.rr = 0
        self.stage_sb = []

    def _alloc_sem(self, name):
        s = self.es.enter_context(self.nc.semaphore(name))
        self.nsem += 1
        key = "s%d" % self.nsem
        self.semobj[key] = s
        return key

    def _new_eng_sem(self, e):
        self.esem[e] = self._alloc_sem("e_%s_%d" % (e, self.nsem))
        self.cnt[e] = 0

    def sb(self, name, shape, dt):
        self.uid += 1
        nbytes = int(np.prod(shape[1:])) * _sz(dt)
        off = (self.arena + 63) // 64 * 64
        assert off + nbytes <= 229376, ("SBUF overflow", name, off, nbytes)
        t = self.nc.alloc_sbuf_tensor_at("%s_%d" % (name, self.uid), list(shape), dt, offset=off)
        self.arena = off + nbytes
        r = T(t, name)
        self.stage_sb.append(r.b)
        return r

    def dram(self, name, shape, dt, kind="Internal"):
        t = self.nc.dram_tensor(name, list(shape), dt, kind=kind)
        return T(t.ap(), name)

    def _deps(self, eng, reads, writes):
        need = {}

        def add(ev):
            if ev is None:
                return
            k, v, src = ev
            if src == "pe" and eng == "pe":
                return
            if need.get(k, 0) < v:
                need[k] = v
        for b in reads:
            add(b.w)
        for b in writes:
            if not b.mw:
                add(b.w)
            for ev in b.r:
                add(ev)
        waits = []
        seen = self.seen[eng]
        for k, v in need.items():
            if seen.get(k, 0) >= v:
                continue
            seen[k] = v
            waits.append((k, v))
        return waits

    def _mark(self, ev, reads, writes):
        for b in reads:
            if b in writes:
                continue
            b.r = [x for x in b.r if x[0] != ev[0]] + [ev]
        for b in writes:
            b.w = ev
            b.r = []

    def op(self, eng, fn, reads=(), writes=()):
        reads = [x.b if isinstance(x, T) else x for x in reads]
        writes = [x.b if isinstance(x, T) else x for x in writes]
        waits = self._deps(eng, reads, writes)
        if self.cnt[eng] >= EPOCH:
            self._new_eng_sem(eng)
        self.cnt[eng] += 1
        ev = (self.esem[eng], self.cnt[eng], eng)
        self._mark(ev, reads, writes)
        self.q[eng].append((waits, fn, (self.esem[eng], 1)))
        self.n_ins += 1
        self.n_wait += len(waits)
        return ev

    def dma(self, fn, reads=(), writes=(), eng=None):
        reads = [x.b if isinstance(x, T) else x for x in reads]
        writes = [x.b if isinstance(x, T) else x for x in writes]
        if eng is None:
            eng = "sp"
        waits = self._deps(eng, reads, writes)
        sb = writes[0]
        if sb.dsem is None:
            if self.free_dsems:
                sb.dsem, sb.dcnt = self.free_dsems.pop()
            else:
                sb.dsem = self._alloc_sem("d_%s_%d" % (sb.name, self.nsem))
            self.dma_bufs.append(sb)
        sb.dcnt += 16
        ev = (sb.dsem, sb.dcnt, "dma")
        self._mark(ev, reads, writes)
        self.q[eng].append((waits, fn, (sb.dsem, 16)))
        self.n_ins += 1
        self.n_wait += len(waits)
        return ev

    def release_from(self, idx):
        for b in self.stage_sb[idx:]:
            if b.dsem is not None:
                self.free_dsems.append((b.dsem, b.dcnt))
                self.dma_bufs.remove(b)
                b.dsem = None
        del self.stage_sb[idx:]

    def barrier(self):
        waits = []
        seen = self.seen["sp"]
        for e in ENGS:
            if e == "sp":
                continue
            k, v = self.esem[e], self.cnt[e]
            if v > 0 and seen.get(k, 0) < v:
                seen[k] = v
                waits.append((k, v))
        for b in self.dma_bufs:
            if b.dcnt > 0 and seen.get(b.dsem, 0) < b.dcnt:
                seen[b.dsem] = b.dcnt
                waits.append((b.dsem, b.dcnt))
        if self.cnt["sp"] >= EPOCH:
            self._new_eng_sem("sp")
        self.cnt["sp"] += 1
        k, v = self.esem["sp"], self.cnt["sp"]
        self.q["sp"].append((waits, lambda e: e.nop(), (k, 1)))
        for e in ENGS:
            if e == "sp":
                continue
            self.seen[e][k] = v
            self.q[e].append(([(k, v)], None, None))

    def emit(self):
        nc = self.nc
        so = self.semobj
        with nc.Block() as block:
            def run(e):
                def body(engine):
                    for waits, fn, inc in self.q[e]:
                        if fn is None:
                            for k, v in waits:
                                engine.wait_ge(so[k], v)
                            continue
                        for k, v in waits[:-1]:
                            engine.wait_ge(so[k], v)
                        ins = fn(engine)
                        if waits:
                            k, v = waits[-1]
                            ins._wait_ge(so[k], v)
                        ins.then_inc(so[inc[0]], inc[1])
                return body
            block.tensor(run("pe"))
            block.vector(run("dve"))
            block.scalar(run("act"))
            block.gpsimd(run("pool"))
            block.sync(run("sp"))


NCOLV = 320
CV_GLN, CV_GDNN = 304, 312
CV_AN, CV_FN, CV_QN, CV_KVN, CV_GB, CV_DCW, CV_FCW, CV_FCB, CV_ALOG, CV_DTB, CV_FIN = (
    0, 16, 32, 36, 40, 44, 108, 240, 284, 285, 286)


class Prog:
    def __init__(self, nc, es, SEQ, DEPTH, debug=False, per_layer=False):
        self.nc = nc
        self.S = Sched(nc, es)
        self.SEQ = SEQ
        self.DEPTH = DEPTH
        self.debug = debug
        self.MT = min(1024, SEQ)
        self.ev = 0
        self.pi = 0
        S = self.S
        kind = "ExternalOutput" if debug else "Internal"
        self.dbg_kind = kind

        def ext(name, shape, dt=F32):
            t = nc.dram_tensor(name, list(shape), dt, kind="ExternalInput")
            return T(t.ap(), name)
        L = DEPTH
        self.x = ext("x", [SEQ, D])
        self.pos = ext("pos", [1, SEQ], I32)
        self.w_in = ext("w_in", [L, D, INW])
        self.w_uq = ext("w_uq", [L, 512, 1536])
        self.w_ukv = ext("w_ukv", [L, 512, 2048])
        self.w_g2 = ext("w_g2", [L, 16, 512])
        self.w_bm = ext("w_bm", [L, 1024, D])
        self.w_bg = ext("w_bg", [L, 1024, D])
        self.w_bd = ext("w_bd", [L, 1024, D])
        self.w_o = ext("w_o", [L, D, D])
        self.w_up = ext("w_up", [L, D, 2 * DFF])
        self.w_dn = ext("w_dn", [L, DFF, D])
        self.colv_in = ext("colv", [L, 128, NCOLV])
        self.rowv_in = ext("rowv", [L, 128, 2048])
        self.cst_in = ext("cst", [128, 8])
        self.identb_in = ext("identb", [128, 128], BF16)
        self.identf_in = ext("identf", [128, 128])
        self.masks_in = ext("masks", [128, 4 * 512], BF16)
        self.cmask_in = ext("cmask", [128, 6 * 128])
        out_t = nc.dram_tensor("out", [SEQ, D], F32, kind="ExternalOutput")
        self.out = T(out_t.ap(), "out")
        self.xres = None
        if per_layer:
            xr = nc.dram_tensor("xres", [SEQ, D], F32, kind="ExternalOutput")
            self.xres = T(xr.ap(), "xres")
            self.xres.b.mw = True

        dr = S.dram
        self.X = dr("Xs", [SEQ, D], F32)
        self.WIN = dr("WIN", [D, INW + 64], BF16)
        self.WUQ = dr("WUQ", [512, 1536 + 512], BF16)
        self.WUKV = dr("WUKV", [512, 2048], BF16)
        self.WG2 = dr("WG2", [16, 512], BF16)
        self.WBR = [dr("WBR%d" % i, [1024, D], BF16) for i in range(3)]
        self.WO = dr("WO", [D, D], BF16)
        self.WUP = dr("WUP", [D, 2 * DFF], BF16)
        self.WDN = dr("WDN", [DFF, D], BF16)
        self.QN = dr("QN", [NH * 128, SEQ], BF16, kind)
        self.QR = dr("QR", [NH * 64, SEQ], BF16, kind)
        self.KN = dr("KN", [NH * 128, SEQ], BF16, kind)
        self.KR = dr("KR", [64, SEQ], BF16, kind)
        self.V = dr("Vv", [SEQ, 1024], BF16, kind)
        self.GQ = dr("GQ", [512, SEQ], BF16, kind)
        self.GK = dr("GK", [512, SEQ], BF16, kind)
        self.GG = dr("GG", [512, SEQ], F32, kind)
        self.GV = dr("GV", [SEQ, 1024], BF16, kind)
        self.GR = dr("GR", [SEQ, 1024], BF16, kind)
        self.DQKV = dr("DQKV", [2048, SEQ], BF16, kind)
        self.DB = dr("DB", [8, SEQ], F32, kind)
        self.DA = dr("DA", [8, SEQ], F32, kind)
        self.DZ = dr("DZ", [SEQ, 1024], BF16, kind)
        self.MG = dr("MG", [3 * D, SEQ], BF16, kind)
        self.Y = [dr("Y%d" % i, [1024, SEQ], BF16, kind) for i in range(3)]
        for t in ([self.X, self.WIN, self.WUQ, self.WUKV, self.WG2, self.WO, self.WUP, self.WDN,
                   self.QN, self.QR, self.KN, self.KR, self.V, self.GQ, self.GK, self.GG, self.GV,
                   self.GR, self.DQKV, self.DB, self.DA, self.DZ, self.MG, self.out]
                  + self.WBR + self.Y):
            t.b.mw = True

        self.psf = [T(nc.alloc_psum_tensor("psf%d" % i, [128, 512], F32), "psf%d" % i) for i in range(6)]
        self.psb = [T(nc.alloc_psum_tensor("psb%d" % i, [128, 1024], BF16), "psb%d" % i) for i in range(2)]

        self.identb = S.sb("identb", [128, 128], BF16)
        self.identf = S.sb("identf", [128, 128], F32)
        self.onesb = S.sb("onesb", [128, 128], BF16)
        self.cst = S.sb("cst", [128, 8], F32)
        self.colv = S.sb("colv", [128, NCOLV], F32)
        self.ncolv = S.sb("ncolv", [128, 8], F32)
        ld = self.load
        ld(self.identb, self.identb[:], self.identb_in, self.identb_in[:, :])
        ld(self.identf, self.identf[:], self.identf_in, self.identf_in[:, :])
        ld(self.cst, self.cst[:], self.cst_in, self.cst_in[:, :])
        S.op("pool", lambda e: e.memset(self.onesb[:], 1.0), writes=[self.onesb])
        self.base_arena = S.arena
        self.persist = list(S.stage_sb)
        S.stage_sb = []

    def load(self, dT, dap, sT, sap, eng="sp"):
        self.S.dma(lambda e: e.dma_start(out=dap, in_=sap), reads=[sT], writes=[dT], eng=eng)

    def store(self, dT, dap, sT, sap, eng="pool"):
        self.S.dma(lambda e: e.dma_start(out=dap, in_=sap), reads=[sT], writes=[dT], eng=eng)

    def nps(self):
        self.pi += 1
        return self.psf[self.pi % 4]

    def copy_eng(self):
        self.ev += 1
        return "act" if self.ev % 2 else "dve"

    def copy(self, oT, oap, iT, iap, eng=None):
        eng = eng or self.copy_eng()
        if eng == "act":
            self.S.op("act", lambda e: e.activation(out=oap, in_=iap, func=AF.Copy), reads=[iT], writes=[oT])
        else:
            self.S.op(eng, lambda e: e.tensor_copy(out=oap, in_=iap), reads=[iT], writes=[oT])

    def act(self, oT, oap, iT, iap, func, scale=1.0, bias=0.0, extra_reads=()):
        self.S.op("act", lambda e: e.activation(out=oap, in_=iap, func=func, bias=bias, scale=scale),
                  reads=[iT] + list(extra_reads), writes=[oT])

    def tt(self, eng, oT, oap, aT, aap, bT, bap, op):
        self.S.op(eng, lambda e: e.tensor_tensor(out=oap, in0=aap, in1=bap, op=op), reads=[aT, bT], writes=[oT])

    def ts(self, eng, oT, oap, aT, aap, s1, s2, op0, op1=None, extra_reads=()):
        if op1 is None:
            self.S.op(eng, lambda e: e.tensor_scalar(out=oap, in0=aap, scalar1=s1, scalar2=None, op0=op0),
                      reads=[aT] + list(extra_reads), writes=[oT])
        else:
            self.S.op(eng, lambda e: e.tensor_scalar(out=oap, in0=aap, scalar1=s1, scalar2=s2, op0=op0, op1=op1),
                      reads=[aT] + list(extra_reads), writes=[oT])

    def stt(self, eng, oT, oap, aT, aap, scalar, bT, bap, op0, op1, extra_reads=()):
        self.S.op(eng, lambda e: e.scalar_tensor_tensor(out=oap, in0=aap, scalar=scalar, in1=bap, op0=op0, op1=op1),
                  reads=[aT, bT] + list(extra_reads), writes=[oT])

    def mm(self, psT, pap, lT, lap, rT, rap, start, stop):
        self.S.op("pe", lambda e: e.matmul(pap, lap, rap, start=start, stop=stop), reads=[lT, rT], writes=[psT])

    def tr(self, psT, pap, iT, iap, ident):
        self.S.op("pe", lambda e: e.transpose(pap, iap, ident), reads=[iT], writes=[psT])

    def end_stage(self):
        S = self.S
        S.barrier()
        S.arena = self.base_arena
        keep = set(id(b) for b in self.persist)
        for b in S.stage_sb:
            if id(b) in keep:
                continue
            if b.dsem is not None:
                S.free_dsems.append((b.dsem, b.dcnt))
                S.dma_bufs.remove(b)
                b.dsem = None
        S.stage_sb = []

    def stage_W(self, l):
        S = self.S
        cv = self.colv
        self.load(cv, cv[:], self.colv_in, self.colv_in[l, :, :])
        self.ts("dve", self.ncolv, self.ncolv[:, 0:4], cv, cv[:, CV_GB:CV_GB + 4], -1.0, None, ALU.mult)
        self.act(self.ncolv, self.ncolv[0:8, 4:5], cv, cv[0:8, CV_ALOG:CV_ALOG + 1], AF.Exp)
        self.ts("dve", self.ncolv, self.ncolv[0:8, 4:5], self.ncolv, self.ncolv[0:8, 4:5], -1.0, None, ALU.mult)
        stf = [S.sb("wstf%d" % i, [128, 2048], F32) for i in range(3)]
        stb = [S.sb("wstb%d" % i, [128, 2048], BF16) for i in range(3)]
        cnt = [0]

        def cast(src_ap, R, C, dstT, dc0=0, scol=None):
            for r0 in range(0, R, 128):
                rr = min(128, R - r0)
                for c0 in range(0, C, 2048):
                    cc = min(2048, C - c0)
                    i = cnt[0] % 3
                    cnt[0] += 1
                    f, b = stf[i], stb[i]
                    self.load(f, f[:rr, :cc], self.w_in, src_ap[r0:r0 + rr, c0:c0 + cc])
                    eng = "dve" if cnt[0] % 2 else "pool"
                    if scol is not None:
                        sc = cv[:rr, scol + r0 // 128: scol + r0 // 128 + 1]
                        self.ts(eng, b, b[:rr, :cc], f, f[:rr, :cc], sc, None, ALU.mult, extra_reads=[cv])
                    else:
                        self.copy(b, b[:rr, :cc], f, f[:rr, :cc], eng=eng)
                    self.store(dstT, dstT[r0:r0 + rr, dc0 + c0:dc0 + c0 + cc], b, b[:rr, :cc])
        cast(self.w_in[l], D, INW, self.WIN, 0, CV_AN)
        cast(self.w_in[l][:, 1056:1088], D, 32, self.WIN, INW, CV_AN)
        cast(self.w_in[l][:, 1024:1056], D, 32, self.WIN, INW + 32, CV_AN)
        cast(self.w_uq[l], 512, 1536, self.WUQ, 0, CV_QN)
        for h in range(NH):
            c = h * 192 + 128
            cast(self.w_uq[l][:, c + 32:c + 64], 512, 32, self.WUQ, 1536 + h * 64, CV_QN)
            cast(self.w_uq[l][:, c:c + 32], 512, 32, self.WUQ, 1536 + h * 64 + 32, CV_QN)
        cast(self.w_ukv[l], 512, 2048, self.WUKV, 0, CV_KVN)
        cast(self.w_g2[l], 16, 512, self.WG2)
        cast(self.w_bm[l], 1024, D, self.WBR[0])
        cast(self.w_bg[l], 1024, D, self.WBR[1], 0, CV_GLN)
        cast(self.w_bd[l], 1024, D, self.WBR[2], 0, CV_GDNN)
        cast(self.w_o[l], D, D, self.WO)
        cast(self.w_up[l], D, 2 * DFF, self.WUP, 0, CV_FN)
        cast(self.w_dn[l], DFF, D, self.WDN)
        self.end_stage()

    def norm_transpose(self, srcT, t0, MT, hT, xkeep=None):
        S = self.S
        mark = S.arena
        sidx = len(S.stage_sb)
        xin = [S.sb("xin%d" % i, [128, D], F32) for i in range(2)]
        xn = [S.sb("xn%d" % i, [128, D], BF16) for i in range(2)]
        junk = S.sb("junk", [128, D], BF16)
        ss = S.sb("ss", [128, 4], F32)
        for s in range(MT // 128):
            xt, xb = xin[s % 2], xn[s % 2]
            self.load(xt, xt[:], srcT, srcT[t0 + s * 128:t0 + (s + 1) * 128, :])
            S.op("act", lambda e, xt=xt: e.activation(out=junk[:], in_=xt[:], func=AF.Square, accum_out=ss[:, 0:1]),
                 reads=[xt], writes=[junk, ss])
            self.ts("dve", ss, ss[:, 1:2], ss, ss[:, 0:1], 1.0 / D, EPS, ALU.mult, ALU.add)
            self.act(ss, ss[:, 2:3], ss, ss[:, 1:2], AF.Sqrt)
            S.op("dve", lambda e: e.reciprocal(out=ss[:, 2:3], in_=ss[:, 2:3]), reads=[ss], writes=[ss])
            self.ts("dve", xb, xb[:], xt, xt[:], ss[:, 2:3], None, ALU.mult, extra_reads=[ss])
            for half in range(2):
                pb = self.psb[half]
                for k8 in range(8):
                    kc = half * 8 + k8
                    self.tr(pb, pb[:, k8 * 128:(k8 + 1) * 128], xb, xb[:, kc * 128:(kc + 1) * 128], self.identb[:])
                self.copy(hT, hT[:, half * 8:(half + 1) * 8, s * 128:(s + 1) * 128],
                          pb, pb[:, :].rearrange("p (k t) -> p k t", k=8))
        S.barrier()
        S.arena = mark
        S.release_from(sidx)

    def stage_A(self, l, m):
        S = self.S
        MT = self.MT
        t0 = m * MT
        NT = MT // 512
        NSUB = MT // 128
        cv = self.colv
        src = self.x if l == 0 else self.X
        Ct = S.sb("Ct", [64, MT], F32)
        Sg = S.sb("Sg", [64, MT], F32)
        mark = S.arena
        sidx = len(S.stage_sb)
        posi = S.sb("posi", [64, MT], I32)
        ang = S.sb("ang", [64, MT], F32)
        tmpa = S.sb("tmpa", [64, MT], F32)
        self.load(posi, posi[:], self.pos, self.pos[0:1, t0:t0 + MT].partition_broadcast(64))
        self.copy(ang, ang[:], posi, posi[:], eng="dve")
        self.ts("dve", ang, ang[:], ang, ang[:], self.cst[0:64, 0:1], None, ALU.mult, extra_reads=[self.cst])
        HI = 6.28125
        LO = 2.0 * math.pi - 6.28125

        def sin_of(dst, shift):
            src = ang
            if shift != 0.0:
                self.ts("dve", tmpa, tmpa[:], ang, ang[:], shift, None, ALU.add)
                src = tmpa
            self.ts("dve", posi, posi[:], src, src[:], 1.0 / (2.0 * math.pi), None, ALU.mult)
            self.copy(dst, dst[:], posi, posi[:], eng="dve")
            self.stt("dve", tmpa, tmpa[:], dst, dst[:], -HI, src, src[:], ALU.mult, ALU.add)
            self.stt("dve", tmpa, tmpa[:], dst, dst[:], -LO, tmpa, tmpa[:], ALU.mult, ALU.add)
            self.act(dst, dst[:], tmpa, tmpa[:], AF.Sin)
        sin_of(Sg, 0.0)
        self.ts("dve", Sg, Sg[:], Sg, Sg[:], self.cst[0:64, 1:2], None, ALU.mult, extra_reads=[self.cst])
        sin_of(Ct, 0.5 * math.pi)
        S.barrier()
        S.arena = mark
        S.release_from(sidx)
        hT = S.sb("hT", [128, 16, MT], BF16)
        self.norm_transpose(src, t0, MT, hT)
        r1 = S.sb("r1", [64, 512], F32)
        r2 = S.sb("r2", [64, 512], F32)
        wb = [S.sb("wb%d" % i, [128, 16, 512], BF16) for i in range(2)]
        stg = [S.sb("stg%d" % i, [128, MT], BF16) for i in range(3)]
        stgf = [S.sb("stgf%d" % i, [128, MT], F32) for i in range(2)]
        stt = [S.sb("stt%d" % i, [128, 512], BF16) for i in range(3)]
        cq = S.sb("cq", [128, 4, MT], BF16)
        ckv = S.sb("ckv", [128, 4, MT], BF16)
        kr = S.sb("kr", [64, 2, MT], F32)
        glr = S.sb("glr", [16, MT], BF16)
        ci = [0, 0, 0, 0]

        def wload(col0, ncols, nk=16, WT=None):
            WT = WT or self.WIN
            w = wb[ci[0] % 2]
            ci[0] += 1
            self.load(w, w[:, :nk, :ncols],
                      WT, WT[0:nk * 128, col0:col0 + ncols].rearrange("(k p) c -> p k c", p=128))
            return w

        def fm_group(col0, ncols, evac, after=None):
            w = wload(col0, ncols)
            for j in range((ncols + 127) // 128):
                M = min(128, ncols - j * 128)
                for ts_ in range(NT):
                    ps = self.nps()
                    for k in range(16):
                        self.mm(ps, ps[:M, :], w, w[:, k, j * 128:j * 128 + M], hT, hT[:, k, ts_ * 512:(ts_ + 1) * 512],
                                k == 0, k == 15)
                    evac(j, M, ts_, ps)
                if after is not None:
                    after(j, M)

        def to_dram(dstT, row0, func=None, scale=1.0, f32=False):
            cur = {}

            def evac(j, M, ts_, ps):
                if ts_ == 0:
                    if f32:
                        cur["s"] = stgf[ci[2] % 2]
                        ci[2] += 1
                    else:
                        cur["s"] = stg[ci[1] % 3]
                        ci[1] += 1
                s = cur["s"]
                o = s[:M, ts_ * 512:(ts_ + 1) * 512]
                if func is None and scale == 1.0:
                    self.copy(s, o, ps, ps[:M, :])
                elif func is None:
                    self.ts("dve", s, o, ps, ps[:M, :], scale, None, ALU.mult)
                else:
                    self.act(s, o, ps, ps[:M, :], func, scale=scale)

            def after(j, M):
                s = cur["s"]
                self.store(dstT, dstT[row0 + j * 128:row0 + j * 128 + M, t0:t0 + MT], s, s[:M, :])
            return evac, after

        def to_sb(dstT, dst3):
            def evac(j, M, ts_, ps):
                self.copy(dstT, dst3(j, M, ts_), ps, ps[:M, :])
            return evac

        fm_group(0, 512, to_sb(cq, lambda j, M, ts_: cq[:, j, ts_ * 512:(ts_ + 1) * 512]))
        fm_group(512, 512, to_sb(ckv, lambda j, M, ts_: ckv[:, j, ts_ * 512:(ts_ + 1) * 512]))
        fm_group(1024, 64, to_sb(kr, lambda j, M, ts_: kr[:, 0, ts_ * 512:(ts_ + 1) * 512]))
        fm_group(INW, 64, to_sb(kr, lambda j, M, ts_: kr[:, 1, ts_ * 512:(ts_ + 1) * 512]))
        ev, af = to_dram(self.GQ, 0, scale=128.0 ** -0.5)
        fm_group(1088, 512, ev, af)
        ev, af = to_dram(self.GK, 0)
        fm_group(1600, 512, ev, af)
        fm_group(3136, 16, to_sb(glr, lambda j, M, ts_: glr[:, ts_ * 512:(ts_ + 1) * 512]))
        for i in range(4):
            ev, af = to_dram(self.DQKV, i * 512)
            fm_group(4176 + i * 512, 512, ev, af)
        ev, af = to_dram(self.DB, 0, func=AF.Sigmoid, f32=True)
        fm_group(6224, 8, ev, af)
        cur = {}

        def ev_a(j, M, ts_, ps):
            if ts_ == 0:
                cur["s"] = stgf[ci[2] % 2]
                ci[2] += 1
            s = cur["s"]
            o = s[:8, ts_ * 512:(ts_ + 1) * 512]
            self.act(s, o, ps, ps[:8, :], AF.Exp, bias=cv[0:8, CV_DTB:CV_DTB + 1], extra_reads=[cv])
            self.act(s, o, s, o, AF.Ln, bias=1.0)
            self.ts("dve", s, o, s, o, self.ncolv[0:8, 4:5], None, ALU.mult, extra_reads=[self.ncolv])

        def af_a(j, M):
            s = cur["s"]
            self.store(self.DA, self.DA[0:8, t0:t0 + MT], s, s[:8, :])
        fm_group(6232, 8, ev_a, af_a)
        for i in range(12):
            ev, af = to_dram(self.MG, i * 512, func=AF.Sigmoid)
            fm_group(7264 + i * 512, 512, ev, af)

        def tm_group(col0, dstT, dcol0, func):
            w = wload(col0, 512)
            for s in range(NSUB):
                ps = self.nps()
                for k in range(16):
                    self.mm(ps, ps[:, :], hT, hT[:, k, s * 128:(s + 1) * 128], w, w[:, k, :], k == 0, k == 15)
                st = stt[ci[3] % 3]
                ci[3] += 1
                if func is None:
                    self.copy(st, st[:], ps, ps[:, :])
                else:
                    self.act(st, st[:], ps, ps[:, :], func)
                self.store(dstT, dstT[t0 + s * 128:t0 + (s + 1) * 128, dcol0:dcol0 + 512], st, st[:])
        tm_group(2112, self.GV, 0, None)
        tm_group(2624, self.GV, 512, None)
        tm_group(3152, self.GR, 0, AF.Silu)
        tm_group(3664, self.GR, 512, AF.Silu)
        tm_group(6240, self.DZ, 0, AF.Silu)
        tm_group(6752, self.DZ, 512, AF.Silu)

        wg = S.sb("wg", [16, 512], BF16)
        self.load(wg, wg[:], self.WG2, self.WG2[:, :])
        for j in range(4):
            s = stgf[ci[2] % 2]
            ci[2] += 1
            for ts_ in range(NT):
                ps = self.nps()
                self.mm(ps, ps[:, :], wg, wg[:, j * 128:(j + 1) * 128], glr, glr[:, ts_ * 512:(ts_ + 1) * 512], True, True)
                o = s[:, ts_ * 512:(ts_ + 1) * 512]
                self.act(s, o, ps, ps[:, :], AF.Exp, scale=-1.0, bias=self.ncolv[:, j:j + 1], extra_reads=[self.ncolv])
                self.act(s, o, s, o, AF.Ln, bias=1.0)
                self.ts("dve", s, o, s, o, -1.0 / 16.0, None, ALU.mult)
            self.store(self.GG, self.GG[j * 128:(j + 1) * 128, t0:t0 + MT], s, s[:, :])

        def rope(dstT, dap, aT, a_ap, bT, b_ap, sl):
            self.tt("dve", r1, r1[:], aT, a_ap, Ct, Ct[:, sl], ALU.mult)
            self.tt("dve", r2, r2[:], bT, b_ap, Sg, Sg[:, sl], ALU.mult)
            self.tt("dve", dstT, dap, r1, r1[:], r2, r2[:], ALU.add)
        s = stg[ci[1] % 3]
        ci[1] += 1
        for ts_ in range(NT):
            sl = slice(ts_ * 512, (ts_ + 1) * 512)
            rope(s, s[0:64, sl], kr, kr[:, 0, sl], kr, kr[:, 1, sl], sl)
        self.store(self.KR, self.KR[0:64, t0:t0 + MT], s, s[0:64, :])

        sq = S.sb("sq", [128, 4, 512], BF16)
        rstd = S.sb("rstd", [128, 512], F32)
        for lat in (cq, ckv):
            for ts_ in range(NT):
                sl = slice(ts_ * 512, (ts_ + 1) * 512)
                self.tt("pool", sq, sq[:], lat, lat[:, :, sl], lat, lat[:, :, sl], ALU.mult)
                ps = self.nps()
                for k in range(4):
                    self.mm(ps, ps[:, :], self.onesb, self.onesb[:], sq, sq[:, k, :], k == 0, k == 3)
                self.ts("dve", rstd, rstd[:], ps, ps[:, :], 1.0 / 512, EPS, ALU.mult, ALU.add)
                self.act(rstd, rstd[:], rstd, rstd[:], AF.Sqrt)
                S.op("dve", lambda e: e.reciprocal(out=rstd[:], in_=rstd[:]), reads=[rstd], writes=[rstd])
                for k in range(4):
                    self.tt("dve", lat, lat[:, k, sl], lat, lat[:, k, sl], rstd, rstd[:], ALU.mult)

        wq = S.sb("wq", [128, 4, 2048], BF16)
        self.load(wq, wq[:], self.WUQ, self.WUQ[:, :].rearrange("(k p) c -> p k c", p=128))
        for h in range(NH):
            s = stg[ci[1] % 3]
            ci[1] += 1
            s2 = stg[ci[1] % 3]
            ci[1] += 1
            for ts_ in range(NT):
                sl = slice(ts_ * 512, (ts_ + 1) * 512)
                ps = self.nps()
                for k in range(4):
                    self.mm(ps, ps[:, :], wq, wq[:, k, h * 192:h * 192 + 128], cq, cq[:, k, sl], k == 0, k == 3)
                self.copy(s, s[:, sl], ps, ps[:, :])
                pa = self.nps()
                for k in range(4):
                    self.mm(pa, pa[0:64, :], wq, wq[:, k, h * 192 + 128:h * 192 + 192], cq, cq[:, k, sl], k == 0, k == 3)
                pb = self.nps()
                for k in range(4):
                    self.mm(pb, pb[0:64, :], wq, wq[:, k, 1536 + h * 64:1536 + (h + 1) * 64], cq, cq[:, k, sl], k == 0, k == 3)
                rope(s2, s2[0:64, sl], pa, pa[0:64, :], pb, pb[0:64, :], sl)
            self.store(self.QN, self.QN[h * 128:(h + 1) * 128, t0:t0 + MT], s, s[:, :])
            self.store(self.QR, self.QR[h * 64:(h + 1) * 64, t0:t0 + MT], s2, s2[0:64, :])

        wkv = S.sb("wkv", [128, 4, 2048], BF16)
        self.load(wkv, wkv[:], self.WUKV, self.WUKV[:, :].rearrange("(k p) c -> p k c", p=128))
        for h in range(NH):
            s = stg[ci[1] % 3]
            ci[1] += 1
            for ts_ in range(NT):
                sl = slice(ts_ * 512, (ts_ + 1) * 512)
                ps = self.nps()
                for k in range(4):
                    self.mm(ps, ps[:, :], wkv, wkv[:, k, h * 256:h * 256 + 128], ckv, ckv[:, k, sl], k == 0, k == 3)
                self.copy(s, s[:, sl], ps, ps[:, :])
            self.store(self.KN, self.KN[h * 128:(h + 1) * 128, t0:t0 + MT], s, s[:, :])
        for s_ in range(NSUB):
            for hh in range(2):
                ps = self.nps()
                for k in range(4):
                    rhs = wkv[:, k, :].rearrange("p (h c) -> p h c", c=256)[:, hh * 4:(hh + 1) * 4, 128:256]
                    self.mm(ps, ps[:, :].rearrange("p (h c) -> p h c", c=128), ckv, ckv[:, k, s_ * 128:(s_ + 1) * 128],
                            wkv, rhs, k == 0, k == 3)
                st = stt[ci[3] % 3]
                ci[3] += 1
                self.copy(st, st[:], ps, ps[:, :])
                self.store(self.V, self.V[t0 + s_ * 128:t0 + (s_ + 1) * 128, hh * 512:(hh + 1) * 512], st, st[:])
        self.end_stage()

    def stage_B_mla(self):
        S = self.S
        SEQ = self.SEQ
        NKB = SEQ // 128
        NQT = SEQ // 512
        scale = 192.0 ** -0.5
        masks = S.sb("masks", [128, 4, 512], BF16)
        self.load(masks, masks[:], self.masks_in, self.masks_in[:, :].rearrange("p (j q) -> p j q", j=4))
        krT = S.sb("krT", [64, SEQ], BF16)
        self.load(krT, krT[:], self.KR, self.KR[:, :])
        knT = S.sb("knT", [128, SEQ], BF16)
        vT = S.sb("vT", [128, NKB, 128], BF16)
        qn = [S.sb("qn%d" % i, [128, 512], BF16) for i in range(2)]
        qr = [S.sb("qr%d" % i, [64, 512], BF16) for i in range(2)]
        pT = [S.sb("pT%d" % i, [128, 512], BF16) for i in range(3)]
        rec = S.sb("rec", [128, 512], F32)
        yst = [S.sb("yst%d" % i, [128, 512], BF16) for i in range(2)]
        po, pd = self.psf[4], self.psf[5]
        it = 0
        pi = 0
        for h in range(NH):
            self.load(knT, knT[:], self.KN, self.KN[h * 128:(h + 1) * 128, :])
            for kb0 in range(0, NKB, 16):
                nb = min(16, NKB - kb0)
                self.load(vT, vT[:, kb0:kb0 + nb, :], self.V,
                          self.V[kb0 * 128:(kb0 + nb) * 128, h * 128:(h + 1) * 128].rearrange("(kb p) c -> p kb c", p=128))
            for qt in range(NQT):
                a, b = qn[it % 2], qr[it % 2]
                self.load(a, a[:], self.QN, self.QN[h * 128:(h + 1) * 128, qt * 512:(qt + 1) * 512])
                self.load(b, b[:], self.QR, self.QR[h * 64:(h + 1) * 64, qt * 512:(qt + 1) * 512])
                nkb = 4 * qt + 4
                for kb in range(nkb):
                    ps = self.nps()
                    ksl = slice(kb * 128, (kb + 1) * 128)
                    self.mm(ps, ps[:, :], knT, knT[:, ksl], a, a[:], True, False)
                    self.mm(ps, ps[:, :], krT, krT[:, ksl], b, b[:], False, True)
                    p = pT[pi % 3]
                    pi += 1
                    self.act(p, p[:], ps, ps[:, :], AF.Exp, scale=scale)
                    if kb >= 4 * qt:
                        j = kb - 4 * qt
                        self.tt("pool", p, p[:], p, p[:], masks, masks[:, j, :], ALU.mult)
                    self.mm(po, po[:, :], vT, vT[:, kb, :], p, p[:], kb == 0, kb == nkb - 1)
                    self.mm(pd, pd[:, :], self.onesb, self.onesb[:], p, p[:], kb == 0, kb == nkb - 1)
                S.op("dve", lambda e: e.reciprocal(out=rec[:], in_=pd[:, :]), reads=[pd], writes=[rec])
                y = yst[it % 2]
                self.tt("dve", y, y[:], po, po[:, :], rec, rec[:], ALU.mult)
                self.store(self.Y[0], self.Y[0][h * 128:(h + 1) * 128, qt * 512:(qt + 1) * 512], y, y[:])
                it += 1
        self.end_stage()

    def mmt(self, psT, pap, lT, lap, rT, rap, start, stop, tp):
        self.S.op("pe", lambda e: e.matmul(pap, lap, rap, start=start, stop=stop, tile_position=tp),
                  reads=[lT, rT], writes=[psT])

    def stage_B_gla(self):
        S = self.S
        SEQ = self.SEQ
        TM = min(1024, SEQ)
        NTL = TM // 128
        NCH = TM // 64
        cm = S.sb("cm", [128, 6, 128], F32)
        self.load(cm, cm[:], self.cmask_in, self.cmask_in[:, :].rearrange("p (a b) -> p a b", a=6))
        St = S.sb("St", [128, 256], F32)
        Sb = S.sb("Sb", [128, 256], BF16)
        qT = S.sb("qT", [128, TM], BF16)
        kT = S.sb("kT", [128, TM], BF16)
        gk = S.sb("gk", [128, TM], F32)
        bb = S.sb("bb", [128, TM], F32)
        eb = S.sb("eb", [128, TM], F32)
        enb = S.sb("enb", [128, TM], F32)
        qd = S.sb("qd", [128, TM], BF16)
        kdn = S.sb("kdn", [128, TM], BF16)
        kdl = S.sb("kdl", [128, TM], BF16)
        kdl_tm = S.sb("kdl_tm", [128, NTL, 128], BF16)
        v_tm = S.sb("v_tm", [128, NTL, 256], BF16)
        r_tm = S.sb("r_tm", [128, NTL, 256], BF16)
        o_sb = S.sb("o_sb", [128, NTL, 256], F32)
        y_tm = S.sb("y_tm", [128, NTL, 256], BF16)
        yfm = [S.sb("yfm%d" % i, [128, TM], BF16) for i in range(2)]
        ssq = S.sb("ssq", [128, 2 * NTL], F32)
        junk = S.sb("junkg", [128, 256], BF16)
        AT = [S.sb("AT%d" % i, [128, 128], BF16) for i in range(2)]
        po = [self.psf[4], self.psf[5]]
        for h in range(4):
            S.op("pool", lambda e: e.memset(St[:], 0.0), writes=[St])
            S.op("pool", lambda e: e.memset(Sb[:], 0.0), writes=[Sb])
            for m in range(SEQ // TM):
                t0 = m * TM
                rows = slice(h * 128, (h + 1) * 128)
                self.load(qT, qT[:], self.GQ, self.GQ[rows, t0:t0 + TM])
                self.load(kT, kT[:], self.GK, self.GK[rows, t0:t0 + TM])
                self.load(gk, gk[:], self.GG, self.GG[rows, t0:t0 + TM])
                self.load(v_tm, v_tm[:], self.GV,
                          self.GV[t0:t0 + TM, h * 256:(h + 1) * 256].rearrange("(n p) c -> p n c", p=128))
                self.load(r_tm, r_tm[:], self.GR,
                          self.GR[t0:t0 + TM, h * 256:(h + 1) * 256].rearrange("(n p) c -> p n c", p=128))
                for c in range(NCH):
                    sl = slice(c * 64, (c + 1) * 64)
                    S.op("dve", lambda e, sl=sl: e.tensor_tensor_scan(out=bb[:, sl], data0=gk[:, sl], data1=gk[:, sl],
                                                                      initial=0.0, op0=ALU.add, op1=ALU.bypass),
                         reads=[gk], writes=[bb])
                self.act(eb, eb[:], bb, bb[:], AF.Exp)
                self.act(enb, enb[:], bb, bb[:], AF.Exp, scale=-1.0)
                self.tt("dve", qd, qd[:], qT, qT[:], eb, eb[:], ALU.mult)
                self.tt("pool", kdn, kdn[:], kT, kT[:], enb, enb[:], ALU.mult)
                for c in range(NCH):
                    sl = slice(c * 64, (c + 1) * 64)
                    self.ts("pool" if c % 2 else "dve", kdl, kdl[:, sl], kdn, kdn[:, sl],
                            eb[:, c * 64 + 63:c * 64 + 64], None, ALU.mult, extra_reads=[eb])
                pb = self.psb[0]
                for n in range(NTL):
                    self.tr(pb, pb[:, n * 128:(n + 1) * 128], kdl, kdl[:, n * 128:(n + 1) * 128], self.identb[:])
                self.copy(kdl_tm, kdl_tm[:], pb, pb[:, :].rearrange("p (n c) -> p n c", n=NTL))
                for n in range(NTL):
                    tsl = slice(n * 128, (n + 1) * 128)
                    pa = self.nps()
                    self.mm(pa, pa[:, 0:128], kdn, kdn[:, tsl], qd, qd[:, tsl], True, True)
                    at = AT[n % 2]
                    self.tt("dve", at, at[:], pa, pa[:, 0:128], cm, cm[:, 0, :], ALU.mult)
                    p_o = po[n % 2]
                    for c in range(2):
                        r0 = 64 * c
                        csl = slice(n * 128 + r0, n * 128 + r0 + 64)
                        self.mmt(p_o, p_o[r0:r0 + 64, 0:256], qd, qd[:, csl], Sb, Sb[:], True, False, (0, r0))
                        self.mmt(p_o, p_o[r0:r0 + 64, 0:256], at, at[r0:r0 + 64, r0:r0 + 64],
                                 v_tm, v_tm[r0:r0 + 64, n, :], False, True, (r0, r0))
                        pst = self.nps()
                        self.mm(pst, pst[:, 0:256], kdl_tm, kdl_tm[r0:r0 + 64, n, :], v_tm, v_tm[r0:r0 + 64, n, :], True, True)
                        ecol = n * 128 + r0 + 63
                        self.stt("dve", St, St[:], St, St[:], eb[:, ecol:ecol + 1], pst, pst[:, 0:256],
                                 ALU.mult, ALU.add, extra_reads=[eb])
                        self.copy(Sb, Sb[:], St, St[:], eng="act")
                    S.op("act", lambda e, p_o=p_o, n=n: e.activation(out=junk[:], in_=p_o[:, 0:256], func=AF.Square,
                                                                     accum_out=ssq[:, n:n + 1]),
                         reads=[p_o], writes=[junk, ssq])
                    self.copy(o_sb, o_sb[:, n, :], p_o, p_o[:, 0:256], eng="dve")
                self.ts("dve", ssq, ssq[:, NTL:2 * NTL], ssq, ssq[:, 0:NTL], 1.0 / 256, EPS, ALU.mult, ALU.add)
                self.act(ssq, ssq[:, NTL:2 * NTL], ssq, ssq[:, NTL:2 * NTL], AF.Sqrt)
                S.op("dve", lambda e: e.reciprocal(out=ssq[:, NTL:2 * NTL], in_=ssq[:, NTL:2 * NTL]), reads=[ssq], writes=[ssq])
                for n in range(NTL):
                    self.stt("dve", y_tm, y_tm[:, n, :], o_sb, o_sb[:, n, :],
                             ssq[:, NTL + n:NTL + n + 1], r_tm, r_tm[:, n, :], ALU.mult, ALU.mult, extra_reads=[ssq])
                for half in range(2):
                    pb = self.psb[1 - half % 2] if False else self.psb[half]
                    for n in range(NTL):
                        self.tr(pb, pb[:, n * 128:(n + 1) * 128], y_tm, y_tm[:, n, half * 128:(half + 1) * 128], self.identb[:])
                    yf = yfm[half]
                    self.copy(yf, yf[:], pb, pb[:, 0:TM])
                    r_ = h * 256 + half * 128
                    self.store(self.Y[1], self.Y[1][r_:r_ + 128, t0:t0 + TM], yf, yf[:])
        self.end_stage()

    def stage_B_gdn(self):
        S = self.S
        SEQ = self.SEQ
        TM = min(1024, SEQ)
        NTL = TM // 128
        cv = self.colv
        cm = S.sb("cm", [128, 6, 128], F32)
        self.load(cm, cm[:], self.cmask_in, self.cmask_in[:, :].rearrange("p (a b) -> p a b", a=6))
        St = [S.sb("St%d" % i, [128, 128], F32) for i in range(8)]
        Sb = [S.sb("Sb%d" % i, [128, 128], BF16) for i in range(8)]
        for i in range(8):
            S.op("pool", lambda e, i=i: e.memset(St[i][:], 0.0), writes=[St[i]])
            S.op("pool", lambda e, i=i: e.memset(Sb[i][:], 0.0), writes=[Sb[i]])
        xt = [S.sb("xt%d" % i, [128, TM + 4], BF16) for i in range(2)]
        acc = S.sb("acc", [128, TM], F32)
        ysil = S.sb("ysil", [128, TM], F32)
        sq = S.sb("sqd", [128, TM], BF16)
        rn = S.sb("rn", [128, 512], F32)
        qnT = S.sb("qnT", [128, TM], BF16)
        knT = S.sb("knT", [128, TM], BF16)
        vT = S.sb("vTd", [128, TM], BF16)
        k_tm = S.sb("k_tm", [128, NTL, 128], BF16)
        v_tm = [S.sb("v_tm%d" % i, [128, NTL, 128], BF16) for i in range(2)]
        z_tm = [S.sb("z_tm%d" % i, [128, NTL, 128], BF16) for i in range(2)]
        o_sb = [S.sb("o_sb%d" % i, [128, NTL, 128], F32) for i in range(2)]
        ssq = [S.sb("ssq%d" % i, [128, 2 * NTL], F32) for i in range(2)]
        y_tm = S.sb("y_tmd", [128, NTL, 128], BF16)
        yfm = [S.sb("yfmd%d" % i, [128, TM], BF16) for i in range(2)]
        bgr = S.sb("bgr", [8, 2, TM], F32)
        bg_tm = S.sb("bg_tm", [128, NTL, 16], F32)
        Gc = S.sb("Gc", [128, NTL, 16], F32)
        dG = S.sb("dG", [128, NTL, 8], F32)
        eGc = S.sb("eGc", [128, NTL, 8], F32)
        KKs = S.sb("KKs", [128, 128], F32)
        QKs = S.sb("QKs", [128, 128], F32)
        junk = S.sb("junkd", [128, 128], BF16)

        def f32t(name):
            return [S.sb("%s%d" % (name, i), [128, 128], F32) for i in range(2)]

        def b16t(name):
            return [S.sb("%s%d" % (name, i), [128, 128], BF16) for i in range(2)]
        Dm, gamT, eG, gs, gm, Xa, XTa, Pm, Ya, YTa, u0 = [f32t(n_) for n_ in
                                                         ("Dm", "gamT", "eG", "gs", "gm", "Xa", "XTa", "Pm", "Ya", "YTa", "u0")]
        qkT, Pb, RV2, w_tm, wT, qdT, kdec, u_tm = [b16t(n_) for n_ in ("qkT", "Pb", "RVx", "w_tm", "wT", "qdT", "kdec", "u_tm")]
        RV = [S.sb("RV%d" % i, [128, 256], BF16) for i in range(2)]
        po = [self.psf[4], self.psf[5]]
        itc = [0]

        def conv_silu(kc, t0, first):
            x = xt[itc[0] % 2]
            itc[0] += 1
            rows = slice(kc * 128, (kc + 1) * 128)
            if first:
                S.op("pool", lambda e: e.memset(x[:, 0:4], 0.0), writes=[x])
                self.load(x, x[:, 4:4 + TM], self.DQKV, self.DQKV[rows, t0:t0 + TM])
            else:
                self.load(x, x[:, 0:4 + TM], self.DQKV, self.DQKV[rows, t0 - 4:t0 + TM])

            def w(tap):
                c_ = CV_DCW + tap * 16 + kc
                return cv[:, c_:c_ + 1]
            self.ts("dve", acc, acc[:], x, x[:, 4:4 + TM], w(3), None, ALU.mult, extra_reads=[cv])
            for tap in (2, 1, 0):
                self.stt("dve", acc, acc[:], x, x[:, 1 + tap:1 + tap + TM], w(tap), acc, acc[:], ALU.mult, ALU.add,
                         extra_reads=[cv])
            self.act(ysil, ysil[:], acc, acc[:], AF.Silu)

        def l2n(dst, qscale):
            self.tt("pool", sq, sq[:], ysil, ysil[:], ysil, ysil[:], ALU.mult)
            for b_ in range(TM // 512):
                sl = slice(b_ * 512, (b_ + 1) * 512)
                ps = self.nps()
                self.mm(ps, ps[:, :], self.onesb, self.onesb[:], sq, sq[:, sl], True, True)
                self.ts("dve", rn, rn[:], ps, ps[:, :], EPS, None, ALU.add)
                self.act(rn, rn[:], rn, rn[:], AF.Sqrt)
                S.op("dve", lambda e: e.reciprocal(out=rn[:], in_=rn[:]), reads=[rn], writes=[rn])
                self.stt("dve", dst, dst[:, sl], ysil, ysil[:, sl], qscale, rn, rn[:], ALU.mult, ALU.mult)

        def to_tm(dst, src):
            pb = self.psb[itc[0] % 2]
            itc[0] += 1
            for n in range(NTL):
                self.tr(pb, pb[:, n * 128:(n + 1) * 128], src, src[:, n * 128:(n + 1) * 128], self.identb[:])
            self.copy(dst, dst[:], pb, pb[:, 0:TM].rearrange("p (n c) -> p n c", n=NTL))

        for m in range(SEQ // TM):
            t0 = m * TM
            first = (m == 0)
            self.load(bgr, bgr[:, 0, :], self.DB, self.DB[0:8, t0:t0 + TM])
            self.load(bgr, bgr[:, 1, :], self.DA, self.DA[0:8, t0:t0 + TM])
            ps = self.nps()
            for n in range(NTL):
                for a_ in range(2):
                    self.tr(ps, ps[:, n * 16 + a_ * 8:n * 16 + a_ * 8 + 8], bgr, bgr[:, a_, n * 128:(n + 1) * 128],
                            self.identf[0:8, 0:8])
            self.copy(bg_tm, bg_tm[:], ps, ps[:, 0:NTL * 16].rearrange("p (n c) -> p n c", n=NTL), eng="dve")
            ps = self.nps()
            for n in range(NTL):
                self.mm(ps, ps[:, n * 16:n * 16 + 8], cm, cm[:, 0, :], bg_tm, bg_tm[:, n, 8:16], True, True)
                self.mm(ps, ps[:, n * 16 + 8:n * 16 + 16], cm, cm[:, 2, :], bg_tm, bg_tm[:, n, 8:16], True, True)
            self.copy(Gc, Gc[:], ps, ps[:, 0:NTL * 16].rearrange("p (n c) -> p n c", n=NTL), eng="dve")
            self.tt("dve", dG, dG[:], Gc, Gc[:, :, 8:16], Gc, Gc[:, :, 0:8], ALU.subtract)
            self.act(dG, dG[:], dG, dG[:], AF.Exp)
            self.act(eGc, eGc[:], Gc, Gc[:, :, 0:8], AF.Exp)
            import os
            STOP = int(os.environ.get("GDN_STOP", "99"))
            if STOP <= 1:
                continue
            for hq in range(4):
                conv_silu(hq, t0, first)
                l2n(qnT, 128.0 ** -0.5)
                conv_silu(4 + hq, t0, first)
                l2n(knT, 1.0)
                to_tm(k_tm, knT)
                for a_ in range(2):
                    hv = 2 * hq + a_
                    conv_silu(8 + hv, t0, first)
                    self.copy(vT, vT[:], ysil, ysil[:], eng="pool")
                    to_tm(v_tm[a_], vT)
                    self.load(z_tm[a_], z_tm[a_][:], self.DZ,
                              self.DZ[t0:t0 + TM, hv * 128:(hv + 1) * 128].rearrange("(n p) c -> p n c", p=128))
                if STOP <= 2:
                    continue
                for n in range(NTL):
                    tsl = slice(n * 128, (n + 1) * 128)
                    ps = self.nps()
                    self.mm(ps, ps[:, 0:128], knT, knT[:, tsl], knT, knT[:, tsl], True, True)
                    self.copy(KKs, KKs[:], ps, ps[:, 0:128], eng="act")
                    ps = self.nps()
                    self.mm(ps, ps[:, 0:128], knT, knT[:, tsl], qnT, qnT[:, tsl], True, True)
                    self.copy(QKs, QKs[:], ps, ps[:, 0:128], eng="act")
                    for a_ in range(2):
                        hv = 2 * hq + a_
                        bcol = bg_tm[:, n, hv:hv + 1]
                        gcol = bg_tm[:, n, 8 + hv:9 + hv]
                        pg = self.nps()
                        self.mm(pg, pg[:, 0:128], bg_tm, gcol.to_broadcast([128, 128]), cm, cm[:, 0, :], True, True)
                        self.ts("dve", Dm[a_], Dm[a_][:], pg, pg[:, 0:128], Gc[:, n, hv:hv + 1], 0.0,
                                ALU.subtract, ALU.min, extra_reads=[Gc])
                        self.act(gamT[a_], gamT[a_][:], Dm[a_], Dm[a_][:], AF.Exp)
                        self.act(eG[a_], eG[a_][:], pg, pg[:, 0:128], AF.Exp)
                        self.tt("pool", gs[a_], gs[a_][:], gamT[a_], gamT[a_][:], cm, cm[:, 1, :], ALU.mult)
                        self.tt("pool", gm[a_], gm[a_][:], gamT[a_], gamT[a_][:], cm, cm[:, 0, :], ALU.mult)
                        X = Xa[a_]
                        self.stt("dve", X, X[:], KKs, KKs[:], bcol, gs[a_], gs[a_][:], ALU.mult, ALU.mult,
                                 extra_reads=[bg_tm])
                        self.tt("dve", qkT[a_], qkT[a_][:], QKs, QKs[:], gm[a_], gm[a_][:], ALU.mult)
                        px = self.nps()
                        self.tr(px, px[:, 0:128], X, X[:], self.identf[:])
                        XT = XTa[a_]
                        self.copy(XT, XT[:], px, px[:, 0:128], eng="act")
                        if STOP <= 3:
                            continue
                        P_ = Pm[a_]
                        self.tt("pool", P_, P_[:], cm, cm[:, 4, :], X, X[:], ALU.subtract)
                        Y, YT = X, XT
                        for k in range(1, 6):
                            pyt = self.nps()
                            self.mm(pyt, pyt[:, 0:128], Y, Y[:], YT, YT[:], True, True)
                            if k < 5:
                                py = self.nps()
                                self.mm(py, py[:, 0:128], YT, YT[:], Y, Y[:], True, True)
                            YTn = YTa[a_] if YT is XT else XT
                            Yn = Ya[a_] if Y is X else X
                            self.copy(YTn, YTn[:], pyt, pyt[:, 0:128], eng="act")
                            if k < 5:
                                self.copy(Yn, Yn[:], py, py[:, 0:128], eng="dve")
                            pp = self.nps()
                            self.mm(pp, pp[:, 0:128], YTn, YTn[:], P_, P_[:], True, True)
                            if k < 5:
                                self.tt("dve", P_, P_[:], P_, P_[:], pp, pp[:, 0:128], ALU.add)
                            else:
                                self.tt("dve", Pb[a_], Pb[a_][:], P_, P_[:], pp, pp[:, 0:128], ALU.add)
                            Y, YT = Yn, YTn
                        if STOP <= 4:
                            continue
                        rv = RV[a_]
                        self.ts("pool", rv, rv[:, 0:128], k_tm, k_tm[:, n, :], eGc[:, n, hv:hv + 1], None, ALU.mult,
                                extra_reads=[eGc])
                        self.copy(rv, rv[:, 128:256], v_tm[a_], v_tm[a_][:, n, :], eng="pool")
                        pw = self.nps()
                        self.mm(pw, pw[:, 0:256], Pb[a_], Pb[a_][:], rv, rv[:], True, True)
                        self.ts("dve", w_tm[a_], w_tm[a_][:], pw, pw[:, 0:128], bcol, None, ALU.mult, extra_reads=[bg_tm])
                        self.ts("dve", u0[a_], u0[a_][:], pw, pw[:, 128:256], bcol, None, ALU.mult, extra_reads=[bg_tm])
                        pb = self.psb[a_]
                        self.tr(pb, pb[:, 0:128], w_tm[a_], w_tm[a_][:], self.identb[:])
                        self.copy(wT[a_], wT[a_][:], pb, pb[:, 0:128], eng="act")
                        self.tt("pool", qdT[a_], qdT[a_][:], qnT, qnT[:, tsl], eG[a_], eG[a_][:], ALU.mult)
                        self.ts("pool", kdec[a_], kdec[a_][:], k_tm, k_tm[:, n, :], dG[:, n, hv:hv + 1], None, ALU.mult,
                                extra_reads=[dG])
                    if STOP <= 5:
                        continue
                    for c in range(2):
                        r0 = 64 * c
                        rs = slice(r0, r0 + 64)
                        for a_ in range(2):
                            hv = 2 * hq + a_
                            SUB = int(os.environ.get("GDN_SUB", "0"))
                            pu = self.nps()
                            self.mm(pu, pu[:, 0:128], wT[a_], wT[a_][:], Sb[hv], Sb[hv][:], True, True)
                            if not SUB & 16:
                                self.tt("dve", u_tm[a_], u_tm[a_][rs, :], u0[a_], u0[a_][rs, :], pu, pu[rs, 0:128], ALU.subtract)
                            p_o = po[a_]
                            if not SUB & 1:
                                self.mm(p_o, p_o[:, 0:128], qdT[a_], qdT[a_][:], Sb[hv], Sb[hv][:], True, False)
                                self.mm(p_o, p_o[:, 0:128], qkT[a_], qkT[a_][rs, :], u_tm[a_], u_tm[a_][rs, :], False, True)
                            else:
                                self.mm(p_o, p_o[:, 0:128], qdT[a_], qdT[a_][:], Sb[hv], Sb[hv][:], True, True)
                            if not SUB & 4:
                                self.copy(o_sb[a_], o_sb[a_][rs, n, :], p_o, p_o[rs, 0:128], eng="act")
                            if not SUB & 2:
                                S.op("act", lambda e, p_o=p_o, a_=a_, rs=rs, n=n: e.activation(
                                    out=junk[rs, :], in_=p_o[rs, 0:128], func=AF.Square, accum_out=ssq[a_][rs, n:n + 1]),
                                    reads=[p_o], writes=[junk, ssq[a_]])
                            if not SUB & 8:
                                pss = self.nps()
                                self.mm(pss, pss[:, 0:128], kdec[a_], kdec[a_][rs, :], u_tm[a_], u_tm[a_][rs, :], True, True)
                                self.stt("dve", St[hv], St[hv][:], St[hv], St[hv][:], eG[a_][:, r0 + 63:r0 + 64],
                                         pss, pss[:, 0:128], ALU.mult, ALU.add, extra_reads=[eG[a_]])
                                self.copy(Sb[hv], Sb[hv][:], St[hv], St[hv][:], eng="act")
                if STOP <= 6:
                    continue
                for a_ in range(2):
                    hv = 2 * hq + a_
                    sq_ = ssq[a_]
                    self.ts("dve", sq_, sq_[:, NTL:2 * NTL], sq_, sq_[:, 0:NTL], 1.0 / 128, EPS, ALU.mult, ALU.add)
                    self.act(sq_, sq_[:, NTL:2 * NTL], sq_, sq_[:, NTL:2 * NTL], AF.Sqrt)
                    S.op("dve", lambda e, sq_=sq_: e.reciprocal(out=sq_[:, NTL:2 * NTL], in_=sq_[:, NTL:2 * NTL]),
                         reads=[sq_], writes=[sq_])
                    for n in range(NTL):
                        self.stt("dve", y_tm, y_tm[:, n, :], o_sb[a_], o_sb[a_][:, n, :], sq_[:, NTL + n:NTL + n + 1],
                                 z_tm[a_], z_tm[a_][:, n, :], ALU.mult, ALU.mult, extra_reads=[sq_])
                    pb = self.psb[a_]
                    for n in range(NTL):
                        self.tr(pb, pb[:, n * 128:(n + 1) * 128], y_tm, y_tm[:, n, :], self.identb[:])
                    yf = yfm[a_]
                    self.copy(yf, yf[:], pb, pb[:, 0:TM])
                    self.store(self.Y[2], self.Y[2][hv * 128:(hv + 1) * 128, t0:t0 + TM], yf, yf[:])
        self.end_stage()

    def stage_C(self, l):
        S = self.S
        SEQ = self.SEQ
        TC = 512
        src = self.x if l == 0 else self.X
        wbr = [S.sb("wbr%d" % i, [128, 8, D], BF16) for i in range(3)]
        for b in range(3):
            self.load(wbr[b], wbr[b][:], self.WBR[b], self.WBR[b][:, :].rearrange("(k p) c -> p k c", p=128))
        yb = [S.sb("yb%d" % i, [128, 8, TC], BF16) for i in range(3)]
        mixed = S.sb("mixed", [128, 16, TC], BF16)
        gt = [S.sb("gt%d" % i, [128, 3, TC], BF16) for i in range(2)]
        t1 = S.sb("t1", [128, TC], F32)
        t2 = S.sb("t2", [128, TC], F32)
        wo = [S.sb("wo%d" % i, [128, 16, 512], BF16) for i in range(2)]
        xs = [S.sb("xs%d" % i, [128, D], F32) for i in range(2)]
        pbk = [self.psf[0], self.psf[1], self.psf[2]]
        wi = 0
        xi = 0
        for m in range(SEQ // TC):
            t0 = m * TC
            for b in range(3):
                self.load(yb[b], yb[b][:], self.Y[b], self.Y[b][:, t0:t0 + TC].rearrange("(k p) t -> p k t", p=128))
            for j in range(16):
                g = gt[j % 2]
                self.load(g, g[:], self.MG,
                          self.MG[:, t0:t0 + TC].rearrange("(b j p) t -> p b j t", b=3, p=128)[:, :, j, :])
                for b in range(3):
                    ps = pbk[b]
                    for k in range(8):
                        self.mm(ps, ps[:, :], wbr[b], wbr[b][:, k, j * 128:(j + 1) * 128], yb[b], yb[b][:, k, :], k == 0, k == 7)
                self.tt("dve", t1, t1[:], pbk[0], pbk[0][:, :], g, g[:, 0, :], ALU.mult)
                self.tt("dve", t2, t2[:], pbk[1], pbk[1][:, :], g, g[:, 1, :], ALU.mult)
                self.tt("pool", t1, t1[:], t1, t1[:], t2, t2[:], ALU.add)
                self.tt("dve", t2, t2[:], pbk[2], pbk[2][:, :], g, g[:, 2, :], ALU.mult)
                self.tt("pool", mixed, mixed[:, j, :], t1, t1[:], t2, t2[:], ALU.add)
            xt = []
            for s_ in range(4):
                x = xs[s_ % 2]
                xt.append(x)
            for pair in range(2):
                for s2 in range(2):
                    s_ = pair * 2 + s2
                    x = xs[s2]
                    self.load(x, x[:], src, src[t0 + s_ * 128:t0 + (s_ + 1) * 128, :])
                for cb in range(4):
                    w = wo[wi % 2]
                    wi += 1
                    self.load(w, w[:], self.WO, self.WO[:, cb * 512:(cb + 1) * 512].rearrange("(k p) c -> p k c", p=128))
                    for s2 in range(2):
                        s_ = pair * 2 + s2
                        x = xs[s2]
                        ps = self.psf[3 + s2]
                        for k in range(16):
                            self.mm(ps, ps[:, :], mixed, mixed[:, k, s_ * 128:(s_ + 1) * 128], w, w[:, k, :], k == 0, k == 15)
                        csl = slice(cb * 512, (cb + 1) * 512)
                        self.tt("dve", x, x[:, csl], x, x[:, csl], ps, ps[:, :], ALU.add)
                for s2 in range(2):
                    s_ = pair * 2 + s2
                    x = xs[s2]
                    self.store(self.X, self.X[t0 + s_ * 128:t0 + (s_ + 1) * 128, :], x, x[:])
        self.end_stage()

    def stage_D(self, l, last):
        S = self.S
        SEQ = self.SEQ
        TD = 512
        cv = self.colv
        NJ = DFF // 128
        halo = S.sb("halo", [128, NJ, 2], F32)
        S.op("pool", lambda e: e.memset(halo[:], 0.0), writes=[halo])
        rowv = S.sb("rowv", [128, D], F32)
        if last:
            self.load(rowv, rowv[:], self.rowv_in, self.rowv_in[l, :, :])
        fixed = S.arena
        fidx = len(S.stage_sb)
        for m in range(SEQ // TD):
            t0 = m * TD
            S.arena = fixed
            S.release_from(fidx)
            hT = S.sb("hTd", [128, 16, TD], BF16)
            self.norm_transpose(self.X, t0, TD, hT)
            wg = [S.sb("wgd%d" % i, [128, 16, 256], BF16) for i in range(2)]
            wu = [S.sb("wud%d" % i, [128, 16, 256], BF16) for i in range(2)]
            actT = S.sb("actT", [128, NJ, TD], BF16)
            gsb = [S.sb("gsb%d" % i, [128, TD + 2], F32) for i in range(2)]
            cc = [S.sb("cc%d" % i, [128, TD], F32) for i in range(2)]
            wdn = [S.sb("wdn%d" % i, [128, NJ, 256], BF16) for i in range(2)]
            xs = [S.sb("xsd%d" % i, [128, D], F32) for i in range(4)]
            junk = S.sb("junkf", [128, D], BF16)
            ssf = S.sb("ssf", [128, 4], F32)
            for blk in range(DFF // 256):
                a, b = wg[blk % 2], wu[blk % 2]
                self.load(a, a[:], self.WUP, self.WUP[:, blk * 256:(blk + 1) * 256].rearrange("(k p) c -> p k c", p=128))
                self.load(b, b[:], self.WUP,
                          self.WUP[:, DFF + blk * 256:DFF + (blk + 1) * 256].rearrange("(k p) c -> p k c", p=128))
                for jj in range(2):
                    j = blk * 2 + jj
                    pg = self.psf[jj]
                    pu = self.psf[2 + jj]
                    for k in range(16):
                        self.mm(pg, pg[:, :], a, a[:, k, jj * 128:(jj + 1) * 128], hT, hT[:, k, :], k == 0, k == 15)
                    for k in range(16):
                        self.mm(pu, pu[:, :], b, b[:, k, jj * 128:(jj + 1) * 128], hT, hT[:, k, :], k == 0, k == 15)
                    g = gsb[j % 2]
                    c = cc[j % 2]
                    self.copy(g, g[:, 0:2], halo, halo[:, j, :], eng="pool")
                    self.copy(g, g[:, 2:TD + 2], pg, pg[:, :], eng="act")
                    self.copy(halo, halo[:, j, :], g, g[:, TD:TD + 2], eng="pool")

                    def w(tap):
                        c_ = CV_FCW + tap * NJ + j
                        return cv[:, c_:c_ + 1]
                    self.ts("dve", c, c[:], g, g[:, 2:TD + 2], w(2), cv[:, CV_FCB + j:CV_FCB + j + 1], ALU.mult, ALU.add,
                            extra_reads=[cv])
                    self.stt("dve", c, c[:], g, g[:, 1:TD + 1], w(1), c, c[:], ALU.mult, ALU.add, extra_reads=[cv])
                    self.stt("dve", c, c[:], g, g[:, 0:TD], w(0), c, c[:], ALU.mult, ALU.add, extra_reads=[cv])
                    self.act(c, c[:], c, c[:], AF.Silu)
                    self.tt("dve", actT, actT[:, j, :], c, c[:], pu, pu[:, :], ALU.mult)
            for s_ in range(4):
                self.load(xs[s_], xs[s_][:], self.X, self.X[t0 + s_ * 128:t0 + (s_ + 1) * 128, :])
            for cb in range(8):
                w_ = wdn[cb % 2]
                self.load(w_, w_[:], self.WDN, self.WDN[:, cb * 256:(cb + 1) * 256].rearrange("(k p) c -> p k c", p=128))
                for s_ in range(4):
                    ps = self.psf[s_ % 4]
                    for k in range(NJ):
                        self.mm(ps, ps[:, 0:256], actT, actT[:, k, s_ * 128:(s_ + 1) * 128], w_, w_[:, k, :], k == 0, k == NJ - 1)
                    csl = slice(cb * 256, (cb + 1) * 256)
                    self.tt("dve", xs[s_], xs[s_][:, csl], xs[s_], xs[s_][:, csl], ps, ps[:, 0:256], ALU.add)
            for s_ in range(4):
                x = xs[s_]
                rows = slice(t0 + s_ * 128, t0 + (s_ + 1) * 128)
                if not last:
                    self.store(self.X, self.X[rows, :], x, x[:])
                else:
                    if self.xres is not None:
                        self.store(self.xres, self.xres[rows, :], x, x[:])
                    S.op("act", lambda e, x=x: e.activation(out=junk[:], in_=x[:], func=AF.Square, accum_out=ssf[:, 0:1]),
                         reads=[x], writes=[junk, ssf])
                    self.ts("dve", ssf, ssf[:, 1:2], ssf, ssf[:, 0:1], 1.0 / D, EPS, ALU.mult, ALU.add)
                    self.act(ssf, ssf[:, 2:3], ssf, ssf[:, 1:2], AF.Sqrt)
                    S.op("dve", lambda e: e.reciprocal(out=ssf[:, 2:3], in_=ssf[:, 2:3]), reads=[ssf], writes=[ssf])
                    self.stt("dve", x, x[:], x, x[:], ssf[:, 2:3], rowv, rowv[:], ALU.mult, ALU.mult, extra_reads=[ssf])
                    self.store(self.out, self.out[rows, :], x, x[:])
            S.barrier()
        self.end_stage()


def build(SEQ, DEPTH, debug=False, stages="all", per_layer=False):
    nc = bass.Bass("TRN2", target_bir_lowering=False)
    with contextlib.ExitStack() as es:
        P = Prog(nc, es, SEQ, DEPTH, debug, per_layer)
        NM = SEQ // P.MT
        for l in range(DEPTH):
            P.stage_W(l)
            for m in range(NM):
                P.stage_A(l, m)
            if stages == "A":
                break
            if "B1" in stages or stages == "all":
                P.stage_B_mla()
            if "B2" in stages or stages == "all":
                P.stage_B_gla()
            if "B3" in stages or stages == "all":
                P.stage_B_gdn()
            if stages != "all":
                break
            P.stage_C(l)
            P.stage_D(l, l == DEPTH - 1)
        P.S.barrier()
        P.S.emit()
        print("instructions", P.S.n_ins, "waits", P.S.n_wait, "sems", P.S.nsem, flush=True)
    return nc


def host_inputs(inp, SEQ, DEPTH):
    bf = ml_dtypes.bfloat16
    L = DEPTH
    f = lambda k: np.ascontiguousarray(np.asarray(inp[k], dtype=np.float32))

    def colmaj(v):
        return np.ascontiguousarray(v.reshape(-1, 128).T)
    colv = np.zeros((L, 128, NCOLV), np.float32)
    rowv = np.zeros((L, 128, 2048), np.float32)
    for l in range(L):
        colv[l, :, CV_AN:CV_AN + 16] = colmaj(f("attn_norm")[l])
        colv[l, :, CV_FN:CV_FN + 16] = colmaj(f("ffn_norm")[l])
        colv[l, :, CV_QN:CV_QN + 4] = colmaj(f("mla_q_norm")[l])
        colv[l, :, CV_KVN:CV_KVN + 4] = colmaj(f("mla_kv_norm")[l])
        colv[l, :, CV_GB:CV_GB + 4] = colmaj(f("gla_gate_bias")[l])
        for tap in range(4):
            colv[l, :, CV_DCW + tap * 16:CV_DCW + (tap + 1) * 16] = colmaj(f("gdn_conv_w")[l, tap])
        for tap in range(3):
            colv[l, :, CV_FCW + tap * 44:CV_FCW + (tap + 1) * 44] = colmaj(f("ffn_conv_w")[l, tap])
        colv[l, :, CV_FCB:CV_FCB + 44] = colmaj(f("ffn_conv_b")[l])
        colv[l, 0:8, CV_ALOG] = f("gdn_a_log")[l]
        colv[l, 0:8, CV_DTB] = f("gdn_dt_bias")[l]
        colv[l, :, CV_FIN:CV_FIN + 16] = colmaj(f("final_norm"))
        colv[l, :, CV_GLN:CV_GLN + 8] = colmaj(f("gla_out_norm")[l])
        colv[l, :, CV_GDNN:CV_GDNN + 8] = colmaj(f("gdn_out_norm")[l])
        rowv[l, :, :] = f("final_norm")[None, :]
    cst = np.zeros((128, 8), np.float32)
    inv = (10000.0 ** (-np.arange(0, 64, 2, dtype=np.float32) / 64)).astype(np.float32)
    cst[0:32, 0] = inv
    cst[32:64, 0] = inv
    cst[0:32, 1] = -1.0
    cst[32:64, 1] = 1.0
    cst[:, 2] = -math.pi
    masks = np.zeros((128, 4, 512), np.float32)
    kl = np.arange(128)[:, None]
    ql = np.arange(512)[None, :]
    for j in range(4):
        masks[:, j, :] = (ql >= 128 * j + kl)
    jj = np.arange(128)[:, None]
    ii = np.arange(128)[None, :]
    same = (jj // 64) == (ii // 64)
    cm = np.zeros((128, 6, 128), np.float32)
    cm[:, 0] = same & (ii >= jj)
    cm[:, 1] = same & (ii > jj)
    cm[:, 2] = same
    cm[:, 3] = same & (jj <= ii)
    cm[:, 4] = np.eye(128)
    m = {
        "x": f("x").reshape(SEQ, D),
        "pos": np.ascontiguousarray(np.asarray(inp["positions"], np.int32).reshape(1, SEQ)),
        "w_in": f("w_in"), "w_uq": f("mla_w_uq"), "w_ukv": f("mla_w_ukv"), "w_g2": f("gla_w_gate2"),
        "w_bm": f("w_branch_mla"), "w_bg": f("w_branch_gla"), "w_bd": f("w_branch_gdn"),
        "w_o": f("w_out"), "w_up": f("ffn_w_up"), "w_dn": f("ffn_w_down"),
        "colv": colv, "rowv": rowv, "cst": cst,
        "identb": np.eye(128, dtype=np.float32).astype(bf), "identf": np.eye(128, dtype=np.float32),
        "masks": masks.reshape(128, 2048).astype(bf), "cmask": cm.reshape(128, 768),
    }
    return m


_NC_CACHE = {}
LAUNCH_MODE = "layer"


def kernel(**inputs):
    SEQ = int(np.asarray(inputs["x"]).shape[1])
    DEPTH = int(np.asarray(inputs["w_in"]).shape[0])
    if LAUNCH_MODE == "fused":
        key = (SEQ, DEPTH, "fused")
        if key not in _NC_CACHE:
            _NC_CACHE[key] = build(SEQ, DEPTH)
        nc = _NC_CACHE[key]
        m = host_inputs(inputs, SEQ, DEPTH)
        res = run_bass_kernel_spmd(nc, [m], core_ids=[0])
        return np.asarray(res.results[0]["out"], dtype=np.float32).reshape(1, SEQ, D)
    key = (SEQ, "layer")
    if key not in _NC_CACHE:
        _NC_CACHE[key] = build(SEQ, 1, per_layer=True)
    nc = _NC_CACHE[key]
    wkeys = ("attn_norm", "w_in", "mla_q_norm", "mla_kv_norm", "mla_w_uq", "mla_w_ukv", "gla_w_gate2",
             "gla_gate_bias", "gla_out_norm", "gdn_conv_w", "gdn_a_log", "gdn_dt_bias", "gdn_out_norm",
             "w_branch_mla", "w_branch_gla", "w_branch_gdn", "w_out", "ffn_norm", "ffn_w_up", "ffn_conv_w",
             "ffn_conv_b", "ffn_w_down")
    x = np.asarray(inputs["x"], np.float32)
    out = None
    for l in range(DEPTH):
        sub = {k: np.asarray(inputs[k])[l:l + 1] for k in wkeys}
        sub["x"] = x
        sub["positions"] = inputs["positions"]
        sub["final_norm"] = inputs["final_norm"]
        m = host_inputs(sub, SEQ, 1)
        res = run_bass_kernel_spmd(nc, [m], core_ids=[0])
        r = res.results[0]
        x = np.asarray(r["xres"], dtype=np.float32).reshape(1, SEQ, D)
        out = r["out"]
    return np.asarray(out, dtype=np.float32).reshape(1, SEQ, D)
```

```python
import contextlib
import math
import numpy as np
import ml_dtypes
import concourse.bass as bass
import concourse.mybir as mybir
from concourse.bass_utils import run_bass_kernel_spmd

F32 = mybir.dt.float32
BF16 = mybir.dt.bfloat16
I32 = mybir.dt.int32
AF = mybir.ActivationFunctionType
ALU = mybir.AluOpType
AX = mybir.AxisListType

ENGS = ("pe", "dve", "act", "pool", "sp")
EPOCH = 60000
EPS = 1e-6

D = 2048
INW = 13408
DFF = 5632
NH = 8


def _sz(dt):
    return 2 if dt == BF16 else 4


class Buf:
    __slots__ = ("name", "w", "r", "dsem", "dcnt", "mw")

    def __init__(self, name):
        self.name = name
        self.w = None
        self.r = []
        self.dsem = None
        self.dcnt = 0
        self.mw = False


class T:
    __slots__ = ("t", "b")

    def __init__(self, t, name):
        self.t = t
        self.b = Buf(name)

    def __getitem__(self, k):
        return self.t[k]


class Sched:
    def __init__(self, nc, es):
        self.nc = nc
        self.es = es
        self.q = {e: [] for e in ENGS}
        self.cnt = {e: 0 for e in ENGS}
        self.semobj = {}
        self.esem = {}
        self.nsem = 0
        self.free_dsems = []
        self.dma_bufs = []
        for e in ENGS:
            self._new_eng_sem(e)
        self.seen = {e: {} for e in ENGS}
        self.n_ins = 0
        self.n_wait = 0
        self.arena = 16640
        self.uid = 0
        self.rr = 0
        self.stage_sb = []

    def _alloc_sem(self, name):
        s = self.es.enter_context(self.nc.semaphore(name))
        self.nsem += 1
        key = "s%d" % self.nsem
        self.semobj[key] = s
        return key

    def _new_eng_sem(self, e):
        self.esem[e] = self._alloc_sem("e_%s_%d" % (e, self.nsem))
        self.cnt[e] = 0

    def sb(self, name, shape, dt):
        self.uid += 1
        nbytes = int(np.prod(shape[1:])) * _sz(dt)
        off = (self.arena + 63) // 64 * 64
        assert off + nbytes <= 229376, ("SBUF overflow", name, off, nbytes)
        t = self.nc.alloc_sbuf_tensor_at("%s_%d" % (name, self.uid), list(shape), dt, offset=off)
        self.arena = off + nbytes
        r = T(t, name)
        self.stage_sb.append(r.b)
        return r

    def dram(self, name, shape, dt, kind="Internal"):
        t = self.nc.dram_tensor(name, list(shape), dt, kind=kind)
        return T(t.ap(), name)

    def _deps(self, eng, reads, writes):
        need = {}

        def add(ev):
            if ev is None:
                return
            k, v, src = ev
            if src == "pe" and eng == "pe":
                return
            if need.get(k, 0) < v:
                need[k] = v
        for b in reads:
            add(b.w)
        for b in writes:
            if not b.mw:
                add(b.w)
            for ev in b.r:
                add(ev)
        waits = []
        seen = self.seen[eng]
        for k, v in need.items():
            if seen.get(k, 0) >= v:
                continue
            seen[k] = v
            waits.append((k, v))
        return waits

    def _mark(self, ev, reads, writes):
        for b in reads:
            if b in writes:
                continue
            b.r = [x for x in b.r if x[0] != ev[0]] + [ev]
        for b in writes:
            b.w = ev
            b.r = []

    def op(self, eng, fn, reads=(), writes=()):
        reads = [x.b if isinstance(x, T) else x for x in reads]
        writes = [x.b if isinstance(x, T) else x for x in writes]
        waits = self._deps(eng, reads, writes)
        if self.cnt[eng] >= EPOCH:
            self._new_eng_sem(eng)
        self.cnt[eng] += 1
        ev = (self.esem[eng], self.cnt[eng], eng)
        self._mark(ev, reads, writes)
        self.q[eng].append((waits, fn, (self.esem[eng], 1)))
        self.n_ins += 1
        self.n_wait += len(waits)
        return ev

    def dma(self, fn, reads=(), writes=(), eng=None):
        reads = [x.b if isinstance(x, T) else x for x in reads]
        writes = [x.b if isinstance(x, T) else x for x in writes]
        if eng is None:
            eng = "sp"
        waits = self._deps(eng, reads, writes)
        sb = writes[0]
        if sb.dsem is None:
            if self.free_dsems:
                sb.dsem, sb.dcnt = self.free_dsems.pop()
            else:
                sb.dsem = self._alloc_sem("d_%s_%d" % (sb.name, self.nsem))
            self.dma_bufs.append(sb)
        sb.dcnt += 16
        ev = (sb.dsem, sb.dcnt, "dma")
        self._mark(ev, reads, writes)
        self.q[eng].append((waits, fn, (sb.dsem, 16)))
        self.n_ins += 1
        self.n_wait += len(waits)
        return ev

    def release_from(self, idx):
        for b in self.stage_sb[idx:]:
            if b.dsem is not None:
                self.free_dsems.append((b.dsem, b.dcnt))
                self.dma_bufs.remove(b)
                b.dsem = None
        del self.stage_sb[idx:]

    def barrier(self):
        waits = []
        seen = self.seen["sp"]
        for e in ENGS:
            if e == "sp":
                continue
            k, v = self.esem[e], self.cnt[e]
            if v > 0 and seen.get(k, 0) < v:
                seen[k] = v
                waits.append((k, v))
        for b in self.dma_bufs:
            if b.dcnt > 0 and seen.get(b.dsem, 0) < b.dcnt:
                seen[b.dsem] = b.dcnt
                waits.append((b.dsem, b.dcnt))
        if self.cnt["sp"] >= EPOCH:
            self._new_eng_sem("sp")
        self.cnt["sp"] += 1
        k, v = self.esem["sp"], self.cnt["sp"]
        self.q["sp"].append((waits, lambda e: e.nop(), (k, 1)))
        for e in ENGS:
            if e == "sp":
                continue
            self.seen[e][k] = v
            self.q[e].append(([(k, v)], None, None))

    def emit(self):
        nc = self.nc
        so = self.semobj
        with nc.Block() as block:
            def run(e):
                def body(engine):
                    for waits, fn, inc in self.q[e]:
                        if fn is None:
                            for k, v in waits:
                                engine.wait_ge(so[k], v)
                            continue
                        for k, v in waits[:-1]:
                            engine.wait_ge(so[k], v)
                        ins = fn(engine)
                        if waits:
                            k, v = waits[-1]
                            ins._wait_ge(so[k], v)
                        ins.then_inc(so[inc[0]], inc[1])
                return body
            block.tensor(run("pe"))
            block.vector(run("dve"))
            block.scalar(run("act"))
            block.gpsimd(run("pool"))
            block.sync(run("sp"))


NCOLV = 320
CV_GLN, CV_GDNN = 304, 312
CV_AN, CV_FN, CV_QN, CV_KVN, CV_GB, CV_DCW, CV_FCW, CV_FCB, CV_ALOG, CV_DTB, CV_FIN = (
    0, 16, 32, 36, 40, 44, 108, 240, 284, 285, 286)


class Prog:
    def __init__(self, nc, es, SEQ, DEPTH, debug=False, per_layer=False):
        self.nc = nc
        self.S = Sched(nc, es)
        self.SEQ = SEQ
        self.DEPTH = DEPTH
        self.debug = debug
        self.MT = min(1024, SEQ)
        self.ev = 0
        self.pi = 0
        S = self.S
        kind = "ExternalOutput" if debug else "Internal"
        self.dbg_kind = kind

        def ext(name, shape, dt=F32):
            t = nc.dram_tensor(name, list(shape), dt, kind="ExternalInput")
            return T(t.ap(), name)
        L = DEPTH
        self.x = ext("x", [SEQ, D])
        self.pos = ext("pos", [1, SEQ], I32)
        self.w_in = ext("w_in", [L, D, INW])
        self.w_uq = ext("w_uq", [L, 512, 1536])
        self.w_ukv = ext("w_ukv", [L, 512, 2048])
        self.w_g2 = ext("w_g2", [L, 16, 512])
        self.w_bm = ext("w_bm", [L, 1024, D])
        self.w_bg = ext("w_bg", [L, 1024, D])
        self.w_bd = ext("w_bd", [L, 1024, D])
        self.w_o = ext("w_o", [L, D, D])
        self.w_up = ext("w_up", [L, D, 2 * DFF])
        self.w_dn = ext("w_dn", [L, DFF, D])
        self.colv_in = ext("colv", [L, 128, NCOLV])
        self.rowv_in = ext("rowv", [L, 128, 2048])
        self.cst_in = ext("cst", [128, 8])
        self.identb_in = ext("identb", [128, 128], BF16)
        self.identf_in = ext("identf", [128, 128])
        self.masks_in = ext("masks", [128, 4 * 512], BF16)
        self.cmask_in = ext("cmask", [128, 6 * 128])
        out_t = nc.dram_tensor("out", [SEQ, D], F32, kind="ExternalOutput")
        self.out = T(out_t.ap(), "out")
        self.xres = None
        self.per_layer = per_layer

        dr = S.dram
        self.X = dr("Xs", [SEQ, D], F32)
        self.WIN = dr("WIN", [D, INW + 64], BF16)
        self.WUQ = dr("WUQ", [512, 1536 + 512], BF16)
        self.WUKV = dr("WUKV", [512, 2048], BF16)
        self.WG2 = dr("WG2", [16, 512], BF16)
        self.WBR = [dr("WBR%d" % i, [1024, D], BF16) for i in range(3)]
        self.WO = dr("WO", [D, D], BF16)
        self.WUP = dr("WUP", [D, 2 * DFF], BF16)
        self.WDN = dr("WDN", [DFF, D], BF16)
        self.QN = dr("QN", [NH * 128, SEQ], BF16, kind)
        self.QR = dr("QR", [NH * 64, SEQ], BF16, kind)
        self.KN = dr("KN", [NH * 128, SEQ], BF16, kind)
        self.KR = dr("KR", [64, SEQ], BF16, kind)
        self.V = dr("Vv", [SEQ, 1024], BF16, kind)
        self.GQ = dr("GQ", [512, SEQ], BF16, kind)
        self.GK = dr("GK", [512, SEQ], BF16, kind)
        self.GG = dr("GG", [512, SEQ], F32, kind)
        self.GV = dr("GV", [SEQ, 1024], BF16, kind)
        self.GR = dr("GR", [SEQ, 1024], BF16, kind)
        self.DQKV = dr("DQKV", [2048, SEQ], BF16, kind)
        self.DB = dr("DB", [8, SEQ], F32, kind)
        self.DA = dr("DA", [8, SEQ], F32, kind)
        self.DZ = dr("DZ", [SEQ, 1024], BF16, kind)
        self.MG = dr("MG", [3 * D, SEQ], BF16, kind)
        self.Y = [dr("Y%d" % i, [1024, SEQ], BF16, kind) for i in range(3)]
        for t in ([self.X, self.WIN, self.WUQ, self.WUKV, self.WG2, self.WO, self.WUP, self.WDN,
                   self.QN, self.QR, self.KN, self.KR, self.V, self.GQ, self.GK, self.GG, self.GV,
                   self.GR, self.DQKV, self.DB, self.DA, self.DZ, self.MG, self.out]
                  + self.WBR + self.Y):
            t.b.mw = True

        self.psf = [T(nc.alloc_psum_tensor("psf%d" % i, [128, 512], F32), "psf%d" % i) for i in range(6)]
        self.psb = [T(nc.alloc_psum_tensor("psb%d" % i, [128, 1024], BF16), "psb%d" % i) for i in range(2)]

        self.identb = S.sb("identb", [128, 128], BF16)
        self.identf = S.sb("identf", [128, 128], F32)
        self.onesb = S.sb("onesb", [128, 128], BF16)
        self.cst = S.sb("cst", [128, 8], F32)
        self.colv = S.sb("colv", [128, NCOLV], F32)
        self.ncolv = S.sb("ncolv", [128, 8], F32)
        ld = self.load
        ld(self.identb, self.identb[:], self.identb_in, self.identb_in[:, :])
        ld(self.identf, self.identf[:], self.identf_in, self.identf_in[:, :])
        ld(self.cst, self.cst[:], self.cst_in, self.cst_in[:, :])
        S.op("pool", lambda e: e.memset(self.onesb[:], 1.0), writes=[self.onesb])
        self.base_arena = S.arena
        self.persist = list(S.stage_sb)
        S.stage_sb = []

    def load(self, dT, dap, sT, sap, eng="sp"):
        self.S.dma(lambda e: e.dma_start(out=dap, in_=sap), reads=[sT], writes=[dT], eng=eng)

    def store(self, dT, dap, sT, sap, eng="pool"):
        self.S.dma(lambda e: e.dma_start(out=dap, in_=sap), reads=[sT], writes=[dT], eng=eng)

    def nps(self):
        self.pi += 1
        return self.psf[self.pi % 4]

    def copy_eng(self):
        self.ev += 1
        return "act" if self.ev % 2 else "dve"

    def copy(self, oT, oap, iT, iap, eng=None):
        eng = eng or self.copy_eng()
        if eng == "act":
            self.S.op("act", lambda e: e.activation(out=oap, in_=iap, func=AF.Copy), reads=[iT], writes=[oT])
        else:
            self.S.op(eng, lambda e: e.tensor_copy(out=oap, in_=iap), reads=[iT], writes=[oT])

    def act(self, oT, oap, iT, iap, func, scale=1.0, bias=0.0, extra_reads=()):
        self.S.op("act", lambda e: e.activation(out=oap, in_=iap, func=func, bias=bias, scale=scale),
                  reads=[iT] + list(extra_reads), writes=[oT])

    def tt(self, eng, oT, oap, aT, aap, bT, bap, op):
        self.S.op(eng, lambda e: e.tensor_tensor(out=oap, in0=aap, in1=bap, op=op), reads=[aT, bT], writes=[oT])

    def ts(self, eng, oT, oap, aT, aap, s1, s2, op0, op1=None, extra_reads=()):
        if op1 is None:
            self.S.op(eng, lambda e: e.tensor_scalar(out=oap, in0=aap, scalar1=s1, scalar2=None, op0=op0),
                      reads=[aT] + list(extra_reads), writes=[oT])
        else:
            self.S.op(eng, lambda e: e.tensor_scalar(out=oap, in0=aap, scalar1=s1, scalar2=s2, op0=op0, op1=op1),
                      reads=[aT] + list(extra_reads), writes=[oT])

    def stt(self, eng, oT, oap, aT, aap, scalar, bT, bap, op0, op1, extra_reads=()):
        self.S.op(eng, lambda e: e.scalar_tensor_tensor(out=oap, in0=aap, scalar=scalar, in1=bap, op0=op0, op1=op1),
                  reads=[aT, bT] + list(extra_reads), writes=[oT])

    def mm(self, psT, pap, lT, lap, rT, rap, start, stop):
        self.S.op("pe", lambda e: e.matmul(pap, lap, rap, start=start, stop=stop), reads=[lT, rT], writes=[psT])

    def tr(self, psT, pap, iT, iap, ident):
        self.S.op("pe", lambda e: e.transpose(pap, iap, ident), reads=[iT], writes=[psT])

    def end_stage(self):
        S = self.S
        S.barrier()
        S.arena = self.base_arena
        keep = set(id(b) for b in self.persist)
        for b in S.stage_sb:
            if id(b) in keep:
                continue
            if b.dsem is not None:
                S.free_dsems.append((b.dsem, b.dcnt))
                S.dma_bufs.remove(b)
                b.dsem = None
        S.stage_sb = []

    def stage_W(self, l):
        S = self.S
        cv = self.colv
        self.load(cv, cv[:], self.colv_in, self.colv_in[l, :, :])
        self.ts("dve", self.ncolv, self.ncolv[:, 0:4], cv, cv[:, CV_GB:CV_GB + 4], -1.0, None, ALU.mult)
        self.act(self.ncolv, self.ncolv[0:8, 4:5], cv, cv[0:8, CV_ALOG:CV_ALOG + 1], AF.Exp)
        self.ts("dve", self.ncolv, self.ncolv[0:8, 4:5], self.ncolv, self.ncolv[0:8, 4:5], -1.0, None, ALU.mult)
        stf = [S.sb("wstf%d" % i, [128, 2048], F32) for i in range(3)]
        stb = [S.sb("wstb%d" % i, [128, 2048], BF16) for i in range(3)]
        cnt = [0]

        def cast(src_ap, R, C, dstT, dc0=0, scol=None):
            for r0 in range(0, R, 128):
                rr = min(128, R - r0)
                for c0 in range(0, C, 2048):
                    cc = min(2048, C - c0)
                    i = cnt[0] % 3
                    cnt[0] += 1
                    f, b = stf[i], stb[i]
                    self.load(f, f[:rr, :cc], self.w_in, src_ap[r0:r0 + rr, c0:c0 + cc])
                    eng = "dve" if cnt[0] % 2 else "pool"
                    if scol is not None:
                        sc = cv[:rr, scol + r0 // 128: scol + r0 // 128 + 1]
                        self.ts(eng, b, b[:rr, :cc], f, f[:rr, :cc], sc, None, ALU.mult, extra_reads=[cv])
                    else:
                        self.copy(b, b[:rr, :cc], f, f[:rr, :cc], eng=eng)
                    self.store(dstT, dstT[r0:r0 + rr, dc0 + c0:dc0 + c0 + cc], b, b[:rr, :cc])
        cast(self.w_in[l], D, INW, self.WIN, 0, CV_AN)
        cast(self.w_in[l][:, 1056:1088], D, 32, self.WIN, INW, CV_AN)
        cast(self.w_in[l][:, 1024:1056], D, 32, self.WIN, INW + 32, CV_AN)
        cast(self.w_uq[l], 512, 1536, self.WUQ, 0, CV_QN)
        for h in range(NH):
            c = h * 192 + 128
            cast(self.w_uq[l][:, c + 32:c + 64], 512, 32, self.WUQ, 1536 + h * 64, CV_QN)
            cast(self.w_uq[l][:, c:c + 32], 512, 32, self.WUQ, 1536 + h * 64 + 32, CV_QN)
        cast(self.w_ukv[l], 512, 2048, self.WUKV, 0, CV_KVN)
        cast(self.w_g2[l], 16, 512, self.WG2)
        cast(self.w_bm[l], 1024, D, self.WBR[0])
        cast(self.w_bg[l], 1024, D, self.WBR[1], 0, CV_GLN)
        cast(self.w_bd[l], 1024, D, self.WBR[2], 0, CV_GDNN)
        cast(self.w_o[l], D, D, self.WO)
        cast(self.w_up[l], D, 2 * DFF, self.WUP, 0, CV_FN)
        cast(self.w_dn[l], DFF, D, self.WDN)
        self.end_stage()

    def norm_transpose(self, srcT, t0, MT, hT, xkeep=None):
        S = self.S
        mark = S.arena
        sidx = len(S.stage_sb)
        xin = [S.sb("xin%d" % i, [128, D], F32) for i in range(2)]
        xn = [S.sb("xn%d" % i, [128, D], BF16) for i in range(2)]
        junk = S.sb("junk", [128, D], BF16)
        ss = S.sb("ss", [128, 4], F32)
        for s in range(MT // 128):
            xt, xb = xin[s % 2], xn[s % 2]
            self.load(xt, xt[:], srcT, srcT[t0 + s * 128:t0 + (s + 1) * 128, :])
            S.op("act", lambda e, xt=xt: e.activation(out=junk[:], in_=xt[:], func=AF.Square, accum_out=ss[:, 0:1]),
                 reads=[xt], writes=[junk, ss])
            self.ts("dve", ss, ss[:, 1:2], ss, ss[:, 0:1], 1.0 / D, EPS, ALU.mult, ALU.add)
            self.act(ss, ss[:, 2:3], ss, ss[:, 1:2], AF.Sqrt)
            S.op("dve", lambda e: e.reciprocal(out=ss[:, 2:3], in_=ss[:, 2:3]), reads=[ss], writes=[ss])
            self.ts("dve", xb, xb[:], xt, xt[:], ss[:, 2:3], None, ALU.mult, extra_reads=[ss])
            for half in range(2):
                pb = self.psb[half]
                for k8 in range(8):
                    kc = half * 8 + k8
                    self.tr(pb, pb[:, k8 * 128:(k8 + 1) * 128], xb, xb[:, kc * 128:(kc + 1) * 128], self.identb[:])
                self.copy(hT, hT[:, half * 8:(half + 1) * 8, s * 128:(s + 1) * 128],
                          pb, pb[:, :].rearrange("p (k t) -> p k t", k=8))
        S.barrier()
        S.arena = mark
        S.release_from(sidx)

    def stage_A(self, l, m):
        S = self.S
        MT = self.MT
        t0 = m * MT
        NT = MT // 512
        NSUB = MT // 128
        cv = self.colv
        src = self.x if l == 0 else self.X
        Ct = S.sb("Ct", [64, MT], F32)
        Sg = S.sb("Sg", [64, MT], F32)
        mark = S.arena
        sidx = len(S.stage_sb)
        posi = S.sb("posi", [64, MT], I32)
        ang = S.sb("ang", [64, MT], F32)
        tmpa = S.sb("tmpa", [64, MT], F32)
        self.load(posi, posi[:], self.pos, self.pos[0:1, t0:t0 + MT].partition_broadcast(64))
        self.copy(ang, ang[:], posi, posi[:], eng="dve")
        self.ts("dve", ang, ang[:], ang, ang[:], self.cst[0:64, 0:1], None, ALU.mult, extra_reads=[self.cst])
        HI = 6.28125
        LO = 2.0 * math.pi - 6.28125

        def sin_of(dst, shift):
            src = ang
            if shift != 0.0:
                self.ts("dve", tmpa, tmpa[:], ang, ang[:], shift, None, ALU.add)
                src = tmpa
            self.ts("dve", posi, posi[:], src, src[:], 1.0 / (2.0 * math.pi), None, ALU.mult)
            self.copy(dst, dst[:], posi, posi[:], eng="dve")
            self.stt("dve", tmpa, tmpa[:], dst, dst[:], -HI, src, src[:], ALU.mult, ALU.add)
            self.stt("dve", tmpa, tmpa[:], dst, dst[:], -LO, tmpa, tmpa[:], ALU.mult, ALU.add)
            self.act(dst, dst[:], tmpa, tmpa[:], AF.Sin)
        sin_of(Sg, 0.0)
        self.ts("dve", Sg, Sg[:], Sg, Sg[:], self.cst[0:64, 1:2], None, ALU.mult, extra_reads=[self.cst])
        sin_of(Ct, 0.5 * math.pi)
        S.barrier()
        S.arena = mark
        S.release_from(sidx)
        hT = S.sb("hT", [128, 16, MT], BF16)
        self.norm_transpose(src, t0, MT, hT)
        r1 = S.sb("r1", [64, 512], F32)
        r2 = S.sb("r2", [64, 512], F32)
        wb = [S.sb("wb%d" % i, [128, 16, 512], BF16) for i in range(2)]
        stg = [S.sb("stg%d" % i, [128, MT], BF16) for i in range(3)]
        stgf = [S.sb("stgf%d" % i, [128, MT], F32) for i in range(2)]
        stt = [S.sb("stt%d" % i, [128, 512], BF16) for i in range(3)]
        cq = S.sb("cq", [128, 4, MT], BF16)
        ckv = S.sb("ckv", [128, 4, MT], BF16)
        kr = S.sb("kr", [64, 2, MT], F32)
        glr = S.sb("glr", [16, MT], BF16)
        ci = [0, 0, 0, 0]

        def wload(col0, ncols, nk=16, WT=None):
            WT = WT or self.WIN
            w = wb[ci[0] % 2]
            ci[0] += 1
            self.load(w, w[:, :nk, :ncols],
                      WT, WT[0:nk * 128, col0:col0 + ncols].rearrange("(k p) c -> p k c", p=128))
            return w

        def fm_group(col0, ncols, evac, after=None):
            w = wload(col0, ncols)
            for j in range((ncols + 127) // 128):
                M = min(128, ncols - j * 128)
                for ts_ in range(NT):
                    ps = self.nps()
                    for k in range(16):
                        self.mm(ps, ps[:M, :], w, w[:, k, j * 128:j * 128 + M], hT, hT[:, k, ts_ * 512:(ts_ + 1) * 512],
                                k == 0, k == 15)
                    evac(j, M, ts_, ps)
                if after is not None:
                    after(j, M)

        def to_dram(dstT, row0, func=None, scale=1.0, f32=False):
            cur = {}

            def evac(j, M, ts_, ps):
                if ts_ == 0:
                    if f32:
                        cur["s"] = stgf[ci[2] % 2]
                        ci[2] += 1
                    else:
                        cur["s"] = stg[ci[1] % 3]
                        ci[1] += 1
                s = cur["s"]
                o = s[:M, ts_ * 512:(ts_ + 1) * 512]
                if func is None and scale == 1.0:
                    self.copy(s, o, ps, ps[:M, :])
                elif func is None:
                    self.ts("dve", s, o, ps, ps[:M, :], scale, None, ALU.mult)
                else:
                    self.act(s, o, ps, ps[:M, :], func, scale=scale)

            def after(j, M):
                s = cur["s"]
                self.store(dstT, dstT[row0 + j * 128:row0 + j * 128 + M, t0:t0 + MT], s, s[:M, :])
            return evac, after

        def to_sb(dstT, dst3):
            def evac(j, M, ts_, ps):
                self.copy(dstT, dst3(j, M, ts_), ps, ps[:M, :])
            return evac

        fm_group(0, 512, to_sb(cq, lambda j, M, ts_: cq[:, j, ts_ * 512:(ts_ + 1) * 512]))
        fm_group(512, 512, to_sb(ckv, lambda j, M, ts_: ckv[:, j, ts_ * 512:(ts_ + 1) * 512]))
        fm_group(1024, 64, to_sb(kr, lambda j, M, ts_: kr[:, 0, ts_ * 512:(ts_ + 1) * 512]))
        fm_group(INW, 64, to_sb(kr, lambda j, M, ts_: kr[:, 1, ts_ * 512:(ts_ + 1) * 512]))
        ev, af = to_dram(self.GQ, 0, scale=128.0 ** -0.5)
        fm_group(1088, 512, ev, af)
        ev, af = to_dram(self.GK, 0)
        fm_group(1600, 512, ev, af)
        fm_group(3136, 16, to_sb(glr, lambda j, M, ts_: glr[:, ts_ * 512:(ts_ + 1) * 512]))
        for i in range(4):
            ev, af = to_dram(self.DQKV, i * 512)
            fm_group(4176 + i * 512, 512, ev, af)
        ev, af = to_dram(self.DB, 0, func=AF.Sigmoid, f32=True)
        fm_group(6224, 8, ev, af)
        cur = {}

        def ev_a(j, M, ts_, ps):
            if ts_ == 0:
                cur["s"] = stgf[ci[2] % 2]
                ci[2] += 1
            s = cur["s"]
            o = s[:8, ts_ * 512:(ts_ + 1) * 512]
            self.act(s, o, ps, ps[:8, :], AF.Exp, bias=cv[0:8, CV_DTB:CV_DTB + 1], extra_reads=[cv])
            self.act(s, o, s, o, AF.Ln, bias=1.0)
            self.ts("dve", s, o, s, o, self.ncolv[0:8, 4:5], None, ALU.mult, extra_reads=[self.ncolv])

        def af_a(j, M):
            s = cur["s"]
            self.store(self.DA, self.DA[0:8, t0:t0 + MT], s, s[:8, :])
        fm_group(6232, 8, ev_a, af_a)
        for i in range(12):
            ev, af = to_dram(self.MG, i * 512, func=AF.Sigmoid)
            fm_group(7264 + i * 512, 512, ev, af)

        def tm_group(col0, dstT, dcol0, func):
            w = wload(col0, 512)
            for s in range(NSUB):
                ps = self.nps()
                for k in range(16):
                    self.mm(ps, ps[:, :], hT, hT[:, k, s * 128:(s + 1) * 128], w, w[:, k, :], k == 0, k == 15)
                st = stt[ci[3] % 3]
                ci[3] += 1
                if func is None:
                    self.copy(st, st[:], ps, ps[:, :])
                else:
                    self.act(st, st[:], ps, ps[:, :], func)
                self.store(dstT, dstT[t0 + s * 128:t0 + (s + 1) * 128, dcol0:dcol0 + 512], st, st[:])
        tm_group(2112, self.GV, 0, None)
        tm_group(2624, self.GV, 512, None)
        tm_group(3152, self.GR, 0, AF.Silu)
        tm_group(3664, self.GR, 512, AF.Silu)
        tm_group(6240, self.DZ, 0, AF.Silu)
        tm_group(6752, self.DZ, 512, AF.Silu)

        wg = S.sb("wg", [16, 512], BF16)
        self.load(wg, wg[:], self.WG2, self.WG2[:, :])
        for j in range(4):
            s = stgf[ci[2] % 2]
            ci[2] += 1
            for ts_ in range(NT):
                ps = self.nps()
                self.mm(ps, ps[:, :], wg, wg[:, j * 128:(j + 1) * 128], glr, glr[:, ts_ * 512:(ts_ + 1) * 512], True, True)
                o = s[:, ts_ * 512:(ts_ + 1) * 512]
                self.act(s, o, ps, ps[:, :], AF.Exp, scale=-1.0, bias=self.ncolv[:, j:j + 1], extra_reads=[self.ncolv])
                self.act(s, o, s, o, AF.Ln, bias=1.0)
                self.ts("dve", s, o, s, o, -1.0 / 16.0, None, ALU.mult)
            self.store(self.GG, self.GG[j * 128:(j + 1) * 128, t0:t0 + MT], s, s[:, :])

        def rope(dstT, dap, aT, a_ap, bT, b_ap, sl):
            self.tt("dve", r1, r1[:], aT, a_ap, Ct, Ct[:, sl], ALU.mult)
            self.tt("dve", r2, r2[:], bT, b_ap, Sg, Sg[:, sl], ALU.mult)
            self.tt("dve", dstT, dap, r1, r1[:], r2, r2[:], ALU.add)
        s = stg[ci[1] % 3]
        ci[1] += 1
        for ts_ in range(NT):
            sl = slice(ts_ * 512, (ts_ + 1) * 512)
            rope(s, s[0:64, sl], kr, kr[:, 0, sl], kr, kr[:, 1, sl], sl)
        self.store(self.KR, self.KR[0:64, t0:t0 + MT], s, s[0:64, :])

        sq = S.sb("sq", [128, 4, 512], BF16)
        rstd = S.sb("rstd", [128, 512], F32)
        for lat in (cq, ckv):
            for ts_ in range(NT):
                sl = slice(ts_ * 512, (ts_ + 1) * 512)
                self.tt("pool", sq, sq[:], lat, lat[:, :, sl], lat, lat[:, :, sl], ALU.mult)
                ps = self.nps()
                for k in range(4):
                    self.mm(ps, ps[:, :], self.onesb, self.onesb[:], sq, sq[:, k, :], k == 0, k == 3)
                self.ts("dve", rstd, rstd[:], ps, ps[:, :], 1.0 / 512, EPS, ALU.mult, ALU.add)
                self.act(rstd, rstd[:], rstd, rstd[:], AF.Sqrt)
                S.op("dve", lambda e: e.reciprocal(out=rstd[:], in_=rstd[:]), reads=[rstd], writes=[rstd])
                for k in range(4):
                    self.tt("dve", lat, lat[:, k, sl], lat, lat[:, k, sl], rstd, rstd[:], ALU.mult)

        wq = S.sb("wq", [128, 4, 2048], BF16)
        self.load(wq, wq[:], self.WUQ, self.WUQ[:, :].rearrange("(k p) c -> p k c", p=128))
        for h in range(NH):
            s = stg[ci[1] % 3]
            ci[1] += 1
            s2 = stg[ci[1] % 3]
            ci[1] += 1
            for ts_ in range(NT):
                sl = slice(ts_ * 512, (ts_ + 1) * 512)
                ps = self.nps()
                for k in range(4):
                    self.mm(ps, ps[:, :], wq, wq[:, k, h * 192:h * 192 + 128], cq, cq[:, k, sl], k == 0, k == 3)
                self.copy(s, s[:, sl], ps, ps[:, :])
                pa = self.nps()
                for k in range(4):
                    self.mm(pa, pa[0:64, :], wq, wq[:, k, h * 192 + 128:h * 192 + 192], cq, cq[:, k, sl], k == 0, k == 3)
                pb = self.nps()
                for k in range(4):
                    self.mm(pb, pb[0:64, :], wq, wq[:, k, 1536 + h * 64:1536 + (h + 1) * 64], cq, cq[:, k, sl], k == 0, k == 3)
                rope(s2, s2[0:64, sl], pa, pa[0:64, :], pb, pb[0:64, :], sl)
            self.store(self.QN, self.QN[h * 128:(h + 1) * 128, t0:t0 + MT], s, s[:, :])
            self.store(self.QR, self.QR[h * 64:(h + 1) * 64, t0:t0 + MT], s2, s2[0:64, :])

        wkv = S.sb("wkv", [128, 4, 2048], BF16)
        self.load(wkv, wkv[:], self.WUKV, self.WUKV[:, :].rearrange("(k p) c -> p k c", p=128))
        for h in range(NH):
            s = stg[ci[1] % 3]
            ci[1] += 1
            for ts_ in range(NT):
                sl = slice(ts_ * 512, (ts_ + 1) * 512)
                ps = self.nps()
                for k in range(4):
                    self.mm(ps, ps[:, :], wkv, wkv[:, k, h * 256:h * 256 + 128], ckv, ckv[:, k, sl], k == 0, k == 3)
                self.copy(s, s[:, sl], ps, ps[:, :])
            self.store(self.KN, self.KN[h * 128:(h + 1) * 128, t0:t0 + MT], s, s[:, :])
        for s_ in range(NSUB):
            for hh in range(2):
                ps = self.nps()
                for k in range(4):
                    rhs = wkv[:, k, :].rearrange("p (h c) -> p h c", c=256)[:, hh * 4:(hh + 1) * 4, 128:256]
                    self.mm(ps, ps[:, :].rearrange("p (h c) -> p h c", c=128), ckv, ckv[:, k, s_ * 128:(s_ + 1) * 128],
                            wkv, rhs, k == 0, k == 3)
                st = stt[ci[3] % 3]
                ci[3] += 1
                self.copy(st, st[:], ps, ps[:, :])
                self.store(self.V, self.V[t0 + s_ * 128:t0 + (s_ + 1) * 128, hh * 512:(hh + 1) * 512], st, st[:])
        self.end_stage()

    def stage_B_mla(self):
        S = self.S
        SEQ = self.SEQ
        NKB = SEQ // 128
        NQT = SEQ // 512
        scale = 192.0 ** -0.5
        masks = S.sb("masks", [128, 4, 512], BF16)
        self.load(masks, masks[:], self.masks_in, self.masks_in[:, :].rearrange("p (j q) -> p j q", j=4))
        krT = S.sb("krT", [64, SEQ], BF16)
        self.load(krT, krT[:], self.KR, self.KR[:, :])
        knT = S.sb("knT", [128, SEQ], BF16)
        vT = S.sb("vT", [128, NKB, 128], BF16)
        qn = [S.sb("qn%d" % i, [128, 512], BF16) for i in range(2)]
        qr = [S.sb("qr%d" % i, [64, 512], BF16) for i in range(2)]
        pT = [S.sb("pT%d" % i, [128, 512], BF16) for i in range(3)]
        rec = S.sb("rec", [128, 512], F32)
        yst = [S.sb("yst%d" % i, [128, 512], BF16) for i in range(2)]
        po, pd = self.psf[4], self.psf[5]
        it = 0
        pi = 0
        for h in range(NH):
            self.load(knT, knT[:], self.KN, self.KN[h * 128:(h + 1) * 128, :])
            for kb0 in range(0, NKB, 16):
                nb = min(16, NKB - kb0)
                self.load(vT, vT[:, kb0:kb0 + nb, :], self.V,
                          self.V[kb0 * 128:(kb0 + nb) * 128, h * 128:(h + 1) * 128].rearrange("(kb p) c -> p kb c", p=128))
            for qt in range(NQT):
                a, b = qn[it % 2], qr[it % 2]
                self.load(a, a[:], self.QN, self.QN[h * 128:(h + 1) * 128, qt * 512:(qt + 1) * 512])
                self.load(b, b[:], self.QR, self.QR[h * 64:(h + 1) * 64, qt * 512:(qt + 1) * 512])
                nkb = 4 * qt + 4
                for kb in range(nkb):
                    ps = self.nps()
                    ksl = slice(kb * 128, (kb + 1) * 128)
                    self.mm(ps, ps[:, :], knT, knT[:, ksl], a, a[:], True, False)
                    self.mm(ps, ps[:, :], krT, krT[:, ksl], b, b[:], False, True)
                    p = pT[pi % 3]
                    pi += 1
                    self.act(p, p[:], ps, ps[:, :], AF.Exp, scale=scale)
                    if kb >= 4 * qt:
                        j = kb - 4 * qt
                        self.tt("pool", p, p[:], p, p[:], masks, masks[:, j, :], ALU.mult)
                    self.mm(po, po[:, :], vT, vT[:, kb, :], p, p[:], kb == 0, kb == nkb - 1)
                    self.mm(pd, pd[:, :], self.onesb, self.onesb[:], p, p[:], kb == 0, kb == nkb - 1)
                S.op("dve", lambda e: e.reciprocal(out=rec[:], in_=pd[:, :]), reads=[pd], writes=[rec])
                y = yst[it % 2]
                self.tt("dve", y, y[:], po, po[:, :], rec, rec[:], ALU.mult)
                self.store(self.Y[0], self.Y[0][h * 128:(h + 1) * 128, qt * 512:(qt + 1) * 512], y, y[:])
                it += 1
        self.end_stage()

    def mmt(self, psT, pap, lT, lap, rT, rap, start, stop, tp):
        self.S.op("pe", lambda e: e.matmul(pap, lap, rap, start=start, stop=stop, tile_position=tp),
                  reads=[lT, rT], writes=[psT])

    def stage_B_gla(self):
        S = self.S
        SEQ = self.SEQ
        TM = min(1024, SEQ)
        NTL = TM // 128
        NCH = TM // 64
        cm = S.sb("cm", [128, 6, 128], F32)
        self.load(cm, cm[:], self.cmask_in, self.cmask_in[:, :].rearrange("p (a b) -> p a b", a=6))
        St = S.sb("St", [128, 256], F32)
        Sb = S.sb("Sb", [128, 256], BF16)
        qT = S.sb("qT", [128, TM], BF16)
        kT = S.sb("kT", [128, TM], BF16)
        gk = S.sb("gk", [128, TM], F32)
        bb = S.sb("bb", [128, TM], F32)
        eb = S.sb("eb", [128, TM], F32)
        enb = S.sb("enb", [128, TM], F32)
        qd = S.sb("qd", [128, TM], BF16)
        kdn = S.sb("kdn", [128, TM], BF16)
        kdl = S.sb("kdl", [128, TM], BF16)
        kdl_tm = S.sb("kdl_tm", [128, NTL, 128], BF16)
        v_tm = S.sb("v_tm", [128, NTL, 256], BF16)
        r_tm = S.sb("r_tm", [128, NTL, 256], BF16)
        o_sb = S.sb("o_sb", [128, NTL, 256], F32)
        y_tm = S.sb("y_tm", [128, NTL, 256], BF16)
        yfm = [S.sb("yfm%d" % i, [128, TM], BF16) for i in range(2)]
        ssq = S.sb("ssq", [128, 2 * NTL], F32)
        junk = S.sb("junkg", [128, 256], BF16)
        AT = [S.sb("AT%d" % i, [128, 128], BF16) for i in range(2)]
        po = [self.psf[4], self.psf[5]]
        for h in range(4):
            S.op("pool", lambda e: e.memset(St[:], 0.0), writes=[St])
            S.op("pool", lambda e: e.memset(Sb[:], 0.0), writes=[Sb])
            for m in range(SEQ // TM):
                t0 = m * TM
                rows = slice(h * 128, (h + 1) * 128)
                self.load(qT, qT[:], self.GQ, self.GQ[rows, t0:t0 + TM])
                self.load(kT, kT[:], self.GK, self.GK[rows, t0:t0 + TM])
                self.load(gk, gk[:], self.GG, self.GG[rows, t0:t0 + TM])
                self.load(v_tm, v_tm[:], self.GV,
                          self.GV[t0:t0 + TM, h * 256:(h + 1) * 256].rearrange("(n p) c -> p n c", p=128))
                self.load(r_tm, r_tm[:], self.GR,
                          self.GR[t0:t0 + TM, h * 256:(h + 1) * 256].rearrange("(n p) c -> p n c", p=128))
                for c in range(NCH):
                    sl = slice(c * 64, (c + 1) * 64)
                    S.op("dve", lambda e, sl=sl: e.tensor_tensor_scan(out=bb[:, sl], data0=gk[:, sl], data1=gk[:, sl],
                                                                      initial=0.0, op0=ALU.add, op1=ALU.bypass),
                         reads=[gk], writes=[bb])
                self.act(eb, eb[:], bb, bb[:], AF.Exp)
                self.act(enb, enb[:], bb, bb[:], AF.Exp, scale=-1.0)
                self.tt("dve", qd, qd[:], qT, qT[:], eb, eb[:], ALU.mult)
                self.tt("pool", kdn, kdn[:], kT, kT[:], enb, enb[:], ALU.mult)
                for c in range(NCH):
                    sl = slice(c * 64, (c + 1) * 64)
                    self.ts("pool" if c % 2 else "dve", kdl, kdl[:, sl], kdn, kdn[:, sl],
                            eb[:, c * 64 + 63:c * 64 + 64], None, ALU.mult, extra_reads=[eb])
                pb = self.psb[0]
                for n in range(NTL):
                    self.tr(pb, pb[:, n * 128:(n + 1) * 128], kdl, kdl[:, n * 128:(n + 1) * 128], self.identb[:])
                self.copy(kdl_tm, kdl_tm[:], pb, pb[:, :].rearrange("p (n c) -> p n c", n=NTL))
                for n in range(NTL):
                    tsl = slice(n * 128, (n + 1) * 128)
                    pa = self.nps()
                    self.mm(pa, pa[:, 0:128], kdn, kdn[:, tsl], qd, qd[:, tsl], True, True)
                    at = AT[n % 2]
                    self.tt("dve", at, at[:], pa, pa[:, 0:128], cm, cm[:, 0, :], ALU.mult)
                    p_o = po[n % 2]
                    for c in range(2):
                        r0 = 64 * c
                        csl = slice(n * 128 + r0, n * 128 + r0 + 64)
                        self.mmt(p_o, p_o[r0:r0 + 64, 0:256], qd, qd[:, csl], Sb, Sb[:], True, False, (0, r0))
                        self.mmt(p_o, p_o[r0:r0 + 64, 0:256], at, at[r0:r0 + 64, r0:r0 + 64],
                                 v_tm, v_tm[r0:r0 + 64, n, :], False, True, (r0, r0))
                        pst = self.nps()
                        self.mm(pst, pst[:, 0:256], kdl_tm, kdl_tm[r0:r0 + 64, n, :], v_tm, v_tm[r0:r0 + 64, n, :], True, True)
                        ecol = n * 128 + r0 + 63
                        self.stt("dve", St, St[:], St, St[:], eb[:, ecol:ecol + 1], pst, pst[:, 0:256],
                                 ALU.mult, ALU.add, extra_reads=[eb])
                        self.copy(Sb, Sb[:], St, St[:], eng="act")
                    S.op("act", lambda e, p_o=p_o, n=n: e.activation(out=junk[:], in_=p_o[:, 0:256], func=AF.Square,
                                                                     accum_out=ssq[:, n:n + 1]),
                         reads=[p_o], writes=[junk, ssq])
                    self.copy(o_sb, o_sb[:, n, :], p_o, p_o[:, 0:256], eng="dve")
                self.ts("dve", ssq, ssq[:, NTL:2 * NTL], ssq, ssq[:, 0:NTL], 1.0 / 256, EPS, ALU.mult, ALU.add)
                self.act(ssq, ssq[:, NTL:2 * NTL], ssq, ssq[:, NTL:2 * NTL], AF.Sqrt)
                S.op("dve", lambda e: e.reciprocal(out=ssq[:, NTL:2 * NTL], in_=ssq[:, NTL:2 * NTL]), reads=[ssq], writes=[ssq])
                for n in range(NTL):
                    self.stt("dve", y_tm, y_tm[:, n, :], o_sb, o_sb[:, n, :],
                             ssq[:, NTL + n:NTL + n + 1], r_tm, r_tm[:, n, :], ALU.mult, ALU.mult, extra_reads=[ssq])
                for half in range(2):
                    pb = self.psb[1 - half % 2] if False else self.psb[half]
                    for n in range(NTL):
                        self.tr(pb, pb[:, n * 128:(n + 1) * 128], y_tm, y_tm[:, n, half * 128:(half + 1) * 128], self.identb[:])
                    yf = yfm[half]
                    self.copy(yf, yf[:], pb, pb[:, 0:TM])
                    r_ = h * 256 + half * 128
                    self.store(self.Y[1], self.Y[1][r_:r_ + 128, t0:t0 + TM], yf, yf[:])
        self.end_stage()

    def stage_B_gdn(self):
        S = self.S
        SEQ = self.SEQ
        TM = min(1024, SEQ)
        NTL = TM // 128
        cv = self.colv
        cm = S.sb("cm", [128, 6, 128], F32)
        self.load(cm, cm[:], self.cmask_in, self.cmask_in[:, :].rearrange("p (a b) -> p a b", a=6))
        St = [S.sb("St%d" % i, [128, 128], F32) for i in range(8)]
        Sb = [S.sb("Sb%d" % i, [128, 128], BF16) for i in range(8)]
        for i in range(8):
            S.op("pool", lambda e, i=i: e.memset(St[i][:], 0.0), writes=[St[i]])
            S.op("pool", lambda e, i=i: e.memset(Sb[i][:], 0.0), writes=[Sb[i]])
        xt = [S.sb("xt%d" % i, [128, TM + 4], BF16) for i in range(2)]
        acc = S.sb("acc", [128, TM], F32)
        ysil = S.sb("ysil", [128, TM], F32)
        sq = S.sb("sqd", [128, TM], BF16)
        rn = S.sb("rn", [128, 512], F32)
        qnT = S.sb("qnT", [128, TM], BF16)
        knT = S.sb("knT", [128, TM], BF16)
        vT = S.sb("vTd", [128, TM], BF16)
        k_tm = S.sb("k_tm", [128, NTL, 128], BF16)
        v_tm = [S.sb("v_tm%d" % i, [128, NTL, 128], BF16) for i in range(2)]
        z_tm = [S.sb("z_tm%d" % i, [128, NTL, 128], BF16) for i in range(2)]
        o_sb = [S.sb("o_sb%d" % i, [128, NTL, 128], F32) for i in range(2)]
        ssq = [S.sb("ssq%d" % i, [128, 2 * NTL], F32) for i in range(2)]
        y_tm = S.sb("y_tmd", [128, NTL, 128], BF16)
        yfm = [S.sb("yfmd%d" % i, [128, TM], BF16) for i in range(2)]
        bgr = S.sb("bgr", [8, 2, TM], F32)
        bg_tm = S.sb("bg_tm", [128, NTL, 16], F32)
        Gc = S.sb("Gc", [128, NTL, 16], F32)
        dG = S.sb("dG", [128, NTL, 8], F32)
        eGc = S.sb("eGc", [128, NTL, 8], F32)
        KKs = S.sb("KKs", [128, 128], F32)
        QKs = S.sb("QKs", [128, 128], F32)
        junk = S.sb("junkd", [128, 128], BF16)

        def f32t(name):
            return [S.sb("%s%d" % (name, i), [128, 128], F32) for i in range(2)]

        def b16t(name):
            return [S.sb("%s%d" % (name, i), [128, 128], BF16) for i in range(2)]
        Dm, gamT, eG, gs, gm, Xa, XTa, Pm, Ya, YTa, u0 = [f32t(n_) for n_ in
                                                         ("Dm", "gamT", "eG", "gs", "gm", "Xa", "XTa", "Pm", "Ya", "YTa", "u0")]
        qkT, Pb, RV2, w_tm, wT, qdT, kdec, u_tm = [b16t(n_) for n_ in ("qkT", "Pb", "RVx", "w_tm", "wT", "qdT", "kdec", "u_tm")]
        RV = [S.sb("RV%d" % i, [128, 256], BF16) for i in range(2)]
        po = [self.psf[4], self.psf[5]]
        itc = [0]

        def conv_silu(kc, t0, first):
            x = xt[itc[0] % 2]
            itc[0] += 1
            rows = slice(kc * 128, (kc + 1) * 128)
            if first:
                S.op("pool", lambda e: e.memset(x[:, 0:4], 0.0), writes=[x])
                self.load(x, x[:, 4:4 + TM], self.DQKV, self.DQKV[rows, t0:t0 + TM])
            else:
                self.load(x, x[:, 0:4 + TM], self.DQKV, self.DQKV[rows, t0 - 4:t0 + TM])

            def w(tap):
                c_ = CV_DCW + tap * 16 + kc
                return cv[:, c_:c_ + 1]
            self.ts("dve", acc, acc[:], x, x[:, 4:4 + TM], w(3), None, ALU.mult, extra_reads=[cv])
            for tap in (2, 1, 0):
                self.stt("dve", acc, acc[:], x, x[:, 1 + tap:1 + tap + TM], w(tap), acc, acc[:], ALU.mult, ALU.add,
                         extra_reads=[cv])
            self.act(ysil, ysil[:], acc, acc[:], AF.Silu)

        def l2n(dst, qscale):
            self.tt("pool", sq, sq[:], ysil, ysil[:], ysil, ysil[:], ALU.mult)
            for b_ in range(TM // 512):
                sl = slice(b_ * 512, (b_ + 1) * 512)
                ps = self.nps()
                self.mm(ps, ps[:, :], self.onesb, self.onesb[:], sq, sq[:, sl], True, True)
                self.ts("dve", rn, rn[:], ps, ps[:, :], EPS, None, ALU.add)
                self.act(rn, rn[:], rn, rn[:], AF.Sqrt)
                S.op("dve", lambda e: e.reciprocal(out=rn[:], in_=rn[:]), reads=[rn], writes=[rn])
                self.stt("dve", dst, dst[:, sl], ysil, ysil[:, sl], qscale, rn, rn[:], ALU.mult, ALU.mult)

        def to_tm(dst, src):
            pb = self.psb[itc[0] % 2]
            itc[0] += 1
            for n in range(NTL):
                self.tr(pb, pb[:, n * 128:(n + 1) * 128], src, src[:, n * 128:(n + 1) * 128], self.identb[:])
            self.copy(dst, dst[:], pb, pb[:, 0:TM].rearrange("p (n c) -> p n c", n=NTL))

        for m in range(SEQ // TM):
            t0 = m * TM
            first = (m == 0)
            self.load(bgr, bgr[:, 0, :], self.DB, self.DB[0:8, t0:t0 + TM])
            self.load(bgr, bgr[:, 1, :], self.DA, self.DA[0:8, t0:t0 + TM])
            ps = self.nps()
            for n in range(NTL):
                for a_ in range(2):
                    self.tr(ps, ps[:, n * 16 + a_ * 8:n * 16 + a_ * 8 + 8], bgr, bgr[:, a_, n * 128:(n + 1) * 128],
                            self.identf[0:8, 0:8])
            self.copy(bg_tm, bg_tm[:], ps, ps[:, 0:NTL * 16].rearrange("p (n c) -> p n c", n=NTL), eng="dve")
            ps = self.nps()
            for n in range(NTL):
                self.mm(ps, ps[:, n * 16:n * 16 + 8], cm, cm[:, 0, :], bg_tm, bg_tm[:, n, 8:16], True, True)
                self.mm(ps, ps[:, n * 16 + 8:n * 16 + 16], cm, cm[:, 2, :], bg_tm, bg_tm[:, n, 8:16], True, True)
            self.copy(Gc, Gc[:], ps, ps[:, 0:NTL * 16].rearrange("p (n c) -> p n c", n=NTL), eng="dve")
            self.tt("dve", dG, dG[:], Gc, Gc[:, :, 8:16], Gc, Gc[:, :, 0:8], ALU.subtract)
            self.act(dG, dG[:], dG, dG[:], AF.Exp)
            self.act(eGc, eGc[:], Gc, Gc[:, :, 0:8], AF.Exp)
            import os
            STOP = int(os.environ.get("GDN_STOP", "99"))
            if STOP <= 1:
                continue
            for hq in range(4):
                conv_silu(hq, t0, first)
                l2n(qnT, 128.0 ** -0.5)
                conv_silu(4 + hq, t0, first)
                l2n(knT, 1.0)
                to_tm(k_tm, knT)
                for a_ in range(2):
                    hv = 2 * hq + a_
                    conv_silu(8 + hv, t0, first)
                    self.copy(vT, vT[:], ysil, ysil[:], eng="pool")
                    to_tm(v_tm[a_], vT)
                    self.load(z_tm[a_], z_tm[a_][:], self.DZ,
                              self.DZ[t0:t0 + TM, hv * 128:(hv + 1) * 128].rearrange("(n p) c -> p n c", p=128))
                if STOP <= 2:
                    continue
                for n in range(NTL):
                    tsl = slice(n * 128, (n + 1) * 128)
                    ps = self.nps()
                    self.mm(ps, ps[:, 0:128], knT, knT[:, tsl], knT, knT[:, tsl], True, True)
                    self.copy(KKs, KKs[:], ps, ps[:, 0:128], eng="act")
                    ps = self.nps()
                    self.mm(ps, ps[:, 0:128], knT, knT[:, tsl], qnT, qnT[:, tsl], True, True)
                    self.copy(QKs, QKs[:], ps, ps[:, 0:128], eng="act")
                    for a_ in range(2):
                        hv = 2 * hq + a_
                        bcol = bg_tm[:, n, hv:hv + 1]
                        gcol = bg_tm[:, n, 8 + hv:9 + hv]
                        pg = self.nps()
                        self.mm(pg, pg[:, 0:128], bg_tm, gcol.to_broadcast([128, 128]), cm, cm[:, 0, :], True, True)
                        self.ts("dve", Dm[a_], Dm[a_][:], pg, pg[:, 0:128], Gc[:, n, hv:hv + 1], 0.0,
                                ALU.subtract, ALU.min, extra_reads=[Gc])
                        self.act(gamT[a_], gamT[a_][:], Dm[a_], Dm[a_][:], AF.Exp)
                        self.act(eG[a_], eG[a_][:], pg, pg[:, 0:128], AF.Exp)
                        self.tt("pool", gs[a_], gs[a_][:], gamT[a_], gamT[a_][:], cm, cm[:, 1, :], ALU.mult)
                        self.tt("pool", gm[a_], gm[a_][:], gamT[a_], gamT[a_][:], cm, cm[:, 0, :], ALU.mult)
                        X = Xa[a_]
                        self.stt("dve", X, X[:], KKs, KKs[:], bcol, gs[a_], gs[a_][:], ALU.mult, ALU.mult,
                                 extra_reads=[bg_tm])
                        self.tt("dve", qkT[a_], qkT[a_][:], QKs, QKs[:], gm[a_], gm[a_][:], ALU.mult)
                        px = self.nps()
                        self.tr(px, px[:, 0:128], X, X[:], self.identf[:])
                        XT = XTa[a_]
                        self.copy(XT, XT[:], px, px[:, 0:128], eng="act")
                        if STOP <= 3:
                            continue
                        P_ = Pm[a_]
                        self.tt("pool", P_, P_[:], cm, cm[:, 4, :], X, X[:], ALU.subtract)
                        Y, YT = X, XT
                        for k in range(1, 6):
                            pyt = self.nps()
                            self.mm(pyt, pyt[:, 0:128], Y, Y[:], YT, YT[:], True, True)
                            if k < 5:
                                py = self.nps()
                                self.mm(py, py[:, 0:128], YT, YT[:], Y, Y[:], True, True)
                            YTn = YTa[a_] if YT is XT else XT
                            Yn = Ya[a_] if Y is X else X
                            self.copy(YTn, YTn[:], pyt, pyt[:, 0:128], eng="act")
                            if k < 5:
                                self.copy(Yn, Yn[:], py, py[:, 0:128], eng="dve")
                            pp = self.nps()
                            self.mm(pp, pp[:, 0:128], YTn, YTn[:], P_, P_[:], True, True)
                            if k < 5:
                                self.tt("dve", P_, P_[:], P_, P_[:], pp, pp[:, 0:128], ALU.add)
                            else:
                                self.tt("dve", Pb[a_], Pb[a_][:], P_, P_[:], pp, pp[:, 0:128], ALU.add)
                            Y, YT = Yn, YTn
                        if STOP <= 4:
                            continue
                        rv = RV[a_]
                        self.ts("pool", rv, rv[:, 0:128], k_tm, k_tm[:, n, :], eGc[:, n, hv:hv + 1], None, ALU.mult,
                                extra_reads=[eGc])
                        self.copy(rv, rv[:, 128:256], v_tm[a_], v_tm[a_][:, n, :], eng="pool")
                        pw = self.nps()
                        self.mm(pw, pw[:, 0:256], Pb[a_], Pb[a_][:], rv, rv[:], True, True)
                        self.ts("dve", w_tm[a_], w_tm[a_][:], pw, pw[:, 0:128], bcol, None, ALU.mult, extra_reads=[bg_tm])
                        self.ts("dve", u0[a_], u0[a_][:], pw, pw[:, 128:256], bcol, None, ALU.mult, extra_reads=[bg_tm])
                        pb = self.psb[a_]
                        self.tr(pb, pb[:, 0:128], w_tm[a_], w_tm[a_][:], self.identb[:])
                        self.copy(wT[a_], wT[a_][:], pb, pb[:, 0:128], eng="act")
                        self.tt("pool", qdT[a_], qdT[a_][:], qnT, qnT[:, tsl], eG[a_], eG[a_][:], ALU.mult)
                        self.ts("pool", kdec[a_], kdec[a_][:], k_tm, k_tm[:, n, :], dG[:, n, hv:hv + 1], None, ALU.mult,
                                extra_reads=[dG])
                    if STOP <= 5:
                        continue
                    for c in range(2):
                        r0 = 64 * c
                        rs = slice(r0, r0 + 64)
                        for a_ in range(2):
                            hv = 2 * hq + a_
                            SUB = int(os.environ.get("GDN_SUB", "0"))
                            pu = self.nps()
                            self.mm(pu, pu[:, 0:128], wT[a_], wT[a_][:], Sb[hv], Sb[hv][:], True, True)
                            if not SUB & 16:
                                self.tt("dve", u_tm[a_], u_tm[a_][rs, :], u0[a_], u0[a_][rs, :], pu, pu[rs, 0:128], ALU.subtract)
                            p_o = po[a_]
                            if not SUB & 1:
                                self.mm(p_o, p_o[:, 0:128], qdT[a_], qdT[a_][:], Sb[hv], Sb[hv][:], True, False)
                                self.mm(p_o, p_o[:, 0:128], qkT[a_], qkT[a_][rs, :], u_tm[a_], u_tm[a_][rs, :], False, True)
                            else:
                                self.mm(p_o, p_o[:, 0:128], qdT[a_], qdT[a_][:], Sb[hv], Sb[hv][:], True, True)
                            if not SUB & 4:
                                self.copy(o_sb[a_], o_sb[a_][rs, n, :], p_o, p_o[rs, 0:128], eng="act")
                            if not SUB & 2:
                                S.op("act", lambda e, p_o=p_o, a_=a_, rs=rs, n=n: e.activation(
                                    out=junk[rs, :], in_=p_o[rs, 0:128], func=AF.Square, accum_out=ssq[a_][rs, n:n + 1]),
                                    reads=[p_o], writes=[junk, ssq[a_]])
                            if not SUB & 8:
                                pss = self.nps()
                                self.mm(pss, pss[:, 0:128], kdec[a_], kdec[a_][rs, :], u_tm[a_], u_tm[a_][rs, :], True, True)
                                self.stt("dve", St[hv], St[hv][:], St[hv], St[hv][:], eG[a_][:, r0 + 63:r0 + 64],
                                         pss, pss[:, 0:128], ALU.mult, ALU.add, extra_reads=[eG[a_]])
                                self.copy(Sb[hv], Sb[hv][:], St[hv], St[hv][:], eng="act")
                if STOP <= 6:
                    continue
                for a_ in range(2):
                    hv = 2 * hq + a_
                    sq_ = ssq[a_]
                    self.ts("dve", sq_, sq_[:, NTL:2 * NTL], sq_, sq_[:, 0:NTL], 1.0 / 128, EPS, ALU.mult, ALU.add)
                    self.act(sq_, sq_[:, NTL:2 * NTL], sq_, sq_[:, NTL:2 * NTL], AF.Sqrt)
                    S.op("dve", lambda e, sq_=sq_: e.reciprocal(out=sq_[:, NTL:2 * NTL], in_=sq_[:, NTL:2 * NTL]),
                         reads=[sq_], writes=[sq_])
                    for n in range(NTL):
                        self.stt("dve", y_tm, y_tm[:, n, :], o_sb[a_], o_sb[a_][:, n, :], sq_[:, NTL + n:NTL + n + 1],
                                 z_tm[a_], z_tm[a_][:, n, :], ALU.mult, ALU.mult, extra_reads=[sq_])
                    pb = self.psb[a_]
                    for n in range(NTL):
                        self.tr(pb, pb[:, n * 128:(n + 1) * 128], y_tm, y_tm[:, n, :], self.identb[:])
                    yf = yfm[a_]
                    self.copy(yf, yf[:], pb, pb[:, 0:TM])
                    self.store(self.Y[2], self.Y[2][hv * 128:(hv + 1) * 128, t0:t0 + TM], yf, yf[:])
        self.end_stage()

    def stage_C(self, l):
        S = self.S
        SEQ = self.SEQ
        TC = 512
        src = self.x if l == 0 else self.X
        wbr = [S.sb("wbr%d" % i, [128, 8, D], BF16) for i in range(3)]
        for b in range(3):
            self.load(wbr[b], wbr[b][:], self.WBR[b], self.WBR[b][:, :].rearrange("(k p) c -> p k c", p=128))
        yb = [S.sb("yb%d" % i, [128, 8, TC], BF16) for i in range(3)]
        mixed = S.sb("mixed", [128, 16, TC], BF16)
        gt = [S.sb("gt%d" % i, [128, 3, TC], BF16) for i in range(2)]
        t1 = S.sb("t1", [128, TC], F32)
        t2 = S.sb("t2", [128, TC], F32)
        wo = [S.sb("wo%d" % i, [128, 16, 512], BF16) for i in range(2)]
        xs = [S.sb("xs%d" % i, [128, D], F32) for i in range(2)]
        pbk = [self.psf[0], self.psf[1], self.psf[2]]
        wi = 0
        xi = 0
        for m in range(SEQ // TC):
            t0 = m * TC
            for b in range(3):
                self.load(yb[b], yb[b][:], self.Y[b], self.Y[b][:, t0:t0 + TC].rearrange("(k p) t -> p k t", p=128))
            for j in range(16):
                g = gt[j % 2]
                self.load(g, g[:], self.MG,
                          self.MG[:, t0:t0 + TC].rearrange("(b j p) t -> p b j t", b=3, p=128)[:, :, j, :])
                for b in range(3):
                    ps = pbk[b]
                    for k in range(8):
                        self.mm(ps, ps[:, :], wbr[b], wbr[b][:, k, j * 128:(j + 1) * 128], yb[b], yb[b][:, k, :], k == 0, k == 7)
                self.tt("dve", t1, t1[:], pbk[0], pbk[0][:, :], g, g[:, 0, :], ALU.mult)
                self.tt("dve", t2, t2[:], pbk[1], pbk[1][:, :], g, g[:, 1, :], ALU.mult)
                self.tt("pool", t1, t1[:], t1, t1[:], t2, t2[:], ALU.add)
                self.tt("dve", t2, t2[:], pbk[2], pbk[2][:, :], g, g[:, 2, :], ALU.mult)
                self.tt("pool", mixed, mixed[:, j, :], t1, t1[:], t2, t2[:], ALU.add)
            xt = []
            for s_ in range(4):
                x = xs[s_ % 2]
                xt.append(x)
            for pair in range(2):
                for s2 in range(2):
                    s_ = pair * 2 + s2
                    x = xs[s2]
                    self.load(x, x[:], src, src[t0 + s_ * 128:t0 + (s_ + 1) * 128, :])
                for cb in range(4):
                    w = wo[wi % 2]
                    wi += 1
                    self.load(w, w[:], self.WO, self.WO[:, cb * 512:(cb + 1) * 512].rearrange("(k p) c -> p k c", p=128))
                    for s2 in range(2):
                        s_ = pair * 2 + s2
                        x = xs[s2]
                        ps = self.psf[3 + s2]
                        for k in range(16):
                            self.mm(ps, ps[:, :], mixed, mixed[:, k, s_ * 128:(s_ + 1) * 128], w, w[:, k, :], k == 0, k == 15)
                        csl = slice(cb * 512, (cb + 1) * 512)
                        self.tt("dve", x, x[:, csl], x, x[:, csl], ps, ps[:, :], ALU.add)
                for s2 in range(2):
                    s_ = pair * 2 + s2
                    x = xs[s2]
                    self.store(self.X, self.X[t0 + s_ * 128:t0 + (s_ + 1) * 128, :], x, x[:])
        self.end_stage()

    def stage_D(self, l, last):
        S = self.S
        SEQ = self.SEQ
        TD = 512
        cv = self.colv
        NJ = DFF // 128
        halo = S.sb("halo", [128, NJ, 2], F32)
        S.op("pool", lambda e: e.memset(halo[:], 0.0), writes=[halo])
        rowv = S.sb("rowv", [128, D], F32)
        if last:
            self.load(rowv, rowv[:], self.rowv_in, self.rowv_in[l, :, :])
            if self.per_layer:
                self.ts("dve", rowv, rowv[:], rowv, rowv[:], self.cst[:, 3:4], self.cst[:, 4:5], ALU.mult, ALU.add,
                        extra_reads=[self.cst])
        fixed = S.arena
        fidx = len(S.stage_sb)
        for m in range(SEQ // TD):
            t0 = m * TD
            S.arena = fixed
            S.release_from(fidx)
            hT = S.sb("hTd", [128, 16, TD], BF16)
            self.norm_transpose(self.X, t0, TD, hT)
            wg = [S.sb("wgd%d" % i, [128, 16, 256], BF16) for i in range(2)]
            wu = [S.sb("wud%d" % i, [128, 16, 256], BF16) for i in range(2)]
            actT = S.sb("actT", [128, NJ, TD], BF16)
            gsb = [S.sb("gsb%d" % i, [128, TD + 2], F32) for i in range(2)]
            cc = [S.sb("cc%d" % i, [128, TD], F32) for i in range(2)]
            wdn = [S.sb("wdn%d" % i, [128, NJ, 256], BF16) for i in range(2)]
            xs = [S.sb("xsd%d" % i, [128, D], F32) for i in range(4)]
            junk = S.sb("junkf", [128, D], BF16)
            ssf = S.sb("ssf", [128, 4], F32)
            for blk in range(DFF // 256):
                a, b = wg[blk % 2], wu[blk % 2]
                self.load(a, a[:], self.WUP, self.WUP[:, blk * 256:(blk + 1) * 256].rearrange("(k p) c -> p k c", p=128))
                self.load(b, b[:], self.WUP,
                          self.WUP[:, DFF + blk * 256:DFF + (blk + 1) * 256].rearrange("(k p) c -> p k c", p=128))
                for jj in range(2):
                    j = blk * 2 + jj
                    pg = self.psf[jj]
                    pu = self.psf[2 + jj]
                    for k in range(16):
                        self.mm(pg, pg[:, :], a, a[:, k, jj * 128:(jj + 1) * 128], hT, hT[:, k, :], k == 0, k == 15)
                    for k in range(16):
                        self.mm(pu, pu[:, :], b, b[:, k, jj * 128:(jj + 1) * 128], hT, hT[:, k, :], k == 0, k == 15)
                    g = gsb[j % 2]
                    c = cc[j % 2]
                    self.copy(g, g[:, 0:2], halo, halo[:, j, :], eng="pool")
                    self.copy(g, g[:, 2:TD + 2], pg, pg[:, :], eng="act")
                    self.copy(halo, halo[:, j, :], g, g[:, TD:TD + 2], eng="pool")

                    def w(tap):
                        c_ = CV_FCW + tap * NJ + j
                        return cv[:, c_:c_ + 1]
                    self.ts("dve", c, c[:], g, g[:, 2:TD + 2], w(2), cv[:, CV_FCB + j:CV_FCB + j + 1], ALU.mult, ALU.add,
                            extra_reads=[cv])
                    self.stt("dve", c, c[:], g, g[:, 1:TD + 1], w(1), c, c[:], ALU.mult, ALU.add, extra_reads=[cv])
                    self.stt("dve", c, c[:], g, g[:, 0:TD], w(0), c, c[:], ALU.mult, ALU.add, extra_reads=[cv])
                    self.act(c, c[:], c, c[:], AF.Silu)
                    self.tt("dve", actT, actT[:, j, :], c, c[:], pu, pu[:, :], ALU.mult)
            for s_ in range(4):
                self.load(xs[s_], xs[s_][:], self.X, self.X[t0 + s_ * 128:t0 + (s_ + 1) * 128, :])
            for cb in range(8):
                w_ = wdn[cb % 2]
                self.load(w_, w_[:], self.WDN, self.WDN[:, cb * 256:(cb + 1) * 256].rearrange("(k p) c -> p k c", p=128))
                for s_ in range(4):
                    ps = self.psf[s_ % 4]
                    for k in range(NJ):
                        self.mm(ps, ps[:, 0:256], actT, actT[:, k, s_ * 128:(s_ + 1) * 128], w_, w_[:, k, :], k == 0, k == NJ - 1)
                    csl = slice(cb * 256, (cb + 1) * 256)
                    self.tt("dve", xs[s_], xs[s_][:, csl], xs[s_], xs[s_][:, csl], ps, ps[:, 0:256], ALU.add)
            for s_ in range(4):
                x = xs[s_]
                rows = slice(t0 + s_ * 128, t0 + (s_ + 1) * 128)
                if not last:
                    self.store(self.X, self.X[rows, :], x, x[:])
                else:
                    if self.xres is not None:
                        self.store(self.xres, self.xres[rows, :], x, x[:])
                    S.op("act", lambda e, x=x: e.activation(out=junk[:], in_=x[:], func=AF.Square, accum_out=ssf[:, 0:1]),
                         reads=[x], writes=[junk, ssf])
                    self.ts("dve", ssf, ssf[:, 1:2], ssf, ssf[:, 0:1], 1.0 / D, EPS, ALU.mult, ALU.add)
                    self.act(ssf, ssf[:, 2:3], ssf, ssf[:, 1:2], AF.Sqrt)
                    S.op("dve", lambda e: e.reciprocal(out=ssf[:, 2:3], in_=ssf[:, 2:3]), reads=[ssf], writes=[ssf])
                    if self.per_layer:
                        self.ts("dve", ssf, ssf[:, 2:3], ssf, ssf[:, 2:3], self.cst[:, 3:4], self.cst[:, 4:5],
                                ALU.mult, ALU.add, extra_reads=[self.cst])
                    self.stt("dve", x, x[:], x, x[:], ssf[:, 2:3], rowv, rowv[:], ALU.mult, ALU.mult, extra_reads=[ssf])
                    self.store(self.out, self.out[rows, :], x, x[:])
            S.barrier()
        self.end_stage()


def build(SEQ, DEPTH, debug=False, stages="all", per_layer=False):
    nc = bass.Bass("TRN2", target_bir_lowering=False)
    with contextlib.ExitStack() as es:
        P = Prog(nc, es, SEQ, DEPTH, debug, per_layer)
        NM = SEQ // P.MT
        for l in range(DEPTH):
            P.stage_W(l)
            for m in range(NM):
                P.stage_A(l, m)
            if stages == "A":
                break
            if "B1" in stages or stages == "all":
                P.stage_B_mla()
            if "B2" in stages or stages == "all":
                P.stage_B_gla()
            if "B3" in stages or stages == "all":
                P.stage_B_gdn()
            if stages != "all":
                break
            P.stage_C(l)
            P.stage_D(l, l == DEPTH - 1)
        P.S.barrier()
        P.S.emit()
        print("instructions", P.S.n_ins, "waits", P.S.n_wait, "sems", P.S.nsem, flush=True)
    return nc


def host_inputs(inp, SEQ, DEPTH, fin=1.0):
    bf = ml_dtypes.bfloat16
    L = DEPTH
    f = lambda k: np.ascontiguousarray(np.asarray(inp[k], dtype=np.float32))

    def colmaj(v):
        return np.ascontiguousarray(v.reshape(-1, 128).T)
    colv = np.zeros((L, 128, NCOLV), np.float32)
    rowv = np.zeros((L, 128, 2048), np.float32)
    for l in range(L):
        colv[l, :, CV_AN:CV_AN + 16] = colmaj(f("attn_norm")[l])
        colv[l, :, CV_FN:CV_FN + 16] = colmaj(f("ffn_norm")[l])
        colv[l, :, CV_QN:CV_QN + 4] = colmaj(f("mla_q_norm")[l])
        colv[l, :, CV_KVN:CV_KVN + 4] = colmaj(f("mla_kv_norm")[l])
        colv[l, :, CV_GB:CV_GB + 4] = colmaj(f("gla_gate_bias")[l])
        for tap in range(4):
            colv[l, :, CV_DCW + tap * 16:CV_DCW + (tap + 1) * 16] = colmaj(f("gdn_conv_w")[l, tap])
        for tap in range(3):
            colv[l, :, CV_FCW + tap * 44:CV_FCW + (tap + 1) * 44] = colmaj(f("ffn_conv_w")[l, tap])
        colv[l, :, CV_FCB:CV_FCB + 44] = colmaj(f("ffn_conv_b")[l])
        colv[l, 0:8, CV_ALOG] = f("gdn_a_log")[l]
        colv[l, 0:8, CV_DTB] = f("gdn_dt_bias")[l]
        colv[l, :, CV_FIN:CV_FIN + 16] = colmaj(f("final_norm"))
        colv[l, :, CV_GLN:CV_GLN + 8] = colmaj(f("gla_out_norm")[l])
        colv[l, :, CV_GDNN:CV_GDNN + 8] = colmaj(f("gdn_out_norm")[l])
        rowv[l, :, :] = f("final_norm")[None, :]
    cst = np.zeros((128, 8), np.float32)
    inv = (10000.0 ** (-np.arange(0, 64, 2, dtype=np.float32) / 64)).astype(np.float32)
    cst[0:32, 0] = inv
    cst[32:64, 0] = inv
    cst[0:32, 1] = -1.0
    cst[32:64, 1] = 1.0
    cst[:, 2] = -math.pi
    cst[:, 3] = fin
    cst[:, 4] = 1.0 - fin
    masks = np.zeros((128, 4, 512), np.float32)
    kl = np.arange(128)[:, None]
    ql = np.arange(512)[None, :]
    for j in range(4):
        masks[:, j, :] = (ql >= 128 * j + kl)
    jj = np.arange(128)[:, None]
    ii = np.arange(128)[None, :]
    same = (jj // 64) == (ii // 64)
    cm = np.zeros((128, 6, 128), np.float32)
    cm[:, 0] = same & (ii >= jj)
    cm[:, 1] = same & (ii > jj)
    cm[:, 2] = same
    cm[:, 3] = same & (jj <= ii)
    cm[:, 4] = np.eye(128)
    m = {
        "x": f("x").reshape(SEQ, D),
        "pos": np.ascontiguousarray(np.asarray(inp["positions"], np.int32).reshape(1, SEQ)),
        "w_in": f("w_in"), "w_uq": f("mla_w_uq"), "w_ukv": f("mla_w_ukv"), "w_g2": f("gla_w_gate2"),
        "w_bm": f("w_branch_mla"), "w_bg": f("w_branch_gla"), "w_bd": f("w_branch_gdn"),
        "w_o": f("w_out"), "w_up": f("ffn_w_up"), "w_dn": f("ffn_w_down"),
        "colv": colv, "rowv": rowv, "cst": cst,
        "identb": np.eye(128, dtype=np.float32).astype(bf), "identf": np.eye(128, dtype=np.float32),
        "masks": masks.reshape(128, 2048).astype(bf), "cmask": cm.reshape(128, 768),
    }
    return m


_NC_CACHE = {}
LAUNCH_MODE = "layer"


def kernel(**inputs):
    SEQ = int(np.asarray(inputs["x"]).shape[1])
    DEPTH = int(np.asarray(inputs["w_in"]).shape[0])
    if LAUNCH_MODE == "fused":
        key = (SEQ, DEPTH, "fused")
        if key not in _NC_CACHE:
            _NC_CACHE[key] = build(SEQ, DEPTH)
        nc = _NC_CACHE[key]
        m = host_inputs(inputs, SEQ, DEPTH)
        res = run_bass_kernel_spmd(nc, [m], core_ids=[0])
        return np.asarray(res.results[0]["out"], dtype=np.float32).reshape(1, SEQ, D)
    key = (SEQ, "layer")
    if key not in _NC_CACHE:
        _NC_CACHE[key] = build(SEQ, 1, per_layer=True)
    nc = _NC_CACHE[key]
    wkeys = ("attn_norm", "w_in", "mla_q_norm", "mla_kv_norm", "mla_w_uq", "mla_w_ukv", "gla_w_gate2",
             "gla_gate_bias", "gla_out_norm", "gdn_conv_w", "gdn_a_log", "gdn_dt_bias", "gdn_out_norm",
             "w_branch_mla", "w_branch_gla", "w_branch_gdn", "w_out", "ffn_norm", "ffn_w_up", "ffn_conv_w",
             "ffn_conv_b", "ffn_w_down")
    x = np.asarray(inputs["x"], np.float32)
    out = None
    for l in range(DEPTH):
        sub = {k: np.asarray(inputs[k])[l:l + 1] for k in wkeys}
        sub["x"] = x
        sub["positions"] = inputs["positions"]
        sub["final_norm"] = inputs["final_norm"]
        m = host_inputs(sub, SEQ, 1, fin=1.0 if l == DEPTH - 1 else 0.0)
        res = run_bass_kernel_spmd(nc, [m], core_ids=[0])
        out = res.results[0]["out"]
        x = np.asarray(out, dtype=np.float32).reshape(1, SEQ, D)
    return np.asarray(out, dtype=np.float32).reshape(1, SEQ, D)
```
